# Optimizing a Trainium2 kernel written in Bass

```python
import math
import jax, jax.numpy as jnp
from jax import lax
import numpy as np


D_MODEL = 1024
BATCH = 8
SEQ = 2048
DEPTH = 4
DEC_BATCH = 128
DEC_SEQ = 1
PAST_LEN = 16384
PAGE_SIZE = 128

N_MIXERS = 4
EPS = 1e-6
GN_EPS = 1e-5
N_A = len(range(0, DEPTH, N_MIXERS))
N_B = len(range(1, DEPTH, N_MIXERS))
N_C = len(range(2, DEPTH, N_MIXERS))
N_D = len(range(3, DEPTH, N_MIXERS))
POOL_WINDOWS = (2, 4, 8, 16)
N_POOL = len(POOL_WINDOWS)
POOL_GROUP = D_MODEL // N_POOL
POOL_BUF = max(POOL_WINDOWS) - 1
CHUNK = 128
GM_INNER = D_MODEL
GM_GROUPS = 8
GM_GDIM = GM_INNER // GM_GROUPS
RET_HEADS = 8
RET_DK = D_MODEL // RET_HEADS
RET_DV = 2 * D_MODEL // RET_HEADS
RET_QK = RET_HEADS * RET_DK
RET_VW = RET_HEADS * RET_DV
RET_CHUNK = 128
ROPE_BASE = 10000.0
LRU_WIDTH = D_MODEL
LRU_HEADS = 8
LRU_BLOCK = LRU_WIDTH // LRU_HEADS
CONV_W = 4
LRU_C = 8.0
D_FF = 4 * D_MODEL

kernel_name = 'hybrid_pool_gmlp_retention_rglru_step'


def rmsnorm(x, g):
    xf = x.astype(jnp.float32)
    y = xf * lax.rsqrt(jnp.mean(xf * xf, axis=-1, keepdims=True) + EPS)
    return (y * g.astype(jnp.float32)).astype(x.dtype)


def pool_mixer(x, buf, pos0, g, w_grp, scale):
    B, L, D = x.shape
    h = rmsnorm(x, g)
    ext = jnp.concatenate([buf.astype(x.dtype), h], axis=1)
    extf = ext.astype(jnp.float32)
    cs = jnp.pad(jnp.cumsum(extf, axis=1), ((0, 0), (1, 0), (0, 0)))
    pos = pos0 + jnp.arange(L)
    means = []
    for gi, w in enumerate(POOL_WINDOWS):
        c0, c1 = gi * POOL_GROUP, (gi + 1) * POOL_GROUP
        hi = cs[:, POOL_BUF + 1:POOL_BUF + 1 + L, c0:c1]
        lo = cs[:, POOL_BUF + 1 - w:POOL_BUF + 1 - w + L, c0:c1]
        cnt = jnp.minimum(pos + 1, w).astype(jnp.float32)
        means.append((hi - lo) / cnt[None, :, None])
    d = (jnp.concatenate(means, axis=-1) - extf[:, POOL_BUF:]).reshape(B, L, N_POOL, POOL_GROUP)
    y = jnp.einsum('blgc,gce->blge', d, w_grp.astype(jnp.float32)).reshape(B, L, D)
    y = y * scale.astype(jnp.float32)
    return y.astype(x.dtype), ext[:, -POOL_BUF:]


def gmlp_mixer(x, g, w_in, b_in, ln_g, ln_b, w_s, b_s, w_out):
    B, L, _ = x.shape
    h = rmsnorm(x, g)
    z = jax.nn.gelu(h @ w_in + b_in)
    u, v = z[..., :GM_INNER], z[..., GM_INNER:]
    vf = v.astype(jnp.float32)
    mu = jnp.mean(vf, axis=-1, keepdims=True)
    var = jnp.mean(jnp.square(vf - mu), axis=-1, keepdims=True)
    vn = ((vf - mu) * lax.rsqrt(var + EPS) * ln_g.astype(jnp.float32) + ln_b.astype(jnp.float32)).astype(x.dtype)
    cl = min(L, CHUNK)
    Lp = -(-L // cl) * cl
    vp = jnp.pad(vn, ((0, 0), (0, Lp - L), (0, 0))).reshape(B, Lp // cl, cl, GM_GROUPS, GM_GDIM)
    ws = jnp.where(jnp.tril(jnp.ones((cl, cl), dtype=bool))[None], w_s[:, :cl, :cl], 0.0)
    mixed = jnp.einsum('gts,bnsgc->bntgc', ws, vp) + b_s[:, :cl].T[None, None, :, :, None]
    mixed = mixed.reshape(B, Lp, GM_INNER)[:, :L]
    y = (u * mixed) @ w_out
    return y.astype(x.dtype), vn


def rope(t, pos):
    half = t.shape[-1] // 2
    freqs = jnp.exp(-math.log(ROPE_BASE) * jnp.arange(half, dtype=jnp.float32) / half)
    ang = pos.astype(jnp.float32)[:, None] * freqs[None]
    cos = jnp.cos(ang)[None, :, None, :]
    sin = jnp.sin(ang)[None, :, None, :]
    t1, t2 = t[..., :half], t[..., half:]
    return jnp.concatenate([t1 * cos - t2 * sin, t1 * sin + t2 * cos], axis=-1)


def retention_mixer(x, s0, pos0, g, w_in, gn_g, gn_b, w_out):
    B, L, _ = x.shape
    h = rmsnorm(x, g)
    p = (h @ w_in).astype(jnp.float32)
    q = p[..., :RET_QK].reshape(B, L, RET_HEADS, RET_DK)
    k = p[..., RET_QK:2 * RET_QK].reshape(B, L, RET_HEADS, RET_DK)
    v = p[..., 2 * RET_QK:2 * RET_QK + RET_VW].reshape(B, L, RET_HEADS, RET_DV)
    gate = p[..., 2 * RET_QK + RET_VW:]
    pos = pos0 + jnp.arange(L)
    q = rope(q, pos)
    k = rope(k, pos) * (RET_DK ** -0.5)
    C = math.gcd(L, RET_CHUNK)
    N = L // C
    log_gamma = jnp.log1p(-jnp.exp2(-5.0 - jnp.arange(RET_HEADS, dtype=jnp.float32)))
    idx = jnp.arange(C, dtype=jnp.float32)
    diff = idx[:, None] - idx[None, :]
    dmask = jnp.where(diff[None] >= 0, jnp.exp(log_gamma[:, None, None] * jnp.maximum(diff, 0.0)[None]), 0.0)
    q_dec = jnp.exp(log_gamma[:, None] * (idx + 1.0))[..., None]
    k_dec = jnp.exp(log_gamma[:, None] * (C - 1.0 - idx))[..., None]
    chunk_dec = jnp.exp(log_gamma * C)[:, None, None]

    def to_chunks(t):
        return t.reshape(B, N, C, RET_HEADS, t.shape[-1]).transpose(1, 0, 3, 2, 4)

    def step(S, inp):
        qi, ki, vi = inp
        scores = jnp.einsum('bhtd,bhsd->bhts', qi, ki) * dmask
        o = jnp.einsum('bhts,bhsv->bhtv', scores, vi) + jnp.einsum('bhtd,bhdv->bhtv', qi * q_dec, S)
        S = S * chunk_dec + jnp.einsum('bhsd,bhsv->bhdv', ki * k_dec, vi)
        return S, o

    S, o = lax.scan(step, s0.astype(jnp.float32), (to_chunks(q), to_chunks(k), to_chunks(v)))
    o = o.transpose(1, 0, 3, 2, 4).reshape(B, L, RET_HEADS, RET_DV)
    mu = jnp.mean(o, axis=-1, keepdims=True)
    var = jnp.mean(jnp.square(o - mu), axis=-1, keepdims=True)
    on = ((o - mu) * lax.rsqrt(var + GN_EPS)).reshape(B, L, RET_VW)
    on = on * gn_g.astype(jnp.float32) + gn_b.astype(jnp.float32)
    y = (jax.nn.silu(gate) * on) @ w_out.astype(jnp.float32)
    return y.astype(x.dtype), S.astype(s0.dtype)


def lru_combine(c1, c2):
    a1, b1 = c1
    a2, b2 = c2
    return (a1 * a2, a2 * b1 + b2)


def rglru_mixer(x, conv_buf, h0, pos0, g, w_in, conv_w, conv_b, w_a, b_a, w_x, b_x, lam, w_out):
    B, L, _ = x.shape
    h = rmsnorm(x, g)
    z = h @ w_in
    gate = jax.nn.gelu(z[..., :LRU_WIDTH].astype(jnp.float32))
    xb = z[..., LRU_WIDTH:]
    ext = jnp.concatenate([conv_buf.astype(x.dtype), xb], axis=1)
    extf = ext.astype(jnp.float32)
    cw = conv_w.astype(jnp.float32)
    xc = conv_b.astype(jnp.float32) + sum(extf[:, j:j + L] * cw[j] for j in range(CONV_W))
    xh = xc.reshape(B, L, LRU_HEADS, LRU_BLOCK)
    r = jax.nn.sigmoid(jnp.einsum('blhi,hij->blhj', xh, w_a.astype(jnp.float32)).reshape(B, L, LRU_WIDTH) + b_a)
    i = jax.nn.sigmoid(jnp.einsum('blhi,hij->blhj', xh, w_x.astype(jnp.float32)).reshape(B, L, LRU_WIDTH) + b_x)
    log_a = -LRU_C * r * jax.nn.softplus(-lam.astype(jnp.float32))
    a = jnp.exp(log_a)
    mult = jnp.sqrt(-jnp.expm1(2.0 * log_a))
    reset = (pos0 + jnp.arange(L) == 0)[None, :, None]
    mult = jnp.where(reset, 1.0, mult)
    bvec = mult * (i * xc)
    bvec = bvec.at[:, 0].add(a[:, 0] * h0.astype(jnp.float32))
    hs = lax.associative_scan(lru_combine, (a, bvec), axis=1)[1]
    y = (hs * gate) @ w_out.astype(jnp.float32)
    return y.astype(x.dtype), ext[:, -(CONV_W - 1):], hs[:, -1].astype(h0.dtype)


def channel_mlp(x, g, w1, w2):
    h = rmsnorm(x, g)
    return (jnp.square(jax.nn.relu(h @ w1)) @ w2).astype(x.dtype)


def run_trunk(x, pos0, pool_bufs, ret_states, conv_bufs, lru_states, weights):
    (pool_norm, pool_w, pool_scale,
     gm_norm, gm_w_in, gm_b_in, gm_ln_g, gm_ln_b, gm_w_s, gm_b_s, gm_w_out,
     ret_norm, ret_w_in, ret_gn_g, ret_gn_b, ret_w_out,
     lru_norm, lru_w_in, lru_conv_w, lru_conv_b, lru_w_a, lru_b_a, lru_w_x, lru_b_x, lru_lam, lru_w_out,
     mlp_norm, mlp_w1, mlp_w2, final_norm) = weights
    new_pool, new_v, new_ret, new_conv, new_lru = [], [], [], [], []
    for li in range(DEPTH):
        m, j = li % N_MIXERS, li // N_MIXERS
        if m == 0:
            y, nb = pool_mixer(x, pool_bufs[j], pos0, pool_norm[j], pool_w[j], pool_scale[j])
            new_pool.append(nb)
        elif m == 1:
            y, nv = gmlp_mixer(x, gm_norm[j], gm_w_in[j], gm_b_in[j], gm_ln_g[j], gm_ln_b[j],
                               gm_w_s[j], gm_b_s[j], gm_w_out[j])
            new_v.append(nv)
        elif m == 2:
            y, ns = retention_mixer(x, ret_states[j], pos0, ret_norm[j], ret_w_in[j],
                                    ret_gn_g[j], ret_gn_b[j], ret_w_out[j])
            new_ret.append(ns)
        else:
            y, nc, nh = rglru_mixer(x, conv_bufs[j], lru_states[j], pos0, lru_norm[j], lru_w_in[j],
                                    lru_conv_w[j], lru_conv_b[j], lru_w_a[j], lru_b_a[j],
                                    lru_w_x[j], lru_b_x[j], lru_lam[j], lru_w_out[j])
            new_conv.append(nc)
            new_lru.append(nh)
        x = x + y
        x = x + channel_mlp(x, mlp_norm[li], mlp_w1[li], mlp_w2[li])
    out = rmsnorm(x, final_norm)
    return (out, jnp.stack(new_pool), jnp.stack(new_v), jnp.stack(new_ret),
            jnp.stack(new_conv), jnp.stack(new_lru))


def setup_inputs(seed: int = 0) -> dict:
    key = jax.random.key(seed)
    ks = iter(jax.random.split(key, 48))
    f32 = jnp.float32

    def nrm(shape, scale):
        return jax.random.normal(next(ks), shape, f32) * scale

    def gain(shape):
        return 1.0 + nrm(shape, 0.02)

    a0 = jax.random.uniform(next(ks), (N_D, LRU_WIDTH), f32, 0.9, 0.999)
    return {
        'x_prompt': nrm((BATCH, SEQ, D_MODEL), 1.0),
        'x_sample': nrm((DEC_BATCH, DEC_SEQ, D_MODEL), 1.0),
        'state_pool': nrm((N_A, DEC_BATCH, POOL_BUF, D_MODEL), 1.0),
        'state_ret': nrm((N_C, DEC_BATCH, RET_HEADS, RET_DK, RET_DV), 0.5),
        'state_conv': nrm((N_D, DEC_BATCH, CONV_W - 1, LRU_WIDTH), 1.0),
        'state_lru': nrm((N_D, DEC_BATCH, LRU_WIDTH), 0.5),
        'pool_norm': gain((N_A, D_MODEL)),
        'pool_w': nrm((N_A, N_POOL, POOL_GROUP, POOL_GROUP), POOL_GROUP ** -0.5),
        'pool_scale': gain((N_A, D_MODEL)),
        'gm_norm': gain((N_B, D_MODEL)),
        'gm_w_in': nrm((N_B, D_MODEL, 2 * GM_INNER), D_MODEL ** -0.5),
        'gm_b_in': nrm((N_B, 2 * GM_INNER), 0.02),
        'gm_ln_g': gain((N_B, GM_INNER)),
        'gm_ln_b': nrm((N_B, GM_INNER), 0.02),
        'gm_w_s': nrm((N_B, GM_GROUPS, CHUNK, CHUNK), CHUNK ** -0.5),
        'gm_b_s': gain((N_B, GM_GROUPS, CHUNK)),
        'gm_w_out': nrm((N_B, GM_INNER, D_MODEL), GM_INNER ** -0.5),
        'ret_norm': gain((N_C, D_MODEL)),
        'ret_w_in': nrm((N_C, D_MODEL, 2 * RET_QK + 2 * RET_VW), D_MODEL ** -0.5),
        'ret_gn_g': gain((N_C, RET_VW)),
        'ret_gn_b': nrm((N_C, RET_VW), 0.02),
        'ret_w_out': nrm((N_C, RET_VW, D_MODEL), RET_VW ** -0.5),
        'lru_norm': gain((N_D, D_MODEL)),
        'lru_w_in': nrm((N_D, D_MODEL, 2 * LRU_WIDTH), D_MODEL ** -0.5),
        'lru_conv_w': nrm((N_D, CONV_W, LRU_WIDTH), CONV_W ** -0.5),
        'lru_conv_b': nrm((N_D, LRU_WIDTH), 0.02),
        'lru_w_a': nrm((N_D, LRU_HEADS, LRU_BLOCK, LRU_BLOCK), LRU_BLOCK ** -0.5),
        'lru_b_a': nrm((N_D, LRU_WIDTH), 0.02),
        'lru_w_x': nrm((N_D, LRU_HEADS, LRU_BLOCK, LRU_BLOCK), LRU_BLOCK ** -0.5),
        'lru_b_x': nrm((N_D, LRU_WIDTH), 0.02),
        'lru_lam': jnp.log(a0) - jnp.log1p(-a0),
        'lru_w_out': nrm((N_D, LRU_WIDTH, D_MODEL), LRU_WIDTH ** -0.5),
        'mlp_norm': gain((DEPTH, D_MODEL)),
        'mlp_w1': nrm((DEPTH, D_MODEL, D_FF), D_MODEL ** -0.5),
        'mlp_w2': nrm((DEPTH, D_FF, D_MODEL), D_FF ** -0.5),
        'final_norm': gain((D_MODEL,)),
    }


def reference(x_prompt, x_sample, state_pool, state_ret, state_conv, state_lru,
              pool_norm, pool_w, pool_scale,
              gm_norm, gm_w_in, gm_b_in, gm_ln_g, gm_ln_b, gm_w_s, gm_b_s, gm_w_out,
              ret_norm, ret_w_in, ret_gn_g, ret_gn_b, ret_w_out,
              lru_norm, lru_w_in, lru_conv_w, lru_conv_b, lru_w_a, lru_b_a, lru_w_x, lru_b_x, lru_lam, lru_w_out,
              mlp_norm, mlp_w1, mlp_w2, final_norm):
    weights = (pool_norm, pool_w, pool_scale,
               gm_norm, gm_w_in, gm_b_in, gm_ln_g, gm_ln_b, gm_w_s, gm_b_s, gm_w_out,
               ret_norm, ret_w_in, ret_gn_g, ret_gn_b, ret_w_out,
               lru_norm, lru_w_in, lru_conv_w, lru_conv_b, lru_w_a, lru_b_a, lru_w_x, lru_b_x, lru_lam, lru_w_out,
               mlp_norm, mlp_w1, mlp_w2, final_norm)
    dt = x_prompt.dtype
    p_pool0 = jnp.zeros((N_A, BATCH, POOL_BUF, D_MODEL), dt)
    p_ret0 = jnp.zeros((N_C, BATCH, RET_HEADS, RET_DK, RET_DV), state_ret.dtype)
    p_conv0 = jnp.zeros((N_D, BATCH, CONV_W - 1, LRU_WIDTH), dt)
    p_lru0 = jnp.zeros((N_D, BATCH, LRU_WIDTH), state_lru.dtype)
    y_prompt, pool_p, _, ret_p, conv_p, lru_p = run_trunk(
        x_prompt, 0, p_pool0, p_ret0, p_conv0, p_lru0, weights)
    y_sample, pool_s, v_s, ret_s, conv_s, lru_s = run_trunk(
        x_sample, PAST_LEN, state_pool, state_ret, state_conv, state_lru, weights)
    return (y_prompt, y_sample, pool_p, pool_s, v_s, ret_p, ret_s, conv_p, conv_s, lru_p, lru_s)
```

```python
import math
import numpy as np
import concourse.bass as bass
import concourse.mybir as mybir
from concourse.bass_utils import run_bass_kernel_spmd

F32 = mybir.dt.float32
BF16 = mybir.dt.bfloat16
AF = mybir.ActivationFunctionType
ALU = mybir.AluOpType
AX = mybir.AxisListType

NCORES = 8
D = 1024
NCH = 8
SEQ = 2048
NS = 16
T = SEQ + NS
TB = [0, 512, 1024, 1536, 2048, 2064]
NTT = 5
DFF = 4096
EPS = 1e-6
GN_EPS = 1e-5
PAST = 16384
import os
NOMLP = bool(os.environ.get('KDEBUG_NOMLP'))


class Op:
    __slots__ = ("eng", "fn", "deps", "signal", "sig_idx", "dsem", "dval")

    def __init__(self, eng, fn, dsem=None):
        self.eng = eng
        self.fn = fn
        self.deps = []
        self.signal = False
        self.sig_idx = 0
        self.dsem = dsem
        self.dval = 0


class Rec:
    ENGS = ("pe", "act", "dve", "pool", "sp")

    def __init__(self, nc):
        self.nc = nc
        self.ops = {e: [] for e in self.ENGS}
        self.last_w = {}
        self.readers = {}
        self.dsem_tot = {}
        self.dsem_last = {}
        self.extra = {e: [] for e in self.ENGS}

    def op(self, eng, fn, r=(), w=(), dsem=None):
        o = Op(eng, fn, dsem)
        deps = {}
        for t in r:
            d = self.last_w.get(t)
            if d is not None:
                deps[id(d)] = d
        for t in w:
            d = self.last_w.get(t)
            if d is not None:
                deps[id(d)] = d
            for d in self.readers.get(t, ()):
                deps[id(d)] = d
        for d in self.extra[eng]:
            deps[id(d)] = d
        self.extra[eng] = []
        o.deps = list(deps.values())
        for d in o.deps:
            if d.dsem is None:
                d.signal = True
        for t in r:
            self.readers.setdefault(t, []).append(o)
        for t in w:
            self.last_w[t] = o
            self.readers[t] = []
        if dsem is not None:
            self.dsem_tot[dsem] = self.dsem_tot.get(dsem, 0) + 16
            o.dval = self.dsem_tot[dsem]
            self.dsem_last[dsem] = o
        self.ops[eng].append(o)
        return o

    def dma(self, q, out, in_, r=(), w=(), dsem=None):
        if q == "pool" and not dsem.startswith("ws"):
            self.extra["pool"] = list(getattr(self, "bar_deps", []))
        return self.op(q, lambda e: e.dma_start(out=out, in_=in_), r=r, w=w, dsem=dsem)

    def barrier(self):
        deps = []
        for e in self.ENGS:
            if e == "pool":
                continue
            for o in reversed(self.ops[e]):
                if o.dsem is None:
                    deps.append(o)
                    break
        deps += [o for k, o in self.dsem_last.items() if not k.startswith("ws")]
        for e in self.ENGS:
            if e != "pool":
                self.extra[e] = list(deps)
        self.bar_deps = list(deps)
        self.last_w = {k: v for k, v in self.last_w.items() if k.startswith("ws")}
        self.readers = {k: v for k, v in self.readers.items() if k.startswith("ws")}

    def emit(self):
        nc = self.nc
        for e in self.ENGS:
            n = 0
            for o in self.ops[e]:
                if o.dsem is None and o.signal:
                    n += 1
                    o.sig_idx = n
        esem = {e: nc.alloc_semaphore("es_" + e) for e in self.ENGS}
        dsems = {k: nc.alloc_semaphore("ds_" + k) for k in self.dsem_tot}
        rec = self

        def run(ename, eng):
            waited = {}
            for o in rec.ops[ename]:
                for d in o.deps:
                    if d.dsem is not None:
                        key, sem, val = "d" + d.dsem, dsems[d.dsem], d.dval
                    else:
                        if ename == "pe" and d.eng == "pe":
                            continue
                        key, sem, val = "e" + d.eng, esem[d.eng], d.sig_idx
                    if waited.get(key, 0) >= val:
                        continue
                    eng.wait_ge(sem, val)
                    waited[key] = val
                ins = o.fn(eng)
                if o.dsem is not None:
                    ins.then_inc(dsems[o.dsem], 16)
                elif o.signal:
                    ins.then_inc(esem[ename], 1)
            if ename == "sp":
                for k, tot in rec.dsem_tot.items():
                    eng.wait_ge(dsems[k], tot)

        with nc.Block() as block:
            @block.tensor
            def _(eng):
                run("pe", eng)

            @block.scalar
            def _(eng):
                run("act", eng)

            @block.vector
            def _(eng):
                run("dve", eng)

            @block.gpsimd
            def _(eng):
                run("pool", eng)

            @block.sync
            def _(eng):
                run("sp", eng)


VEC_SPECS = [
    ("pool_norm", 8), ("pool_scale", 8), ("gm_norm", 8), ("gm_b_in", 16),
    ("ret_norm", 8), ("ret_gn_g", 16), ("ret_gn_b", 16), ("lru_norm", 8),
    ("lru_conv_w", 32), ("lru_conv_b", 8), ("lru_b_a", 8), ("lru_b_x", 8), ("lru_lam", 8),
    ("mlp_norm", 32), ("final_norm", 8), ("gm_ln_g", 8), ("gm_ln_b", 8), ("gm_ws00", 8), ("gm_bs0", 8),
]
VEC_OFF = {}
_o = 0
for _n, _k in VEC_SPECS:
    VEC_OFF[_n] = _o
    _o += _k
NVEC = _o


def cols(v):
    v = np.asarray(v, np.float32).reshape(-1, 128)
    return np.ascontiguousarray(v.T)


def tt_of(c0, c1):
    return [i for i in range(NTT) if TB[i] < c1 and TB[i + 1] > c0]


class K:
    def __init__(self, mixers=(0, 1, 2, 3), nlayers=4):
        self.mixers = mixers
        self.nlayers = nlayers
        nc = self.nc = bass.Bass("TRN2", target_bir_lowering=False)
        self.R = Rec(nc)
        self.din = {}
        self.dout = {}
        self.bank_i = 0
        self.hold = set()
        self.build()

    def inp(self, name, shape, dt=F32):
        t = self.nc.dram_tensor(name, list(shape), dt, kind="ExternalInput").ap()
        self.din[name] = t
        return t

    def outp(self, name, shape, dt=F32):
        t = self.nc.dram_tensor(name, list(shape), dt, kind="ExternalOutput").ap()
        self.dout[name] = t
        return t

    def sb(self, name, shape, dt):
        return self.nc.alloc_sbuf_tensor(name, list(shape), dt)

    def tmp(self, name, shape, dt):
        nbytes = int(np.prod(shape[1:])) * (2 if dt == BF16 else 4)
        nbytes = (nbytes + 31) // 32 * 32
        off = self.t_off
        self.t_off += nbytes
        assert self.t_off <= self.t_end, (name, self.t_off, self.t_end)
        self.tmp_n += 1
        return self.nc.alloc_sbuf_tensor_at(f"{name}_{self.tmp_n}", list(shape), dt, offset=off)

    def tmp_reset(self):
        self.R.barrier()
        self.t_off = self.t_base

    def bank(self):
        while self.bank_i in self.hold:
            self.bank_i = (self.bank_i + 1) % 8
        b = self.bank_i
        self.bank_i = (self.bank_i + 1) % 8
        return b, self.ps[b], f"ps{b}"

    def wload(self, dram_view, pattern=None, **kw):
        i = self.w_i
        self.w_i = (self.w_i + 1) % self.NW
        slot = self.ws[i]
        n = int(np.prod(dram_view.shape[1:]))
        v = slot[:, 0:n]
        if pattern is not None:
            v = v.rearrange(pattern, **kw)
        self.R.dma("pool", v, dram_view, w=[f"ws{i}"], dsem=f"ws{i}")
        return v, f"ws{i}"

    def wload_parts(self, parts):
        i = self.w_i
        self.w_i = (self.w_i + 1) % self.NW
        slot = self.ws[i]
        views = []
        for dram_view, off, pattern, kw in parts:
            n = int(np.prod(dram_view.shape[1:]))
            v = slot[:, off:off + n]
            if pattern is not None:
                v = v.rearrange(pattern, **kw)
            self.R.dma("pool", v, dram_view, w=[f"ws{i}"], dsem=f"ws{i}")
            views.append(v)
        return views, f"ws{i}"

    def xt(self, c, i):
        return f"x{c}_{i}"

    def ht(self, c, i):
        return f"h{c}_{i}"

    def at(self, c, i):
        return f"a{c}_{i}"

    def vcol(self, name, j):
        o = VEC_OFF[name] + j
        return self.vecs[:, o:o + 1]

    def norm_stats(self):
        for i in range(NTT):
            self._norm_stats_tile(i)

    def _norm_stats_tile(self, i):
        R = self.R
        xres, rstd = self.xres, self.rstd
        c0, c1 = TB[i], TB[i + 1]
        n = c1 - c0
        b, ps, pt = self.bank()
        for c in range(NCH):
            sq = self.sq[c % 2]
            sqt = f"sq{c % 2}"
            R.op("act", lambda e, sq=sq, c=c: e.activation(out=sq[:, 0:n], in_=xres[:, c, c0:c1], func=AF.Square),
                 r=[self.xt(c, i)], w=[sqt])
            R.op("pe", lambda e, sq=sq, c=c: e.matmul(ps[:, 0:n], lhsT=self.ones_bf[:], rhs=sq[:, 0:n],
                                                       start=(c == 0), stop=(c == NCH - 1)), r=[sqt], w=[pt])
        R.op("act", lambda e: e.activation(out=rstd[:, c0:c1], in_=ps[:, 0:n], func=AF.Sqrt,
                                           bias=self.eps_col[:], scale=1.0 / D), r=[pt], w=[f"rstd{i}"])
        R.op("dve", lambda e: e.reciprocal(out=rstd[:, c0:c1], in_=rstd[:, c0:c1]), r=[f"rstd{i}"], w=[f"rstd{i}"])

    def norm(self, gname, goff, out_f32_inplace=False):
        self.norm_stats()
        for i in range(NTT):
            self._norm_apply_tile(i, gname, goff, out_f32_inplace)

    def _norm_apply_tile(self, i, gname, goff, out_f32_inplace):
        R = self.R
        xres, H, rstd = self.xres, self.H, self.rstd
        c0, c1 = TB[i], TB[i + 1]
        for c in range(NCH):
            g = self.vcol(gname, goff + c)
            if out_f32_inplace:
                R.op("dve", lambda e, c=c, g=g: e.scalar_tensor_tensor(
                    out=xres[:, c, c0:c1], in0=xres[:, c, c0:c1], scalar=g, in1=rstd[:, c0:c1],
                    op0=ALU.mult, op1=ALU.mult), r=[self.xt(c, i), f"rstd{i}"], w=[self.xt(c, i)])
            else:
                R.op("dve", lambda e, c=c, g=g: e.scalar_tensor_tensor(
                    out=H[:, c, c0:c1], in0=xres[:, c, c0:c1], scalar=g, in1=rstd[:, c0:c1],
                    op0=ALU.mult, op1=ALU.mult), r=[self.xt(c, i), f"rstd{i}"], w=[self.ht(c, i)])

    def xtoks(self, c):
        return [self.xt(c, i) for i in range(NTT)]

    def htoks(self, c):
        return [self.ht(c, i) for i in range(NTT)]

    def rstd_toks(self):
        return [f"rstd{i}" for i in range(NTT)]

    def add_to_xres(self, oc, i, ps, pt, scale_col=None):
        xres = self.xres
        c0, c1 = TB[i], TB[i + 1]
        n = c1 - c0
        if scale_col is None:
            self.R.op("dve", lambda e: e.tensor_tensor(out=xres[:, oc, c0:c1], in0=xres[:, oc, c0:c1],
                                                       in1=ps[:, 0:n], op=ALU.add),
                      r=[pt, self.xt(oc, i)], w=[self.xt(oc, i)])
        else:
            self.R.op("dve", lambda e: e.scalar_tensor_tensor(out=xres[:, oc, c0:c1], in0=ps[:, 0:n], scalar=scale_col,
                                                              in1=xres[:, oc, c0:c1], op0=ALU.mult, op1=ALU.add),
                      r=[pt, self.xt(oc, i)], w=[self.xt(oc, i)])

    def mlp(self, li):
        R = self.R
        xres, H = self.xres, self.H
        A1 = self.tmp("hid", [128, NCH, T], BF16)
        self.r32 = [self.tmp(f"r32_{i}", [128, 512], F32) for i in range(2)]
        self.norm("mlp_norm", li * 8)
        w1 = self.din["mlp_w1"][li].rearrange("(k p) n -> p k n", p=128)
        w2 = self.din["mlp_w2"][li].rearrange("(f p) n -> p f n", p=128)

        def load_w1(p, j):
            return self.wload(w1[:, :, p * 1024 + j * 512: p * 1024 + (j + 1) * 512], "p (k n) -> p k n", k=8)

        def load_w2(p):
            return [self.wload(w2[:, p * 8 + j * 4: p * 8 + (j + 1) * 4, :], "p (f n) -> p f n", f=4)
                    for j in range(2)]

        pre = None
        for p in range(4):
            w1s = [pre if pre is not None else load_w1(p, 0), load_w1(p, 1)]
            w2s = load_w2(p)
            pre = load_w1(p + 1, 0) if p < 3 else None

            def hd(i, w1s=w1s):
                c0, c1 = TB[i], TB[i + 1]
                n = c1 - c0
                for f in range(8):
                    wv, wt = w1s[f // 4]
                    b, ps, pt = self.bank()
                    for k in range(NCH):
                        R.op("pe", lambda e, wv=wv, f=f, k=k, ps=ps: e.matmul(
                            ps[:, 0:n], lhsT=wv[:, k, (f % 4) * 128:(f % 4 + 1) * 128], rhs=H[:, k, c0:c1],
                            start=(k == 0), stop=(k == NCH - 1)), r=[wt, self.ht(k, i)], w=[pt])
                    r32 = self.r32[f % 2]
                    rt = f"r32_{f % 2}"
                    R.op("act", lambda e, ps=ps, r32=r32: e.activation(out=r32[:, 0:n], in_=ps[:, 0:n], func=AF.Relu),
                         r=[pt], w=[rt])
                    R.op("dve", lambda e, r32=r32, f=f: e.tensor_tensor(out=A1[:, f, c0:c1], in0=r32[:, 0:n],
                                                                         in1=r32[:, 0:n], op=ALU.mult),
                         r=[rt], w=[self.at(f, i)])

            def out(i, w2s=w2s):
                c0, c1 = TB[i], TB[i + 1]
                n = c1 - c0
                for oc in range(NCH):
                    b, ps, pt = self.bank()
                    for f in range(8):
                        wv, wt = w2s[f // 4]
                        R.op("pe", lambda e, wv=wv, f=f, oc=oc, ps=ps: e.matmul(
                            ps[:, 0:n], lhsT=wv[:, f % 4, oc * 128:(oc + 1) * 128], rhs=A1[:, f, c0:c1],
                            start=(f == 0), stop=(f == 7)), r=[wt, self.at(f, i)], w=[pt])
                    self.add_to_xres(oc, i, ps, pt)

            hd(0)
            for i in range(1, NTT):
                hd(i)
                out(i - 1)
            out(NTT - 1)

    def pool_mixer(self):
        R = self.R
        xres, H, rstd = self.xres, self.H, self.rstd
        WINS = (2, 4, 8, 16)
        h32 = self.tmp("h32", [128, T], F32)
        pa = self.tmp("pa", [128, SEQ], F32)
        pb = self.tmp("pb", [128, SEQ], F32)
        spT = self.tmp("spT", [128, NCH, NS, 15], F32)
        pso = self.tmp("pso", [128, NCH, NS, 15], F32)
        ppo = self.tmp("ppo", [128, NCH, 15], F32)
        rc = self.tmp("rc", [128, 4, 16], F32)
        ssum = self.tmp("ssum", [128, NS], F32)
        dfix = self.tmp("dfix", [128, 16], F32)
        R.dma("sp", spT[:], self.din["spT"].rearrange("(c p) b r -> p c b r", p=128), w=["spT"], dsem="spT")
        R.dma("sp", rc[:], self.din["rc16"], w=["rc"], dsem="rc")
        wv, wt = self.wload(self.din["pool_w"][0].rearrange("g (j p) e -> p g j e", p=128),
                            "p (g j e) -> p g j e", g=4, j=2)
        self.norm_stats()
        R.op("act", lambda e: e.activation(out=pso[:, :, :, 0:14], in_=spT[:, :, :, 1:15], func=AF.Copy),
             r=["spT"], w=["pso_a"])

        def chunk(c):
            w = WINS[c // 2]
            gi = c // 2
            g = self.vcol("pool_norm", c)
            R.op("dve", lambda e: e.scalar_tensor_tensor(out=h32[:], in0=xres[:, c, :], scalar=g, in1=rstd[:],
                                                         op0=ALU.mult, op1=ALU.mult),
                 r=self.xtoks(c) + self.rstd_toks(), w=["h32"])
            R.op("act", lambda e: e.activation(out=ppo[:, c, :], in_=h32[:, SEQ - 15:SEQ], func=AF.Copy),
                 r=["h32"], w=[f"ppo{c}"])
            R.op("act", lambda e: e.activation(out=pso[:, c, :, 14:15], in_=h32[:, SEQ:T].unsqueeze(2),
                                               func=AF.Copy), r=["h32"], w=[f"pso_b{c}"])
            src, st = h32, "h32"
            bufs = [(pa, "pa"), (pb, "pb")]
            step = 1
            k = 0
            while step < w:
                dst, dt_ = bufs[k % 2]
                R.op("dve", lambda e, src=src, dst=dst, step=step: e.tensor_tensor(
                    out=dst[:, step:SEQ], in0=src[:, step:SEQ], in1=src[:, 0:SEQ - step], op=ALU.add),
                    r=[st, st + "h"], w=[dt_])
                R.op("act", lambda e, src=src, dst=dst, step=step: e.activation(
                    out=dst[:, 0:step], in_=src[:, 0:step], func=AF.Copy), r=[st, st + "h"], w=[dt_ + "h"])
                src, st = dst, dt_
                step *= 2
                k += 1
            rr = [st, st + "h", "h32"]
            R.op("dve", lambda e, src=src: e.scalar_tensor_tensor(out=H[:, c, 0:SEQ], in0=src[:, 0:SEQ], scalar=1.0 / w,
                                                                   in1=h32[:, 0:SEQ], op0=ALU.mult, op1=ALU.subtract),
                 r=rr, w=self.htoks(c)[0:4])
            R.op("dve", lambda e, src=src: e.tensor_tensor(out=dfix[:, 0:w - 1], in0=src[:, 0:w - 1], in1=rc[:, gi, 0:w - 1],
                                                            op=ALU.mult), r=rr + ["rc"], w=["dfix"])
            R.op("dve", lambda e: e.tensor_tensor(out=H[:, c, 0:w - 1], in0=dfix[:, 0:w - 1], in1=h32[:, 0:w - 1],
                                                  op=ALU.subtract), r=["dfix", "h32"], w=[self.ht(c, 0)])
            R.op("dve", lambda e: e.tensor_reduce(out=ssum[:], in_=spT[:, c, :, 15 - (w - 1):15], axis=AX.X, op=ALU.add),
                 r=["spT"], w=["ssum"])
            R.op("dve", lambda e: e.tensor_tensor(out=ssum[:], in0=ssum[:], in1=h32[:, SEQ:T], op=ALU.add),
                 r=["ssum", "h32"], w=["ssum"])
            R.op("dve", lambda e: e.scalar_tensor_tensor(out=H[:, c, SEQ:T], in0=ssum[:], scalar=1.0 / w, in1=h32[:, SEQ:T],
                                                         op0=ALU.mult, op1=ALU.subtract),
                 r=["ssum", "h32"], w=[self.ht(c, 4)])

        for c in range(NCH):
            chunk(c)
        R.dma("sp", self.dout["poolpT"].rearrange("(c p) r -> p c r", p=128), ppo[:],
              r=[f"ppo{c}" for c in range(NCH)], w=["o_poolp"], dsem="o_poolp")
        R.dma("sp", self.dout["poolsT"].rearrange("(c p) b r -> p c b r", p=128), pso[:],
              r=["pso_a"] + [f"pso_b{c}" for c in range(NCH)], w=["o_pools"], dsem="o_pools")

        def proj(i):
            c0, c1 = TB[i], TB[i + 1]
            n = c1 - c0
            for oc in range(NCH):
                gi = oc // 2
                b, ps, pt = self.bank()
                for j in range(2):
                    R.op("pe", lambda e, j=j, ps=ps, oc=oc, gi=gi: e.matmul(
                        ps[:, 0:n], lhsT=wv[:, gi, j, (oc % 2) * 128:(oc % 2 + 1) * 128], rhs=H[:, 2 * gi + j, c0:c1],
                        start=(j == 0), stop=(j == 1)), r=[wt, self.ht(2 * gi + j, i)], w=[pt])
                self.add_to_xres(oc, i, ps, pt, scale_col=self.vcol("pool_scale", oc))

        for i in range(NTT):
            proj(i)

    def gmlp_mixer(self):
        R = self.R
        nc = self.nc
        xres, H = self.xres, self.H
        Cc = self.tmp("Cc", [128, NCH, 128], F32)
        wsm = self.tmp("wsm", [128, NCH, 128], BF16)
        self.binv_row = self.tmp("binv_row", [1, 1024], BF16)
        mark = self.t_off
        bsb = self.tmp("bsb", [128, NCH, 128], F32)
        wsf = self.tmp("wsf", [128, NCH, 128], F32)
        mk = self.tmp("mk", [128, 128], F32)
        self.bs_row = self.tmp("bs_row", [1, 1024], F32)
        R.dma("sp", wsf[:], self.din["gm_wsT"], w=["wsf"], dsem="gmc0")
        R.dma("sp", mk[:], self.din["trilT"], w=["mk"], dsem="gmc1")
        R.dma("sp", self.bs_row[:], self.din["gm_b_s"].rearrange("a g t -> a (g t)"), w=["bs_row"], dsem="gmc2")
        R.dma("pool", self.binv_row[:], self.din["gm_b_in"][0:1, 1024:2048], w=["binv"], dsem="binv")
        R.op("dve", lambda e: e.tensor_tensor(out=wsm[:], in0=wsf[:], in1=mk[:].unsqueeze(1).to_broadcast([128, NCH, 128]),
                                              op=ALU.mult), r=["wsf", "mk"], w=["wsm"])
        for hf in range(2):
            b, ps, pt = self.bank()
            R.op("pe", lambda e, ps=ps, hf=hf: e.matmul(ps[:], lhsT=self.ones_bf[:], rhs=wsm[:, 4 * hf:4 * hf + 4, :],
                                                        start=True, stop=True), r=["wsm"], w=[pt])
            b2, ps2, pt2 = self.bank()
            R.op("pe", lambda e, ps2=ps2, hf=hf: e.matmul(ps2[:], lhsT=self.ones_row_f[0:1, :],
                                                          rhs=self.bs_row[0:1, 512 * hf:512 * hf + 512],
                                                          start=True, stop=True), r=["bs_row"], w=[pt2])
            R.op("act", lambda e, ps2=ps2, hf=hf: e.activation(out=bsb[:, 4 * hf:4 * hf + 4, :], in_=ps2[:], func=AF.Copy),
                 r=[pt2], w=[f"bsb{hf}"])
            for gq in range(4):
                g = 4 * hf + gq
                R.op("dve", lambda e, ps=ps, g=g, gq=gq: e.scalar_tensor_tensor(
                    out=Cc[:, g, :], in0=ps[:, gq * 128:(gq + 1) * 128], scalar=self.vcol("gm_ln_b", g), in1=bsb[:, g, :],
                    op0=ALU.mult, op1=ALU.add), r=[pt, f"bsb{hf}"], w=[f"Cc{g}"])
        R.barrier()
        self.t_off = mark
        A1 = self.tmp("gU", [128, NCH, T], BF16)
        v32 = self.tmp("v32", [128, 1024], F32)
        vh = [self.tmp(f"vh{j}", [128, 1024], BF16) for j in range(2)]
        tmx = [self.tmp("tmx0", [128, 4, 128], F32)] * 2
        st6 = self.tmp("st6", [128, 2, 6], F32)
        mv = self.tmp("mv", [128, 2], F32)
        rsd = self.tmp("rsd", [128, 1], F32)
        vs32 = v32
        vns = self.tmp("vns", [128, NCH, NS], F32)
        tms = self.tmp("tms", [128, NS], F32)
        w_in = self.din["gm_w_in"][0].rearrange("(k p) n -> p k n", p=128)
        w_out = self.din["gm_w_out"][0].rearrange("(k p) n -> p k n", p=128)
        U = [self.wload(w_in[:, :, j * 512:(j + 1) * 512], "p (k n) -> p k n", k=8) for j in range(2)]
        V = [self.wload(w_in[:, :, 1024 + j * 512:1024 + (j + 1) * 512], "p (k n) -> p k n", k=8) for j in range(2)]
        self.norm("gm_norm", 0)

        def uphase(j, i):
            c0, c1 = TB[i], TB[i + 1]
            n = c1 - c0
            wv, wt = U[j]
            for f in range(4):
                fc = 4 * j + f
                b, ps, pt = self.bank()
                for k in range(NCH):
                    R.op("pe", lambda e, f=f, k=k, ps=ps: e.matmul(ps[:, 0:n], lhsT=wv[:, k, f * 128:(f + 1) * 128],
                                                                   rhs=H[:, k, c0:c1], start=(k == 0), stop=(k == NCH - 1)),
                         r=[wt, self.ht(k, i)], w=[pt])
                R.op("act", lambda e, ps=ps, fc=fc: e.activation(out=A1[:, fc, c0:c1], in_=ps[:, 0:n], func=AF.Gelu_apprx_tanh,
                                                                 bias=self.vcol("gm_b_in", fc)), r=[pt], w=[self.at(fc, i)])

        for j in range(2):
            for i in range(NTT):
                uphase(j, i)

        def vtok(c0, m, i):
            banks = []
            for hf in range(2):
                wv, wt = V[hf]
                b, ps, pt = self.bank()
                for k in range(NCH):
                    R.op("pe", lambda e, k=k, ps=ps, wv=wv: e.matmul(ps[0:m, :], lhsT=H[:, k, c0:c0 + m], rhs=wv[:, k, :],
                                                                      start=(k == 0), stop=False),
                         r=[wt, self.ht(k, i)], w=[pt])
                R.op("pe", lambda e, ps=ps, hf=hf: e.matmul(ps[0:m, :], lhsT=self.ones_row_b[0:1, 0:m],
                                                            rhs=self.binv_row[0:1, 512 * hf:512 * hf + 512],
                                                            start=False, stop=True), r=["binv"], w=[pt])
                banks.append((ps, pt))
            return banks

        def lnorm(src, m, srct):
            for hf in range(2):
                R.op("dve", lambda e, hf=hf: e.bn_stats(out=st6[0:m, hf, :], in_=src[0:m, 512 * hf:512 * hf + 512]),
                     r=srct, w=[f"st6{hf}"])
            R.op("dve", lambda e: e.bn_aggr(out=mv[0:m, :], in_=st6[0:m, :, :].rearrange("p a b -> p (a b)")),
                 r=["st60", "st61"], w=["mv"])
            R.op("act", lambda e: e.activation(out=rsd[0:m, :], in_=mv[0:m, 1:2], func=AF.Sqrt, bias=self.eps_col[0:m, :],
                                               scale=1.0), r=["mv"], w=["rsd"])
            R.op("dve", lambda e: e.reciprocal(out=rsd[0:m, :], in_=rsd[0:m, :]), r=["rsd"], w=["rsd"])

        def vchunk(n_):
            c0 = 128 * n_
            i = c0 // 512
            banks = vtok(c0, 128, i)
            for hf, (ps, pt) in enumerate(banks):
                R.op("act", lambda e, ps=ps, hf=hf: e.activation(out=v32[:, 512 * hf:512 * hf + 512], in_=ps[:],
                                                                 func=AF.Gelu_apprx_tanh), r=[pt], w=[f"v32{hf}"])
            lnorm(v32, 128, ["v320", "v321"])
            vhb = vh[n_ % 2]
            vht = f"vh{n_ % 2}"
            R.op("dve", lambda e: e.tensor_scalar(out=vhb[:], in0=v32[:], scalar1=mv[:, 0:1], scalar2=rsd[:, 0:1],
                                                  op0=ALU.subtract, op1=ALU.mult), r=["v320", "v321", "mv", "rsd"], w=[vht])
            for hf in range(2):
                b, ps, pt = self.bank()
                for gq in range(4):
                    g = 4 * hf + gq
                    R.op("pe", lambda e, ps=ps, g=g, gq=gq: e.matmul(ps[:, gq * 128:(gq + 1) * 128],
                                                                      lhsT=vhb[:, g * 128:(g + 1) * 128], rhs=wsm[:, g, :],
                                                                      start=True, stop=True), r=[vht, "wsm"], w=[pt])
                tm = tmx[hf]
                tmt = "tmx0"
                for gq in range(4):
                    g = 4 * hf + gq
                    R.op("dve", lambda e, ps=ps, g=g, gq=gq, tm=tm: e.scalar_tensor_tensor(
                        out=tm[:, gq, :], in0=ps[:, gq * 128:(gq + 1) * 128], scalar=self.vcol("gm_ln_g", g), in1=Cc[:, g, :],
                        op0=ALU.mult, op1=ALU.add), r=[pt, f"Cc{g}"], w=[tmt + f"_{gq}"])
                R.op("dve", lambda e, tm=tm, hf=hf: e.tensor_tensor(out=A1[:, 4 * hf:4 * hf + 4, c0:c0 + 128], in0=tm[:],
                                                                      in1=A1[:, 4 * hf:4 * hf + 4, c0:c0 + 128], op=ALU.mult),
                     r=[tmt + f"_{gq}" for gq in range(4)] + [self.at(4 * hf + gq, i) for gq in range(4)],
                     w=[self.at(4 * hf + gq, i) for gq in range(4)])

        for n_ in range(16):
            vchunk(n_)

        banks = vtok(SEQ, NS, 4)
        for hf, (ps, pt) in enumerate(banks):
            R.op("act", lambda e, ps=ps, hf=hf: e.activation(out=vs32[0:NS, 512 * hf:512 * hf + 512], in_=ps[0:NS, :],
                                                             func=AF.Gelu_apprx_tanh), r=[pt], w=[f"v32{hf}"])
        lnorm(vs32, NS, ["v320", "v321"])
        R.op("dve", lambda e: e.tensor_scalar(out=vs32[0:NS, :], in0=vs32[0:NS, :], scalar1=mv[0:NS, 0:1], scalar2=rsd[0:NS, 0:1],
                                              op0=ALU.subtract, op1=ALU.mult), r=["v320", "v321", "mv", "rsd"], w=["v320", "v321"])
        for g in range(NCH):
            b, ps, pt = self.bank()
            R.op("pe", lambda e, ps=ps, g=g: e.matmul(ps[:, 0:NS], lhsT=vs32[0:NS, g * 128:(g + 1) * 128],
                                                      rhs=self.ident_f[0:NS, 0:NS], start=True, stop=True),
                 r=["v320", "v321", "ident"], w=[pt])
            R.op("dve", lambda e, ps=ps, g=g: e.tensor_scalar(out=vns[:, g, :], in0=ps[:, 0:NS],
                                                              scalar1=self.vcol("gm_ln_g", g), scalar2=self.vcol("gm_ln_b", g),
                                                              op0=ALU.mult, op1=ALU.add), r=[pt], w=[f"vns{g}"])
            R.op("dve", lambda e, g=g: e.tensor_scalar(out=tms[:], in0=vns[:, g, :], scalar1=self.vcol("gm_ws00", g),
                                                       scalar2=self.vcol("gm_bs0", g), op0=ALU.mult, op1=ALU.add),
                 r=[f"vns{g}"], w=["tms"])
            R.op("dve", lambda e, g=g: e.tensor_tensor(out=A1[:, g, SEQ:T], in0=tms[:], in1=A1[:, g, SEQ:T], op=ALU.mult),
                 r=["tms", self.at(g, 4)], w=[self.at(g, 4)])
        R.dma("sp", self.dout["gvT"].rearrange("(c p) b -> p c b", p=128), vns[:],
              r=[f"vns{g}" for g in range(NCH)], w=["o_gv"], dsem="o_gv")

        O = [self.wload(w_out[:, 4 * j:4 * j + 4, :], "p (k n) -> p k n", k=4) for j in range(2)]

        def outp(i):
            c0, c1 = TB[i], TB[i + 1]
            n = c1 - c0
            for oc in range(NCH):
                b, ps, pt = self.bank()
                for k in range(NCH):
                    wv, wt = O[k // 4]
                    R.op("pe", lambda e, k=k, ps=ps, wv=wv, oc=oc: e.matmul(ps[:, 0:n], lhsT=wv[:, k % 4, oc * 128:(oc + 1) * 128],
                                                                      rhs=A1[:, k, c0:c1], start=(k == 0), stop=(k == NCH - 1)),
                         r=[wt, self.at(k, i)], w=[pt])
                self.add_to_xres(oc, i, ps, pt)

        for i in range(NTT):
            outp(i)

    def lru_mixer(self):
        R = self.R
        xres, H = self.xres, self.H
        XO = 3
        xbp = self.tmp("xbp", [128, XO + SEQ], F32)
        xbs = self.tmp("xbs", [128, NS], F32)
        xc32 = self.tmp("xc32", [128, T], F32)
        xcb = self.tmp("xcb", [128, T], BF16)
        gate = self.tmp("gate", [128, T], BF16)
        R1 = self.tmp("R1", [128, T], F32)
        I1 = self.tmp("I1", [128, T], F32)
        T2 = self.tmp("T2", [128, T], F32)
        Gc = xcb
        scT = self.tmp("scT", [128, NCH, 3, NS], F32)
        slT = self.tmp("slT", [128, NCH, NS], F32)
        convp = self.tmp("convp", [128, NCH, 3], F32)
        convs = self.tmp("convs", [128, NCH, 3, NS], F32)
        lrup = self.tmp("lrup", [128, NCH], F32)
        lrus = self.tmp("lrus", [128, NCH, NS], F32)
        nsp8 = self.tmp("nsp8", [128, NCH], F32)
        tsm = self.tmp("tsm", [128, NS], F32)
        R.dma("sp", scT[:], self.din["scT"].rearrange("(c p) j b -> p c j b", p=128), w=["scT"], dsem="scT")
        R.dma("sp", slT[:], self.din["slT"].rearrange("(c p) b -> p c b", p=128), w=["slT"], dsem="slT")
        lam = self.vecs[:, VEC_OFF["lru_lam"]:VEC_OFF["lru_lam"] + 8]
        R.op("act", lambda e: e.activation(out=nsp8[:], in_=lam, func=AF.Exp, scale=-1.0), w=["nsp8"])
        R.op("act", lambda e: e.activation(out=nsp8[:], in_=nsp8[:], func=AF.Ln, bias=self.one_col[:], scale=1.0),
             r=["nsp8"], w=["nsp8"])
        R.op("dve", lambda e: e.tensor_scalar(out=nsp8[:], in0=nsp8[:], scalar1=-8.0, scalar2=None, op0=ALU.mult),
             r=["nsp8"], w=["nsp8"])
        R.op("dve", lambda e: e.memset(xbp[:, 0:XO], 0.0), w=["xbp_h"])
        R.op("act", lambda e: e.activation(out=convs[:, :, 0:2, :], in_=scT[:, :, 1:3, :], func=AF.Copy), r=["scT"], w=["convs_a"])
        self.norm("lru_norm", 0)
        w_in = self.din["lru_w_in"][0].rearrange("(k p) n -> p k n", p=128)
        w_out = self.din["lru_w_out"][0].rearrange("(k p) n -> p k n", p=128)
        wa_d = self.din["lru_w_a"][0].rearrange("h i j -> i h j")
        wx_d = self.din["lru_w_x"][0].rearrange("h i j -> i h j")
        Wg = {}
        Wx_ = {}
        Wg[0] = self.wload(w_in[:, :, 0:512], "p (k n) -> p k n", k=8)
        Wx_[0] = self.wload(w_in[:, :, 1024:1536], "p (k n) -> p k n", k=8)
        (wa, wxx), wat = self.wload_parts([(wa_d, 0, "p (h j) -> p h j", dict(h=8)),
                                          (wx_d, 1024, "p (h j) -> p h j", dict(h=8))])
        Wo = [self.wload(w_out[:, 4 * j:4 * j + 4, :], "p (k n) -> p k n", k=4) for j in range(2)]

        def chunk(c):
            hh = c // 4
            cc = c % 4
            wg, wgt = Wg[hh]
            wxv, wxt = Wx_[hh]
            cw = [self.vcol("lru_conv_w", j * 8 + c) for j in range(4)]
            cb = self.vcol("lru_conv_b", c)

            def projt(i):
                c0, c1 = TB[i], TB[i + 1]
                n = c1 - c0
                b, ps, pt = self.bank()
                for k in range(NCH):
                    R.op("pe", lambda e, k=k, ps=ps: e.matmul(ps[:, 0:n], lhsT=wg[:, k, cc * 128:(cc + 1) * 128], rhs=H[:, k, c0:c1],
                                                              start=(k == 0), stop=(k == NCH - 1)), r=[wgt, self.ht(k, i)], w=[pt])
                R.op("act", lambda e, ps=ps: e.activation(out=gate[:, c0:c1], in_=ps[:, 0:n], func=AF.Gelu_apprx_tanh),
                     r=[pt], w=[f"gate{i}"])
                b, ps, pt = self.bank()
                for k in range(NCH):
                    R.op("pe", lambda e, k=k, ps=ps: e.matmul(ps[:, 0:n], lhsT=wxv[:, k, cc * 128:(cc + 1) * 128], rhs=H[:, k, c0:c1],
                                                              start=(k == 0), stop=(k == NCH - 1)), r=[wxt, self.ht(k, i)], w=[pt])
                if i < 4:
                    R.op("act", lambda e, ps=ps: e.activation(out=xbp[:, XO + c0:XO + c1], in_=ps[:, 0:n], func=AF.Copy),
                         r=[pt], w=[f"xbp{i}"])
                else:
                    R.op("act", lambda e, ps=ps: e.activation(out=xbs[:], in_=ps[:, 0:n], func=AF.Copy), r=[pt], w=["xbs"])

            for i in range(NTT):
                projt(i)
            xbt = [f"xbp{i}" for i in range(4)] + ["xbp_h"]
            R.op("act", lambda e: e.activation(out=convp[:, c, :], in_=xbp[:, XO + SEQ - 3:XO + SEQ], func=AF.Copy),
                 r=["xbp3"], w=[f"convp{c}"])
            R.op("act", lambda e: e.activation(out=convs[:, c, 2, :], in_=xbs[:], func=AF.Copy), r=["xbs"], w=[f"convs_b{c}"])
            R.op("dve", lambda e: e.tensor_scalar(out=xc32[:, 0:SEQ], in0=xbp[:, XO:XO + SEQ], scalar1=cw[3], scalar2=cb,
                                                  op0=ALU.mult, op1=ALU.add), r=xbt, w=["xc_p"])
            for s_ in (1, 2, 3):
                R.op("dve", lambda e, s_=s_: e.scalar_tensor_tensor(out=xc32[:, 0:SEQ], in0=xbp[:, XO - s_:XO - s_ + SEQ],
                                                                     scalar=cw[3 - s_], in1=xc32[:, 0:SEQ], op0=ALU.mult, op1=ALU.add),
                     r=xbt + ["xc_p"], w=["xc_p"])
            R.op("dve", lambda e: e.tensor_scalar(out=xc32[:, SEQ:T], in0=xbs[:], scalar1=cw[3], scalar2=cb,
                                                  op0=ALU.mult, op1=ALU.add), r=["xbs"], w=["xc_s"])
            for j in range(3):
                R.op("dve", lambda e, j=j: e.scalar_tensor_tensor(out=xc32[:, SEQ:T], in0=scT[:, c, j, :], scalar=cw[j],
                                                                   in1=xc32[:, SEQ:T], op0=ALU.mult, op1=ALU.add),
                     r=["scT", "xc_s"], w=["xc_s"])
            R.op("act", lambda e: e.activation(out=xcb[:], in_=xc32[:], func=AF.Copy), r=["xc_p", "xc_s"], w=["xcb"])

            def gates(i):
                c0, c1 = TB[i], TB[i + 1]
                n = c1 - c0
                b, ps, pt = self.bank()
                R.op("pe", lambda e, ps=ps: e.matmul(ps[:, 0:n], lhsT=wa[:, c, :], rhs=xcb[:, c0:c1], start=True, stop=True),
                     r=[wat, "xcb"], w=[pt])
                R.op("act", lambda e, ps=ps: e.activation(out=R1[:, c0:c1], in_=ps[:, 0:n], func=AF.Sigmoid,
                                                          bias=self.vcol("lru_b_a", c)), r=[pt], w=[f"R1_{i}"])
                b, ps, pt = self.bank()
                R.op("pe", lambda e, ps=ps: e.matmul(ps[:, 0:n], lhsT=wxx[:, c, :], rhs=xcb[:, c0:c1], start=True, stop=True),
                     r=[wat, "xcb"], w=[pt])
                R.op("act", lambda e, ps=ps: e.activation(out=I1[:, c0:c1], in_=ps[:, 0:n], func=AF.Sigmoid,
                                                          bias=self.vcol("lru_b_x", c)), r=[pt], w=[f"I1_{i}"])

            for i in range(NTT):
                gates(i)
            r1t = [f"R1_{i}" for i in range(NTT)]
            i1t = [f"I1_{i}" for i in range(NTT)]
            R.op("act", lambda e: e.activation(out=R1[:], in_=R1[:], func=AF.Exp, scale=nsp8[:, c:c + 1]),
                 r=r1t + ["nsp8"], w=["a"])
            R.op("act", lambda e: e.activation(out=T2[:], in_=R1[:], func=AF.Square), r=["a"], w=["T2"])
            R.op("act", lambda e: e.activation(out=T2[:], in_=T2[:], func=AF.Sqrt, bias=self.one_col[:], scale=-1.0),
                 r=["T2"], w=["T2"])
            R.op("dve", lambda e: e.memset(T2[:, 0:1], 1.0), r=["T2"], w=["T2"])
            R.op("dve", lambda e: e.tensor_tensor(out=I1[:], in0=I1[:], in1=xc32[:], op=ALU.mult),
                 r=i1t + ["xc_p", "xc_s"], w=["b1"])
            R.op("dve", lambda e: e.tensor_tensor(out=I1[:], in0=I1[:], in1=T2[:], op=ALU.mult), r=["b1", "T2"], w=["b1"])
            R.op("dve", lambda e: e.tensor_tensor(out=tsm[:], in0=R1[:, SEQ:T], in1=slT[:, c, :], op=ALU.mult),
                 r=["a", "slT"], w=["tsm"])
            R.op("dve", lambda e: e.tensor_tensor(out=I1[:, SEQ:T], in0=I1[:, SEQ:T], in1=tsm[:], op=ALU.add),
                 r=["b1", "tsm"], w=["b1"])
            R.op("dve", lambda e: e.tensor_tensor_scan(out=T2[:, 0:SEQ], data0=R1[:, 0:SEQ], data1=I1[:, 0:SEQ], initial=0.0,
                                                       op0=ALU.mult, op1=ALU.add), r=["a", "b1", "T2"], w=["hs"])
            R.op("act", lambda e: e.activation(out=T2[:, SEQ:T], in_=I1[:, SEQ:T], func=AF.Copy), r=["b1", "hs"], w=["hs_s"])
            R.op("act", lambda e: e.activation(out=lrup[:, c:c + 1], in_=T2[:, SEQ - 1:SEQ], func=AF.Copy), r=["hs"], w=[f"lrup{c}"])
            R.op("act", lambda e: e.activation(out=lrus[:, c, :], in_=T2[:, SEQ:T], func=AF.Copy), r=["hs_s"], w=[f"lrus{c}"])
            R.op("dve", lambda e: e.tensor_tensor(out=Gc[:], in0=T2[:], in1=gate[:], op=ALU.mult),
                 r=["hs", "hs_s", "xcb"] + [f"gate{i}" for i in range(NTT)], w=["Gc"])

            def outp(i):
                c0, c1 = TB[i], TB[i + 1]
                n = c1 - c0
                wo, wot = Wo[c // 4]
                for oc in range(NCH):
                    b, ps, pt = self.bank()
                    R.op("pe", lambda e, ps=ps, oc=oc: e.matmul(ps[:, 0:n], lhsT=wo[:, c % 4, oc * 128:(oc + 1) * 128], rhs=Gc[:, c0:c1],
                                                                start=True, stop=True), r=[wot, "Gc"], w=[pt])
                    self.add_to_xres(oc, i, ps, pt)

            for i in range(NTT):
                outp(i)

        for c in range(NCH):
            if c == 4:
                Wg[1] = self.wload(w_in[:, :, 512:1024], "p (k n) -> p k n", k=8)
                Wx_[1] = self.wload(w_in[:, :, 1536:2048], "p (k n) -> p k n", k=8)
            chunk(c)
        R.dma("sp", self.dout["convpT"].rearrange("(c p) j -> p c j", p=128), convp[:],
              r=[f"convp{c}" for c in range(NCH)], w=["o_convp"], dsem="o_convp")
        R.dma("sp", self.dout["convsT"].rearrange("(c p) j b -> p c j b", p=128), convs[:],
              r=["convs_a"] + [f"convs_b{c}" for c in range(NCH)], w=["o_convs"], dsem="o_convs")
        R.dma("sp", self.dout["lrupT"], lrup[:],
              r=[f"lrup{c}" for c in range(NCH)], w=["o_lrup"], dsem="o_lrup")
        R.dma("sp", self.dout["lrusT"].rearrange("(c p) b -> p c b", p=128), lrus[:],
              r=[f"lrus{c}" for c in range(NCH)], w=["o_lrus"], dsem="o_lrus")

    def ret_mixer(self):
        R = self.R
        xres, H = self.xres, self.H
        DKS = 128 ** -0.5
        lg = np.log1p(-np.exp2(-5.0 - np.arange(8, dtype=np.float32))).astype(np.float32)
        cdec = [float(np.exp(lg[h] * np.float32(128.0))) for h in range(8)]
        gam = [float(np.exp(lg[h] * np.float32(1.0))) for h in range(8)]
        tmp = self.tmp
        G2 = tmp("G2", [128, 4, T], BF16)
        qs = tmp("qs", [128, 512], BF16)
        qT = tmp("qT", [128, 512], BF16)
        qd = tmp("qd", [128, 512], BF16)
        ks = tmp("ks", [128, 512], BF16)
        kT = tmp("kT", [128, 512], BF16)
        kdt = tmp("kdt", [128, 4, 128], BF16)
        vtk = tmp("vtk", [128, 4, 256], BF16)
        gs = tmp("gs", [128, 2, 512], BF16)
        tA = tmp("tA", [128, 512], F32)
        tB = tmp("tB", [128, 512], F32)
        sc = tmp("sc", [128, 4, 128], BF16)
        o32 = tmp("o32", [128, 2, 512], F32)
        ob = tmp("ob", [128, 2, 512], BF16)
        osq = tmp("osq", [128, 2, 512], BF16)
        S32 = tmp("S32", [128, 256], F32)
        Sbf = tmp("Sbf", [128, 256], BF16)
        dmk = tmp("dmk", [128, 128], F32)
        qdc = tmp("qdc", [128, 128], F32)
        kdc = tmp("kdc", [128, 8], F32)
        gne = tmp("gne", [128, 1], F32)
        permM = tmp("permM", [128, 128], BF16)
        identb = tmp("identb", [128, 128], BF16)
        Sst = [tmp(f"Sst{j}", [128, 4, 256], F32) for j in range(2)]
        Km = tmp("Km", [NS, NS, 128], BF16)
        ktk_s = tmp("ktk_s", [NS, 128], BF16)
        vtk_s = tmp("vtk_s", [NS, 256], BF16)
        qs32 = tmp("qs32", [128, NS], F32)
        ks32 = tmp("ks32", [128, NS], F32)
        qds32 = tmp("qds32", [128, NS], F32)
        prodb = tmp("prodb", [128, NS], BF16)
        vTs = tmp("vTs", [128, 2, NS], F32)
        os_ = tmp("os_", [128, 2, NS], F32)
        R.dma("pool", permM[:], self.din["permM"], w=["permM"], dsem="permM")
        R.dma("pool", identb[:], self.din["ident"], w=["identb"], dsem="identb")
        R.dma("sp", kdc[:], self.din["kdec"], w=["kdc"], dsem="kdc")
        R.op("dve", lambda e: e.memset(gne[:], GN_EPS), w=["gne"])
        self.norm("ret_norm", 0)
        R.barrier()
        ctab = [self.rstd[:, 0:512], self.rstd[:, 1024:1536]]
        stab = [self.rstd[:, 512:1024], self.rstd[:, 1536:2048]]
        w_in = self.din["ret_w_in"][0].rearrange("(k p) n -> p k n", p=128)
        w_out = self.din["ret_w_out"][0].rearrange("(f p) n -> p f n", p=128)
        GRP = [(0, 512), (512, 1024), (1024, 1536), (1536, 2048), (2048, 2064)]
        self._rope_i = 0

        def head(hh, Wo):
            i = self.w_i
            self.w_i = (self.w_i + 1) % self.NW
            slot = self.ws[i]
            WA = slot[:, 0:4096].rearrange("p (k n) -> p k n", k=8)
            wat = f"ws{i}"
            for (lo, hi, src0) in ((0, 128, 128 * hh), (128, 256, 1024 + 128 * hh), (256, 512, 2048 + 256 * hh)):
                R.dma("pool", WA[:, :, lo:hi], w_in[:, :, src0:src0 + (hi - lo)], w=[wat], dsem=wat)
            WB, wbt = self.wload(w_in[:, :, 4096 + 256 * hh:4096 + 256 * hh + 256], "p (k n) -> p k n", k=8)
            R.dma("sp", dmk[:], self.din["dmaskT"][hh], w=["dmk"], dsem="dmk")
            R.dma("sp", qdc[:], self.din["qdecb"][hh], w=["qdc"], dsem="qdc")
            R.op("dve", lambda e: e.memset(S32[:], 0.0), w=["S32"])
            R.op("dve", lambda e: e.memset(Sbf[:], 0.0), w=["Sbf"])
            g2s = (hh % 2) * 2

            def rope(src_bf, psP, pt, n, ct, st_, ctt, stt_, out_ap, outt):
                R.op("dve", lambda e: e.tensor_tensor(out=tA[:, 0:n], in0=src_bf[:, 0:n], in1=ct[:, 0:n], op=ALU.mult),
                     r=[src_bf.tensor.name if False else outt + "_src", ctt], w=["tA"])
                R.op("dve", lambda e: e.tensor_tensor(out=tB[:, 0:n], in0=psP[:, 0:n], in1=st_[:, 0:n], op=ALU.mult),
                     r=[pt, stt_], w=["tB"])
                R.op("dve", lambda e: e.tensor_tensor(out=out_ap, in0=tA[:, 0:n], in1=tB[:, 0:n], op=ALU.add),
                     r=["tA", "tB"], w=[outt])

            def qk_proj(col_lo, c0, c1, i, dst_s, dst_t, scale):
                n = c1 - c0
                b, ps, pt = self.bank()
                for k in range(NCH):
                    R.op("pe", lambda e, k=k, ps=ps: e.matmul(ps[:, 0:n], lhsT=WA[:, k, col_lo:col_lo + 128], rhs=H[:, k, c0:c1],
                                                              start=(k == 0), stop=(k == NCH - 1)), r=[wat, self.ht(k, i)], w=[pt])
                R.op("act", lambda e, ps=ps: e.activation(out=dst_s[:, 0:n], in_=ps[:, 0:n], func=AF.Copy, scale=scale),
                     r=[pt], w=[dst_t + "_src"])
                b, ps2, pt2 = self.bank()
                R.op("pe", lambda e, ps2=ps2: e.matmul(ps2[:, 0:n], lhsT=permM[:], rhs=dst_s[:, 0:n], start=True, stop=True),
                     r=["permM", dst_t + "_src"], w=[pt2])
                return ps2, pt2

            def gnorm(src, srct, n, cols0, i):
                for vc in range(2):
                    R.op("act", lambda e, vc=vc: e.activation(out=o32[:, vc, 0:n], in_=src[vc], func=AF.Copy), r=[srct[vc]], w=[f"o32_{vc}"])
                    R.op("act", lambda e, vc=vc: e.activation(out=osq[:, vc, 0:n], in_=src[vc], func=AF.Square), r=[srct[vc]], w=[f"osq{vc}"])
                    R.op("act", lambda e, vc=vc: e.activation(out=ob[:, vc, 0:n], in_=src[vc], func=AF.Copy), r=[srct[vc]], w=[f"ob{vc}"])
                b, p1, p1t = self.bank()
                for vc in range(2):
                    R.op("pe", lambda e, vc=vc: e.matmul(p1[:, 0:n], lhsT=self.ones_bf[:], rhs=ob[:, vc, 0:n], start=(vc == 0), stop=(vc == 1)),
                         r=[f"ob{vc}"], w=[p1t])
                b, p2, p2t = self.bank()
                for vc in range(2):
                    R.op("pe", lambda e, vc=vc: e.matmul(p2[:, 0:n], lhsT=self.ones_bf[:], rhs=osq[:, vc, 0:n], start=(vc == 0), stop=(vc == 1)),
                         r=[f"osq{vc}"], w=[p2t])
                R.op("act", lambda e: e.activation(out=tA[:, 0:n], in_=p1[:, 0:n], func=AF.Copy, scale=1.0 / 256), r=[p1t], w=["tA"])
                R.op("dve", lambda e: e.tensor_tensor(out=tB[:, 0:n], in0=tA[:, 0:n], in1=tA[:, 0:n], op=ALU.mult), r=["tA"], w=["tB"])
                R.op("dve", lambda e: e.scalar_tensor_tensor(out=tB[:, 0:n], in0=p2[:, 0:n], scalar=1.0 / 256, in1=tB[:, 0:n],
                                                             op0=ALU.mult, op1=ALU.subtract), r=[p2t, "tB"], w=["tB"])
                R.op("act", lambda e: e.activation(out=tB[:, 0:n], in_=tB[:, 0:n], func=AF.Sqrt, bias=gne[:], scale=1.0),
                     r=["tB", "gne"], w=["tB"])
                R.op("dve", lambda e: e.reciprocal(out=tB[:, 0:n], in_=tB[:, 0:n]), r=["tB"], w=["tB"])
                for vc in range(2):
                    ot = f"o32_{vc}"
                    R.op("dve", lambda e, vc=vc: e.tensor_tensor(out=o32[:, vc, 0:n], in0=o32[:, vc, 0:n], in1=tA[:, 0:n], op=ALU.subtract),
                         r=[ot, "tA"], w=[ot])
                    R.op("dve", lambda e, vc=vc: e.tensor_tensor(out=o32[:, vc, 0:n], in0=o32[:, vc, 0:n], in1=tB[:, 0:n], op=ALU.mult),
                         r=[ot, "tB"], w=[ot])
                    R.op("dve", lambda e, vc=vc: e.tensor_scalar(out=o32[:, vc, 0:n], in0=o32[:, vc, 0:n],
                                                                 scalar1=self.vcol("ret_gn_g", 2 * hh + vc),
                                                                 scalar2=self.vcol("ret_gn_b", 2 * hh + vc), op0=ALU.mult, op1=ALU.add),
                         r=[ot], w=[ot])
                    R.op("dve", lambda e, vc=vc: e.tensor_tensor(out=G2[:, g2s + vc, cols0:cols0 + n], in0=o32[:, vc, 0:n],
                                                                 in1=gs[:, vc, 0:n], op=ALU.mult),
                         r=[ot, f"gs{vc}"], w=[f"g2_{g2s + vc}_{i}"])

            def group(gi):
                c0, c1 = GRP[gi]
                n = c1 - c0
                i = gi
                samp = (gi == 4)
                j = self._rope_i % 2
                self._rope_i += 1
                R.dma("sp", ctab[j][:, 0:n], self.din["ropeC"][:, c0:c1], w=[f"ctab{j}"], dsem=f"ctab{j}")
                R.dma("sp", stab[j][:, 0:n], self.din["ropeS"][:, c0:c1], w=[f"stab{j}"], dsem=f"stab{j}")
                ct, st_ = ctab[j], stab[j]
                psP, ptP = qk_proj(0, c0, c1, i, qs, "qT", 1.0)
                rope(qs, psP, ptP, n, ct, st_, f"ctab{j}", f"stab{j}", (qs32[:] if samp else qT[:, 0:n]), "qT")
                psP, ptP = qk_proj(128, c0, c1, i, ks, "kT", DKS)
                rope(ks, psP, ptP, n, ct, st_, f"ctab{j}", f"stab{j}", (ks32[:] if samp else kT[:, 0:n]), "kT")
                for vc in range(2):
                    b, ps, pt = self.bank()
                    for k in range(NCH):
                        R.op("pe", lambda e, k=k, ps=ps, vc=vc: e.matmul(ps[:, 0:n], lhsT=WB[:, k, vc * 128:(vc + 1) * 128], rhs=H[:, k, c0:c1],
                                                                         start=(k == 0), stop=(k == NCH - 1)), r=[wbt, self.ht(k, i)], w=[pt])
                    R.op("act", lambda e, ps=ps, vc=vc: e.activation(out=gs[:, vc, 0:n], in_=ps[:, 0:n], func=AF.Silu), r=[pt], w=[f"gs{vc}"])
                if not samp:
                    R.op("dve", lambda e: e.tensor_tensor(out=qd[:].rearrange("p (j t) -> p j t", j=4), in0=qT[:].rearrange("p (j t) -> p j t", j=4),
                                                          in1=qdc[:].unsqueeze(1).to_broadcast([128, 4, 128]), op=ALU.mult),
                         r=["qT", "qdc"], w=["qd"])
                    b, ps, pt = self.bank()
                    for jj in range(4):
                        R.op("pe", lambda e, jj=jj, ps=ps: e.matmul(ps[:, jj * 128:(jj + 1) * 128], lhsT=kT[:, jj * 128:(jj + 1) * 128], rhs=identb[:],
                                                                    start=True, stop=True), r=["kT", "identb"], w=[pt])
                    R.op("act", lambda e, ps=ps: e.activation(out=kdt[:].rearrange("p j d -> p (j d)"), in_=ps[:], func=AF.Copy,
                                                              scale=kdc[:, hh:hh + 1]), r=[pt, "kdc"], w=["kdt"])
                    for half in range(2):
                        b, ps, pt = self.bank()
                        for jq in range(2):
                            jj = 2 * half + jq
                            for k in range(NCH):
                                R.op("pe", lambda e, k=k, ps=ps, jj=jj, jq=jq: e.matmul(
                                    ps[:, jq * 256:(jq + 1) * 256], lhsT=H[:, k, c0 + jj * 128:c0 + (jj + 1) * 128], rhs=WA[:, k, 256:512],
                                    start=(k == 0), stop=(k == NCH - 1)), r=[wat, self.ht(k, i)], w=[pt])
                        R.op("act", lambda e, ps=ps, half=half: e.activation(out=vtk[:, 2 * half:2 * half + 2, :].rearrange("p j v -> p (j v)"),
                                                                             in_=ps[:], func=AF.Copy), r=[pt], w=[f"vtk{half}"])
                    b, ps, pt = self.bank()
                    for jj in range(4):
                        R.op("pe", lambda e, jj=jj, ps=ps: e.matmul(ps[:, jj * 128:(jj + 1) * 128], lhsT=kT[:, jj * 128:(jj + 1) * 128],
                                                                    rhs=qT[:, jj * 128:(jj + 1) * 128], start=True, stop=True),
                             r=["kT", "qT"], w=[pt])
                    R.op("dve", lambda e, ps=ps: e.tensor_tensor(out=sc[:], in0=ps[:].rearrange("p (j t) -> p j t", j=4),
                                                                 in1=dmk[:].unsqueeze(1).to_broadcast([128, 4, 128]), op=ALU.mult),
                         r=[pt, "dmk"], w=["sc"])
                    bo = [self.bank(), self.bank()]
                    for jj in range(4):
                        for vc in range(2):
                            _, po, pot = bo[vc]
                            R.op("pe", lambda e, jj=jj, vc=vc, po=po: e.matmul(po[:, jj * 128:(jj + 1) * 128], lhsT=vtk[:, jj, vc * 128:(vc + 1) * 128],
                                                                               rhs=sc[:, jj, :], start=True, stop=False),
                                 r=[f"vtk{jj // 2}", "sc"], w=[pot])
                            R.op("pe", lambda e, jj=jj, vc=vc, po=po: e.matmul(po[:, jj * 128:(jj + 1) * 128], lhsT=Sbf[:, vc * 128:(vc + 1) * 128],
                                                                               rhs=qd[:, jj * 128:(jj + 1) * 128], start=False, stop=True),
                                 r=["Sbf", "qd"], w=[pot])
                        b, psS, psSt = self.bank()
                        R.op("pe", lambda e, jj=jj, psS=psS: e.matmul(psS[:, 0:256], lhsT=kdt[:, jj, :], rhs=vtk[:, jj, :], start=True, stop=True),
                             r=["kdt", f"vtk{jj // 2}"], w=[psSt])
                        R.op("dve", lambda e, psS=psS: e.scalar_tensor_tensor(out=S32[:], in0=S32[:], scalar=cdec[hh], in1=psS[:, 0:256],
                                                                              op0=ALU.mult, op1=ALU.add), r=[psSt, "S32"], w=["S32"])
                        R.op("act", lambda e: e.activation(out=Sbf[:], in_=S32[:], func=AF.Copy), r=["S32"], w=["Sbf"])
                    gnorm([bo[0][1][:, 0:n], bo[1][1][:, 0:n]], [bo[0][2], bo[1][2]], n, c0, i)
                else:
                    R.op("dve", lambda e: e.tensor_scalar(out=qds32[:], in0=qs32[:], scalar1=gam[hh], scalar2=None, op0=ALU.mult),
                         r=["qT"], w=["qds32"])
                    R.op("dve", lambda e: e.tensor_tensor(out=prodb[:], in0=qs32[:], in1=ks32[:], op=ALU.mult), r=["qT", "kT"], w=["prodb"])
                    R.op("act", lambda e: e.activation(out=kT[:, 0:NS], in_=ks32[:], func=AF.Copy), r=["kT"], w=["kTs"])
                    bd, pd, pdt = self.bank()
                    self.hold.add(bd)
                    R.op("pe", lambda e: e.matmul(pd[:, 0:NS], lhsT=self.ones_bf[:], rhs=prodb[:], start=True, stop=True), r=["prodb"], w=[pdt])
                    for vc in range(2):
                        b, ps, pt = self.bank()
                        for k in range(NCH):
                            R.op("pe", lambda e, k=k, ps=ps, vc=vc: e.matmul(ps[:, 0:NS], lhsT=WA[:, k, 256 + vc * 128:256 + (vc + 1) * 128],
                                                                             rhs=H[:, k, c0:c1], start=(k == 0), stop=(k == NCH - 1)),
                                 r=[wat, self.ht(k, i)], w=[pt])
                        R.op("act", lambda e, ps=ps, vc=vc: e.activation(out=vTs[:, vc, :], in_=ps[:, 0:NS], func=AF.Copy), r=[pt], w=[f"vTs{vc}"])
                    b, ps, pt = self.bank()
                    for k in range(NCH):
                        R.op("pe", lambda e, k=k, ps=ps: e.matmul(ps[0:NS, 0:256], lhsT=H[:, k, c0:c1], rhs=WA[:, k, 256:512],
                                                                  start=(k == 0), stop=(k == NCH - 1)), r=[wat, self.ht(k, i)], w=[pt])
                    R.op("act", lambda e, ps=ps: e.activation(out=vtk_s[:], in_=ps[0:NS, 0:256], func=AF.Copy), r=[pt], w=["vtk_s"])
                    b, ps, pt = self.bank()
                    R.op("pe", lambda e, ps=ps: e.matmul(ps[0:NS, 0:128], lhsT=kT[:, 0:NS], rhs=identb[:], start=True, stop=True),
                         r=["kTs", "identb"], w=[pt])
                    R.op("act", lambda e, ps=ps: e.activation(out=ktk_s[:], in_=ps[0:NS, 0:128], func=AF.Copy), r=[pt], w=["ktk_s"])
                    R.op("dve", lambda e: e.tensor_tensor(out=Km[:], in0=ktk_s[:].unsqueeze(1).to_broadcast([NS, NS, 128]),
                                                          in1=self.ident_f[0:NS, 0:NS].unsqueeze(2).to_broadcast([NS, NS, 128]), op=ALU.mult),
                         r=["ktk_s", "ident"], w=["Km"])
                    bo = [self.bank(), self.bank()]
                    self.hold.update([bo[0][0], bo[1][0]])
                    for sbi in range(4):
                        St = Sst[sbi % 2]
                        Stt = f"Sst{sbi % 2}"
                        b0 = 4 * sbi
                        R.dma("sp", St[:], self.din["sret"][b0:b0 + 4, hh].rearrange("b d v -> d b v"), w=[Stt], dsem=Stt)
                        for bi in range(4):
                            bb = b0 + bi
                            for vc in range(2):
                                _, po, pot = bo[vc]
                                R.op("pe", lambda e, bi=bi, bb=bb, vc=vc, po=po, St=St: e.matmul(
                                    po[:, bb:bb + 1], lhsT=St[:, bi, vc * 128:(vc + 1) * 128], rhs=qds32[:, bb:bb + 1], start=True, stop=True),
                                    r=[Stt, "qds32"], w=[pot])
                        for bi in range(4):
                            bb = b0 + bi
                            b, psS, psSt = self.bank()
                            R.op("pe", lambda e, bb=bb, psS=psS: e.matmul(psS[:, 0:256], lhsT=Km[:, bb, :], rhs=vtk_s[:], start=True, stop=True),
                                 r=["Km", "vtk_s"], w=[psSt])
                            R.op("dve", lambda e, bi=bi, psS=psS, St=St: e.scalar_tensor_tensor(
                                out=St[:, bi, :], in0=St[:, bi, :], scalar=gam[hh], in1=psS[:, 0:256], op0=ALU.mult, op1=ALU.add),
                                r=[psSt, Stt], w=[Stt])
                        R.dma("sp", self.dout["rets"][b0:b0 + 4, hh].rearrange("b d v -> d b v"), St[:], r=[Stt], w=[f"o_rets{sbi % 2}"],
                              dsem=f"o_rets{sbi % 2}")
                    for vc in range(2):
                        R.op("dve", lambda e, vc=vc: e.tensor_tensor(out=os_[:, vc, :], in0=vTs[:, vc, :], in1=pd[:, 0:NS], op=ALU.mult),
                             r=[f"vTs{vc}", pdt], w=[f"os{vc}"])
                        R.op("dve", lambda e, vc=vc: e.tensor_tensor(out=os_[:, vc, :], in0=os_[:, vc, :], in1=bo[vc][1][:, 0:NS], op=ALU.add),
                             r=[f"os{vc}", bo[vc][2]], w=[f"os{vc}"])
                    self.hold.clear()
                    gnorm([os_[:, 0, :], os_[:, 1, :]], ["os0", "os1"], NS, c0, i)

            for gi in range(5):
                group(gi)
                if gi == 3:
                    R.dma("sp", self.dout["retp"][hh], S32[:], r=["S32"], w=["o_retp"], dsem="o_retp")
            if hh % 2 == 1:
                wo, wot = Wo
                for i in range(NTT):
                    self._ret_out(i, wo, wot, G2)

        for hh in range(8):
            Wo = None
            if hh % 2 == 1:
                pr = hh // 2
                Wo = None
            head(hh, (self.wload(w_out[:, 4 * (hh // 2):4 * (hh // 2) + 4, :], "p (f n) -> p f n", f=4) if hh % 2 == 1 else None))

    def _ret_out(self, i, wo, wot, G2):
        R = self.R
        c0, c1 = TB[i], TB[i + 1]
        n = c1 - c0
        for oc in range(NCH):
            b, ps, pt = self.bank()
            for j in range(4):
                R.op("pe", lambda e, j=j, ps=ps, oc=oc: e.matmul(ps[:, 0:n], lhsT=wo[:, j, oc * 128:(oc + 1) * 128], rhs=G2[:, j, c0:c1],
                                                                 start=(j == 0), stop=(j == 3)), r=[wot, f"g2_{j}_{i}"], w=[pt])
            self.add_to_xres(oc, i, ps, pt)

    def build(self):
        nc, R = self.nc, self.R
        inp, outp = self.inp, self.outp
        mix = [m for m in self.mixers if m < self.nlayers]
        xT = inp("xT", [D, T])
        vecs_d = inp("vecs", [128, NVEC])
        ident_d = inp("ident", [128, 128])
        inp("mlp_w1", [4, D, DFF])
        inp("mlp_w2", [4, DFF, D])
        yT = outp("yT", [D, T])
        if 0 in mix:
            inp("pool_w", [1, 4, 256, 256])
            inp("spT", [D, NS, 15])
            inp("rc16", [128, 4, 16])
            outp("poolpT", [D, 15])
            outp("poolsT", [D, NS, 15])
        if 1 in mix:
            inp("gm_w_in", [1, D, 2 * D])
            inp("gm_w_out", [1, D, D])
            inp("gm_b_in", [1, 2 * D])
            inp("gm_b_s", [1, 8, 128])
            inp("gm_wsT", [128, 8, 128])
            inp("trilT", [128, 128])
            outp("gvT", [D, NS])
        if 2 in mix:
            inp("ret_w_in", [1, D, 6144])
            inp("ret_w_out", [1, 2048, D])
            inp("sret", [NS, 8, 128, 256])
            inp("permM", [128, 128])
            inp("kdec", [128, 8])
            inp("dmaskT", [8, 128, 128])
            inp("qdecb", [8, 128, 128])
            inp("ropeC", [128, T])
            inp("ropeS", [128, T])
            outp("retp", [8, 128, 256])
            outp("rets", [NS, 8, 128, 256])
        if 3 in mix:
            inp("lru_w_in", [1, D, 2 * D])
            inp("lru_w_out", [1, D, D])
            inp("lru_w_a", [1, 8, 128, 128])
            inp("lru_w_x", [1, 8, 128, 128])
            inp("scT", [D, 3, NS])
            inp("slT", [D, NS])
            outp("convpT", [D, 3])
            outp("convsT", [D, 3, NS])
            outp("lrupT", [128, NCH])
            outp("lrusT", [D, NS])
        self.xres = self.sb("xres", [128, NCH, T], F32)
        self.H = self.sb("H", [128, NCH, T], BF16)
        self.NW = 5
        self.ws = [self.sb(f"wslot{i}", [128, 4096], BF16) for i in range(self.NW)]
        self.w_i = 0
        self.vecs = self.sb("vecs_sb", [128, NVEC], F32)
        self.ones_bf = self.sb("ones_bf", [128, 128], BF16)
        self.ident_f = self.sb("ident_f", [128, 128], F32)
        self.eps_col = self.sb("eps_col", [128, 1], F32)
        self.one_col = self.sb("one_col", [128, 1], F32)
        self.rstd = self.sb("rstd", [128, T], F32)
        self.sq = [self.sb(f"sq{i}", [128, 512], BF16) for i in range(2)]
        self.ones_row_b = self.sb("ones_row_b", [1, 128], BF16)
        self.ones_row_f = self.sb("ones_row_f", [1, 128], F32)
        self.ps = [nc.alloc_psum_tensor(f"psb{i}", [128, 512], F32) for i in range(8)]
        self.t_base = (nc.sbuf_base + 63) // 64 * 64
        self.t_end = nc.sbuf_top
        self.t_off = self.t_base
        self.tmp_n = 0

        R.op("dve", lambda e: e.memset(self.ones_bf[:], 1.0), w=["ones"])
        R.op("dve", lambda e: e.memset(self.eps_col[:], EPS), w=["epsc"])
        R.op("dve", lambda e: e.memset(self.one_col[:], 1.0), w=["onec"])
        R.op("dve", lambda e: e.memset(self.ones_row_b[:], 1.0), w=["onesrb"])
        R.op("dve", lambda e: e.memset(self.ones_row_f[:], 1.0), w=["onesrf"])
        R.dma("sp", self.vecs[:], vecs_d, w=["vecs"], dsem="vecs")
        R.dma("sp", self.ident_f[:], ident_d, w=["ident"], dsem="vecs")
        xv = xT.rearrange("(c p) t -> p c t", p=128)
        for c in range(NCH):
            R.dma("sp", self.xres[:, c, :], xv[:, c, :], w=self.xtoks(c), dsem="xin")

        fns = {0: self.pool_mixer, 1: self.gmlp_mixer, 2: self.ret_mixer, 3: self.lru_mixer}
        for li in range(self.nlayers):
            m = li % 4
            if m in mix:
                self.tmp_reset()
                fns[m]()
            if not NOMLP:
                self.tmp_reset()
                self.mlp(li)

        self.tmp_reset()
        self.norm("final_norm", 0, out_f32_inplace=True)
        yv = yT.rearrange("(c p) t -> p c t", p=128)
        for c in range(NCH):
            R.dma("sp", yv[:, c, :], self.xres[:, c, :], r=self.xtoks(c), w=[f"yout{c}"], dsem="yout")
        R.emit()

    def host_const(self, n, inputs, core):
        f = lambda k: np.asarray(inputs[k], np.float32)
        sl = slice(core * NS, (core + 1) * NS)
        if n == "ident":
            return np.eye(128, dtype=np.float32)
        if n == "spT":
            return np.ascontiguousarray(f("state_pool")[0, sl].transpose(2, 0, 1))
        if n == "rc16":
            t = np.arange(16, dtype=np.float32)
            rc = np.stack([1.0 / np.minimum(t + 1, w) for w in (2, 4, 8, 16)], 0).astype(np.float32)
            return np.ascontiguousarray(np.broadcast_to(rc[None], (128, 4, 16)))
        if n == "sret":
            return np.ascontiguousarray(f("state_ret")[0, sl])
        if n == "permM":
            k = np.arange(128)
            return (k[:, None] == ((k[None, :] + 64) % 128)).astype(np.float32)
        if n in ("kdec", "dmaskT", "qdecb"):
            lg = np.log1p(-np.exp2(-5.0 - np.arange(8, dtype=np.float32))).astype(np.float32)
            idx = np.arange(128, dtype=np.float32)
            if n == "kdec":
                return np.ascontiguousarray(np.exp(lg[None, :] * (127.0 - idx)[:, None]).astype(np.float32))
            if n == "qdecb":
                qd_ = np.exp(lg[:, None] * (idx + 1.0)[None, :]).astype(np.float32)
                return np.ascontiguousarray(np.broadcast_to(qd_[:, None, :], (8, 128, 128)))
            diff = idx[None, :] - idx[:, None]
            dm = np.where(diff[None] >= 0, np.exp(lg[:, None, None] * np.maximum(diff, 0.0)[None]), 0.0)
            return np.ascontiguousarray(dm.astype(np.float32))
        if n in ("ropeC", "ropeS"):
            half = 64
            freqs = np.exp(np.float32(-math.log(10000.0)) * np.arange(half, dtype=np.float32) / np.float32(half)).astype(np.float32)
            pos = np.concatenate([np.arange(SEQ), np.full(NS, PAST)]).astype(np.float32)
            ang = (pos[None, :] * freqs[:, None]).astype(np.float32)
            if n == "ropeC":
                c_ = np.cos(ang).astype(np.float32)
                return np.ascontiguousarray(np.concatenate([c_, c_], 0))
            s_ = np.sin(ang).astype(np.float32)
            return np.ascontiguousarray(np.concatenate([-s_, s_], 0))
        if n == "scT":
            return np.ascontiguousarray(f("state_conv")[0, sl].transpose(2, 1, 0))
        if n == "slT":
            return np.ascontiguousarray(f("state_lru")[0, sl].T)
        if n == "gm_wsT":
            return np.ascontiguousarray(f("gm_w_s")[0].transpose(2, 0, 1))
        if n == "trilT":
            s_ = np.arange(128)
            return (s_[None, :] >= s_[:, None]).astype(np.float32)
        raise KeyError(n)


_CACHE = {}


def get_prog(key=((0, 1, 2, 3), 4)):
    if key not in _CACHE:
        _CACHE[key] = K(*key)
    return _CACHE[key]


def pack_inputs(inputs, core, prog):
    f = lambda n: np.asarray(inputs[n], np.float32)
    m = {}
    xp = f("x_prompt")[core]
    xs = f("x_sample")[core * NS:(core + 1) * NS, 0]
    m["xT"] = np.ascontiguousarray(np.concatenate([xp, xs], axis=0).T)
    vec = np.zeros((128, NVEC), np.float32)
    for n, k in VEC_SPECS:
        if n == "gm_ws00":
            v = np.broadcast_to(f("gm_w_s")[0, :, 0, 0][None, :], (128, 8))
        elif n == "gm_bs0":
            v = np.broadcast_to(f("gm_b_s")[0, :, 0][None, :], (128, 8))
        else:
            v = cols(f(n))
        vec[:, VEC_OFF[n]:VEC_OFF[n] + k] = v
    m["vecs"] = vec
    for n in prog.din:
        if n in m:
            continue
        if n in inputs:
            m[n] = f(n)
        else:
            m[n] = prog.host_const(n, inputs, core)
    return {k: m[k] for k in prog.din}


def unpack(rs, prog):
    o = {}
    g = lambda c, n: np.asarray(rs[c][n])
    C = range(NCORES)
    o["y_prompt"] = np.ascontiguousarray(np.stack([g(c, "yT")[:, :SEQ].T for c in C], 0))
    o["y_sample"] = np.ascontiguousarray(np.concatenate([g(c, "yT")[:, SEQ:].T for c in C], 0)[:, None, :])
    d = prog.dout
    if "poolpT" in d:
        o["pool_prompt"] = np.ascontiguousarray(np.stack([g(c, "poolpT").T for c in C], 0)[None])
        o["pool_sample"] = np.ascontiguousarray(np.concatenate([g(c, "poolsT").transpose(1, 2, 0) for c in C], 0)[None])
    if "gvT" in d:
        o["gmlp_v_sample"] = np.ascontiguousarray(np.concatenate([g(c, "gvT").T for c in C], 0)[None, :, None, :])
    if "retp" in d:
        o["ret_prompt"] = np.ascontiguousarray(np.stack([g(c, "retp") for c in C], 0)[None])
        o["ret_sample"] = np.ascontiguousarray(np.concatenate([g(c, "rets") for c in C], 0)[None])
    if "convpT" in d:
        o["conv_prompt"] = np.ascontiguousarray(np.stack([g(c, "convpT").T for c in C], 0)[None])
        o["conv_sample"] = np.ascontiguousarray(np.concatenate([g(c, "convsT").transpose(2, 1, 0) for c in C], 0)[None])
        o["lru_prompt"] = np.ascontiguousarray(np.stack([g(c, "lrupT").T.reshape(-1) for c in C], 0)[None])
        o["lru_sample"] = np.ascontiguousarray(np.concatenate([g(c, "lrusT").T for c in C], 0)[None])
    return o


ORDER = ["y_prompt", "y_sample", "pool_prompt", "pool_sample", "gmlp_v_sample", "ret_prompt", "ret_sample",
         "conv_prompt", "conv_sample", "lru_prompt", "lru_sample"]


def kernel(**inputs):
    prog = get_prog()
    in_maps = [pack_inputs(inputs, c, prog) for c in range(NCORES)]
    res = run_bass_kernel_spmd(prog.nc, in_maps, core_ids=list(range(NCORES)))
    o = unpack(res.results, prog)
    return tuple(o[k] for k in ORDER)
```

```python
import math
import numpy as np
import concourse.bass as bass
import concourse.mybir as mybir
from concourse.bass_utils import run_bass_kernel_spmd

F32 = mybir.dt.float32
BF16 = mybir.dt.bfloat16
AF = mybir.ActivationFunctionType
ALU = mybir.AluOpType
AX = mybir.AxisListType

NCORES = 8
D = 1024
NCH = 8
SEQ = 2048
NS = 16
T = SEQ + NS
TB = [0, 512, 1024, 1536, 2048, 2064]
NTT = 5
DFF = 4096
EPS = 1e-6
GN_EPS = 1e-5
PAST = 16384
import os
NOMLP = bool(os.environ.get('KDEBUG_NOMLP'))


class Op:
    __slots__ = ("eng", "fn", "deps", "signal", "sig_idx", "dsem", "dval")

    def __init__(self, eng, fn, dsem=None):
        self.eng = eng
        self.fn = fn
        self.deps = []
        self.signal = False
        self.sig_idx = 0
        self.dsem = dsem
        self.dval = 0


class Rec:
    ENGS = ("pe", "act", "dve", "pool", "sp")

    def __init__(self, nc):
        self.nc = nc
        self.ops = {e: [] for e in self.ENGS}
        self.last_w = {}
        self.readers = {}
        self.dsem_tot = {}
        self.dsem_last = {}
        self.extra = {e: [] for e in self.ENGS}

    def op(self, eng, fn, r=(), w=(), dsem=None):
        o = Op(eng, fn, dsem)
        deps = {}
        for t in r:
            d = self.last_w.get(t)
            if d is not None:
                deps[id(d)] = d
        for t in w:
            d = self.last_w.get(t)
            if d is not None:
                deps[id(d)] = d
            for d in self.readers.get(t, ()):
                deps[id(d)] = d
        for d in self.extra[eng]:
            deps[id(d)] = d
        self.extra[eng] = []
        o.deps = list(deps.values())
        for d in o.deps:
            if d.dsem is None:
                d.signal = True
        for t in r:
            self.readers.setdefault(t, []).append(o)
        for t in w:
            self.last_w[t] = o
            self.readers[t] = []
        if dsem is not None:
            self.dsem_tot[dsem] = self.dsem_tot.get(dsem, 0) + 16
            o.dval = self.dsem_tot[dsem]
            self.dsem_last[dsem] = o
        self.ops[eng].append(o)
        return o

    def dma(self, q, out, in_, r=(), w=(), dsem=None):
        if q == "pool" and not dsem.startswith("ws"):
            self.extra["pool"] = list(getattr(self, "bar_deps", []))
        return self.op(q, lambda e: e.dma_start(out=out, in_=in_), r=r, w=w, dsem=dsem)

    def barrier(self):
        deps = []
        for e in self.ENGS:
            if e == "pool":
                continue
            for o in reversed(self.ops[e]):
                if o.dsem is None:
                    deps.append(o)
                    break
        deps += [o for k, o in self.dsem_last.items() if not k.startswith("ws")]
        for e in self.ENGS:
            if e != "pool":
                self.extra[e] = list(deps)
        self.bar_deps = list(deps)
        self.last_w = {k: v for k, v in self.last_w.items() if k.startswith("ws")}
        self.readers = {k: v for k, v in self.readers.items() if k.startswith("ws")}

    def emit(self):
        nc = self.nc
        for e in self.ENGS:
            n = 0
            for o in self.ops[e]:
                if o.dsem is None and o.signal:
                    n += 1
                    o.sig_idx = n
        esem = {e: nc.alloc_semaphore("es_" + e) for e in self.ENGS}
        dsems = {k: nc.alloc_semaphore("ds_" + k) for k in self.dsem_tot}
        rec = self

        def run(ename, eng):
            waited = {}
            for o in rec.ops[ename]:
                for d in o.deps:
                    if d.dsem is not None:
                        key, sem, val = "d" + d.dsem, dsems[d.dsem], d.dval
                    else:
                        if ename == "pe" and d.eng == "pe":
                            continue
                        key, sem, val = "e" + d.eng, esem[d.eng], d.sig_idx
                    if waited.get(key, 0) >= val:
                        continue
                    eng.wait_ge(sem, val)
                    waited[key] = val
                ins = o.fn(eng)
                if o.dsem is not None:
                    ins.then_inc(dsems[o.dsem], 16)
                elif o.signal:
                    ins.then_inc(esem[ename], 1)
            if ename == "sp":
                for k, tot in rec.dsem_tot.items():
                    eng.wait_ge(dsems[k], tot)

        with nc.Block() as block:
            @block.tensor
            def _(eng):
                run("pe", eng)

            @block.scalar
            def _(eng):
                run("act", eng)

            @block.vector
            def _(eng):
                run("dve", eng)

            @block.gpsimd
            def _(eng):
                run("pool", eng)

            @block.sync
            def _(eng):
                run("sp", eng)


VEC_SPECS = [
    ("pool_norm", 8), ("pool_scale", 8), ("gm_norm", 8), ("gm_b_in", 16),
    ("ret_norm", 8), ("ret_gn_g", 16), ("ret_gn_b", 16), ("lru_norm", 8),
    ("lru_conv_w", 32), ("lru_conv_b", 8), ("lru_b_a", 8), ("lru_b_x", 8), ("lru_lam", 8),
    ("mlp_norm", 32), ("final_norm", 8), ("gm_ln_g", 8), ("gm_ln_b", 8), ("gm_ws00", 8), ("gm_bs0", 8),
]
VEC_OFF = {}
_o = 0
for _n, _k in VEC_SPECS:
    VEC_OFF[_n] = _o
    _o += _k
NVEC = _o


def cols(v):
    v = np.asarray(v, np.float32).reshape(-1, 128)
    return np.ascontiguousarray(v.T)


def tt_of(c0, c1):
    return [i for i in range(NTT) if TB[i] < c1 and TB[i + 1] > c0]


class K:
    def __init__(self, mixers=(0, 1, 2, 3), nlayers=4):
        self.mixers = mixers
        self.nlayers = nlayers
        nc = self.nc = bass.Bass("TRN2", target_bir_lowering=False)
        self.R = Rec(nc)
        self.din = {}
        self.dout = {}
        self.bank_i = 0
        self.hold = set()
        self.build()

    def inp(self, name, shape, dt=F32):
        t = self.nc.dram_tensor(name, list(shape), dt, kind="ExternalInput").ap()
        self.din[name] = t
        return t

    def outp(self, name, shape, dt=F32):
        t = self.nc.dram_tensor(name, list(shape), dt, kind="ExternalOutput").ap()
        self.dout[name] = t
        return t

    def sb(self, name, shape, dt):
        return self.nc.alloc_sbuf_tensor(name, list(shape), dt)

    def tmp(self, name, shape, dt):
        nbytes = int(np.prod(shape[1:])) * (2 if dt == BF16 else 4)
        nbytes = (nbytes + 31) // 32 * 32
        off = self.t_off
        self.t_off += nbytes
        assert self.t_off <= self.t_end, (name, self.t_off, self.t_end)
        self.tmp_n += 1
        return self.nc.alloc_sbuf_tensor_at(f"{name}_{self.tmp_n}", list(shape), dt, offset=off)

    def tmp_reset(self):
        self.R.barrier()
        self.t_off = self.t_base

    def bank(self):
        while self.bank_i in self.hold:
            self.bank_i = (self.bank_i + 1) % 8
        b = self.bank_i
        self.bank_i = (self.bank_i + 1) % 8
        return b, self.ps[b], f"ps{b}"

    def wload(self, dram_view, pattern=None, **kw):
        i = self.w_i
        self.w_i = (self.w_i + 1) % self.NW
        slot = self.ws[i]
        n = int(np.prod(dram_view.shape[1:]))
        v = slot[:, 0:n]
        if pattern is not None:
            v = v.rearrange(pattern, **kw)
        self.R.dma("pool", v, dram_view, w=[f"ws{i}"], dsem=f"ws{i}")
        return v, f"ws{i}"

    def wload_parts(self, parts):
        i = self.w_i
        self.w_i = (self.w_i + 1) % self.NW
        slot = self.ws[i]
        views = []
        for dram_view, off, pattern, kw in parts:
            n = int(np.prod(dram_view.shape[1:]))
            v = slot[:, off:off + n]
            if pattern is not None:
                v = v.rearrange(pattern, **kw)
            self.R.dma("pool", v, dram_view, w=[f"ws{i}"], dsem=f"ws{i}")
            views.append(v)
        return views, f"ws{i}"

    def xt(self, c, i):
        return f"x{c}_{i}"

    def ht(self, c, i):
        return f"h{c}_{i}"

    def at(self, c, i):
        return f"a{c}_{i}"

    def vcol(self, name, j):
        o = VEC_OFF[name] + j
        return self.vecs[:, o:o + 1]

    def norm_stats(self):
        for i in range(NTT):
            self._norm_stats_tile(i)

    def _norm_stats_tile(self, i):
        R = self.R
        xres, rstd = self.xres, self.rstd
        c0, c1 = TB[i], TB[i + 1]
        n = c1 - c0
        b, ps, pt = self.bank()
        for c in range(NCH):
            sq = self.sq[c % 2]
            sqt = f"sq{c % 2}"
            R.op("act", lambda e, sq=sq, c=c: e.activation(out=sq[:, 0:n], in_=xres[:, c, c0:c1], func=AF.Square),
                 r=[self.xt(c, i)], w=[sqt])
            R.op("pe", lambda e, sq=sq, c=c: e.matmul(ps[:, 0:n], lhsT=self.ones_bf[:], rhs=sq[:, 0:n],
                                                       start=(c == 0), stop=(c == NCH - 1)), r=[sqt], w=[pt])
        R.op("act", lambda e: e.activation(out=rstd[:, c0:c1], in_=ps[:, 0:n], func=AF.Sqrt,
                                           bias=self.eps_col[:], scale=1.0 / D), r=[pt], w=[f"rstd{i}"])
        R.op("dve", lambda e: e.reciprocal(out=rstd[:, c0:c1], in_=rstd[:, c0:c1]), r=[f"rstd{i}"], w=[f"rstd{i}"])

    def norm(self, gname, goff, out_f32_inplace=False):
        self.norm_stats()
        for i in range(NTT):
            self._norm_apply_tile(i, gname, goff, out_f32_inplace)

    def _norm_apply_tile(self, i, gname, goff, out_f32_inplace):
        R = self.R
        xres, H, rstd = self.xres, self.H, self.rstd
        c0, c1 = TB[i], TB[i + 1]
        for c in range(NCH):
            g = self.vcol(gname, goff + c)
            if out_f32_inplace:
                R.op("dve", lambda e, c=c, g=g: e.scalar_tensor_tensor(
                    out=xres[:, c, c0:c1], in0=xres[:, c, c0:c1], scalar=g, in1=rstd[:, c0:c1],
                    op0=ALU.mult, op1=ALU.mult), r=[self.xt(c, i), f"rstd{i}"], w=[self.xt(c, i)])
            else:
                R.op("dve", lambda e, c=c, g=g: e.scalar_tensor_tensor(
                    out=H[:, c, c0:c1], in0=xres[:, c, c0:c1], scalar=g, in1=rstd[:, c0:c1],
                    op0=ALU.mult, op1=ALU.mult), r=[self.xt(c, i), f"rstd{i}"], w=[self.ht(c, i)])

    def xtoks(self, c):
        return [self.xt(c, i) for i in range(NTT)]

    def htoks(self, c):
        return [self.ht(c, i) for i in range(NTT)]

    def rstd_toks(self):
        return [f"rstd{i}" for i in range(NTT)]

    def add_to_xres(self, oc, i, ps, pt, scale_col=None):
        xres = self.xres
        c0, c1 = TB[i], TB[i + 1]
        n = c1 - c0
        if scale_col is None:
            self.R.op("dve", lambda e: e.tensor_tensor(out=xres[:, oc, c0:c1], in0=xres[:, oc, c0:c1],
                                                       in1=ps[:, 0:n], op=ALU.add),
                      r=[pt, self.xt(oc, i)], w=[self.xt(oc, i)])
        else:
            self.R.op("dve", lambda e: e.scalar_tensor_tensor(out=xres[:, oc, c0:c1], in0=ps[:, 0:n], scalar=scale_col,
                                                              in1=xres[:, oc, c0:c1], op0=ALU.mult, op1=ALU.add),
                      r=[pt, self.xt(oc, i)], w=[self.xt(oc, i)])

    def mlp(self, li):
        R = self.R
        xres, H = self.xres, self.H
        A1 = self.tmp("hid", [128, NCH, T], BF16)
        self.r32 = [self.tmp(f"r32_{i}", [128, 512], F32) for i in range(2)]
        self.norm("mlp_norm", li * 8)
        w1 = self.din["mlp_w1"][li].rearrange("(k p) n -> p k n", p=128)
        w2 = self.din["mlp_w2"][li].rearrange("(f p) n -> p f n", p=128)

        def load_w1(p, j):
            return self.wload(w1[:, :, p * 1024 + j * 512: p * 1024 + (j + 1) * 512], "p (k n) -> p k n", k=8)

        def load_w2(p):
            return [self.wload(w2[:, p * 8 + j * 4: p * 8 + (j + 1) * 4, :], "p (f n) -> p f n", f=4)
                    for j in range(2)]

        pre = None
        for p in range(4):
            w1s = [pre if pre is not None else load_w1(p, 0), load_w1(p, 1)]
            w2s = load_w2(p)
            pre = load_w1(p + 1, 0) if p < 3 else None

            def hd(i, w1s=w1s):
                c0, c1 = TB[i], TB[i + 1]
                n = c1 - c0
                for f in range(8):
                    wv, wt = w1s[f // 4]
                    b, ps, pt = self.bank()
                    for k in range(NCH):
                        R.op("pe", lambda e, wv=wv, f=f, k=k, ps=ps: e.matmul(
                            ps[:, 0:n], lhsT=wv[:, k, (f % 4) * 128:(f % 4 + 1) * 128], rhs=H[:, k, c0:c1],
                            start=(k == 0), stop=(k == NCH - 1)), r=[wt, self.ht(k, i)], w=[pt])
                    r32 = self.r32[f % 2]
                    rt = f"r32_{f % 2}"
                    R.op("act", lambda e, ps=ps, r32=r32: e.activation(out=r32[:, 0:n], in_=ps[:, 0:n], func=AF.Relu),
                         r=[pt], w=[rt])
                    R.op("dve", lambda e, r32=r32, f=f: e.tensor_tensor(out=A1[:, f, c0:c1], in0=r32[:, 0:n],
                                                                         in1=r32[:, 0:n], op=ALU.mult),
                         r=[rt], w=[self.at(f, i)])

            def out(i, w2s=w2s):
                c0, c1 = TB[i], TB[i + 1]
                n = c1 - c0
                for oc in range(NCH):
                    b, ps, pt = self.bank()
                    for f in range(8):
                        wv, wt = w2s[f // 4]
                        R.op("pe", lambda e, wv=wv, f=f, oc=oc, ps=ps: e.matmul(
                            ps[:, 0:n], lhsT=wv[:, f % 4, oc * 128:(oc + 1) * 128], rhs=A1[:, f, c0:c1],
                            start=(f == 0), stop=(f == 7)), r=[wt, self.at(f, i)], w=[pt])
                    self.add_to_xres(oc, i, ps, pt)

            hd(0)
            for i in range(1, NTT):
                hd(i)
                out(i - 1)
            out(NTT - 1)

    def pool_mixer(self):
        R = self.R
        xres, H, rstd = self.xres, self.H, self.rstd
        WINS = (2, 4, 8, 16)
        h32 = self.tmp("h32", [128, T], F32)
        pa = self.tmp("pa", [128, SEQ], F32)
        pb = self.tmp("pb", [128, SEQ], F32)
        spT = self.tmp("spT", [128, NCH, NS, 15], F32)
        pso = self.tmp("pso", [128, NCH, NS, 15], F32)
        ppo = self.tmp("ppo", [128, NCH, 15], F32)
        rc = self.tmp("rc", [128, 4, 16], F32)
        ssum = self.tmp("ssum", [128, NS], F32)
        dfix = self.tmp("dfix", [128, 16], F32)
        R.dma("sp", spT[:], self.din["spT"].rearrange("(c p) b r -> p c b r", p=128), w=["spT"], dsem="spT")
        R.dma("sp", rc[:], self.din["rc16"], w=["rc"], dsem="rc")
        wv, wt = self.wload(self.din["pool_w"][0].rearrange("g (j p) e -> p g j e", p=128),
                            "p (g j e) -> p g j e", g=4, j=2)
        self.norm_stats()
        R.op("act", lambda e: e.activation(out=pso[:, :, :, 0:14], in_=spT[:, :, :, 1:15], func=AF.Copy),
             r=["spT"], w=["pso_a"])

        def chunk(c):
            w = WINS[c // 2]
            gi = c // 2
            g = self.vcol("pool_norm", c)
            R.op("dve", lambda e: e.scalar_tensor_tensor(out=h32[:], in0=xres[:, c, :], scalar=g, in1=rstd[:],
                                                         op0=ALU.mult, op1=ALU.mult),
                 r=self.xtoks(c) + self.rstd_toks(), w=["h32"])
            R.op("act", lambda e: e.activation(out=ppo[:, c, :], in_=h32[:, SEQ - 15:SEQ], func=AF.Copy),
                 r=["h32"], w=[f"ppo{c}"])
            R.op("act", lambda e: e.activation(out=pso[:, c, :, 14:15], in_=h32[:, SEQ:T].unsqueeze(2),
                                               func=AF.Copy), r=["h32"], w=[f"pso_b{c}"])
            src, st = h32, "h32"
            bufs = [(pa, "pa"), (pb, "pb")]
            step = 1
            k = 0
            while step < w:
                dst, dt_ = bufs[k % 2]
                R.op("dve", lambda e, src=src, dst=dst, step=step: e.tensor_tensor(
                    out=dst[:, step:SEQ], in0=src[:, step:SEQ], in1=src[:, 0:SEQ - step], op=ALU.add),
                    r=[st, st + "h"], w=[dt_])
                R.op("act", lambda e, src=src, dst=dst, step=step: e.activation(
                    out=dst[:, 0:step], in_=src[:, 0:step], func=AF.Copy), r=[st, st + "h"], w=[dt_ + "h"])
                src, st = dst, dt_
                step *= 2
                k += 1
            rr = [st, st + "h", "h32"]
            R.op("dve", lambda e, src=src: e.scalar_tensor_tensor(out=H[:, c, 0:SEQ], in0=src[:, 0:SEQ], scalar=1.0 / w,
                                                                   in1=h32[:, 0:SEQ], op0=ALU.mult, op1=ALU.subtract),
                 r=rr, w=self.htoks(c)[0:4])
            R.op("dve", lambda e, src=src: e.tensor_tensor(out=dfix[:, 0:w - 1], in0=src[:, 0:w - 1], in1=rc[:, gi, 0:w - 1],
                                                            op=ALU.mult), r=rr + ["rc"], w=["dfix"])
            R.op("dve", lambda e: e.tensor_tensor(out=H[:, c, 0:w - 1], in0=dfix[:, 0:w - 1], in1=h32[:, 0:w - 1],
                                                  op=ALU.subtract), r=["dfix", "h32"], w=[self.ht(c, 0)])
            R.op("dve", lambda e: e.tensor_reduce(out=ssum[:], in_=spT[:, c, :, 15 - (w - 1):15], axis=AX.X, op=ALU.add),
                 r=["spT"], w=["ssum"])
            R.op("dve", lambda e: e.tensor_tensor(out=ssum[:], in0=ssum[:], in1=h32[:, SEQ:T], op=ALU.add),
                 r=["ssum", "h32"], w=["ssum"])
            R.op("dve", lambda e: e.scalar_tensor_tensor(out=H[:, c, SEQ:T], in0=ssum[:], scalar=1.0 / w, in1=h32[:, SEQ:T],
                                                         op0=ALU.mult, op1=ALU.subtract),
                 r=["ssum", "h32"], w=[self.ht(c, 4)])

        for c in range(NCH):
            chunk(c)
        R.dma("sp", self.dout["poolpT"].rearrange("(c p) r -> p c r", p=128), ppo[:],
              r=[f"ppo{c}" for c in range(NCH)], w=["o_poolp"], dsem="o_poolp")
        R.dma("sp", self.dout["poolsT"].rearrange("(c p) b r -> p c b r", p=128), pso[:],
              r=["pso_a"] + [f"pso_b{c}" for c in range(NCH)], w=["o_pools"], dsem="o_pools")

        def proj(i):
            c0, c1 = TB[i], TB[i + 1]
            n = c1 - c0
            for oc in range(NCH):
                gi = oc // 2
                b, ps, pt = self.bank()
                for j in range(2):
                    R.op("pe", lambda e, j=j, ps=ps, oc=oc, gi=gi: e.matmul(
                        ps[:, 0:n], lhsT=wv[:, gi, j, (oc % 2) * 128:(oc % 2 + 1) * 128], rhs=H[:, 2 * gi + j, c0:c1],
                        start=(j == 0), stop=(j == 1)), r=[wt, self.ht(2 * gi + j, i)], w=[pt])
                self.add_to_xres(oc, i, ps, pt, scale_col=self.vcol("pool_scale", oc))

        for i in range(NTT):
            proj(i)

    def gmlp_mixer(self):
        R = self.R
        nc = self.nc
        xres, H = self.xres, self.H
        Cc = self.tmp("Cc", [128, NCH, 128], F32)
        wsm = self.tmp("wsm", [128, NCH, 128], BF16)
        self.binv_row = self.tmp("binv_row", [1, 1024], BF16)
        mark = self.t_off
        bsb = self.tmp("bsb", [128, NCH, 128], F32)
        wsf = self.tmp("wsf", [128, NCH, 128], F32)
        mk = self.tmp("mk", [128, 128], F32)
        self.bs_row = self.tmp("bs_row", [1, 1024], F32)
        R.dma("sp", wsf[:], self.din["gm_wsT"], w=["wsf"], dsem="gmc0")
        R.dma("sp", mk[:], self.din["trilT"], w=["mk"], dsem="gmc1")
        R.dma("sp", self.bs_row[:], self.din["gm_b_s"].rearrange("a g t -> a (g t)"), w=["bs_row"], dsem="gmc2")
        R.dma("pool", self.binv_row[:], self.din["gm_b_in"][0:1, 1024:2048], w=["binv"], dsem="binv")
        R.op("dve", lambda e: e.tensor_tensor(out=wsm[:], in0=wsf[:], in1=mk[:].unsqueeze(1).to_broadcast([128, NCH, 128]),
                                              op=ALU.mult), r=["wsf", "mk"], w=["wsm"])
        for hf in range(2):
            b, ps, pt = self.bank()
            R.op("pe", lambda e, ps=ps, hf=hf: e.matmul(ps[:], lhsT=self.ones_bf[:], rhs=wsm[:, 4 * hf:4 * hf + 4, :],
                                                        start=True, stop=True), r=["wsm"], w=[pt])
            b2, ps2, pt2 = self.bank()
            R.op("pe", lambda e, ps2=ps2, hf=hf: e.matmul(ps2[:], lhsT=self.ones_row_f[0:1, :],
                                                          rhs=self.bs_row[0:1, 512 * hf:512 * hf + 512],
                                                          start=True, stop=True), r=["bs_row"], w=[pt2])
            R.op("act", lambda e, ps2=ps2, hf=hf: e.activation(out=bsb[:, 4 * hf:4 * hf + 4, :], in_=ps2[:], func=AF.Copy),
                 r=[pt2], w=[f"bsb{hf}"])
            for gq in range(4):
                g = 4 * hf + gq
                R.op("dve", lambda e, ps=ps, g=g, gq=gq: e.scalar_tensor_tensor(
                    out=Cc[:, g, :], in0=ps[:, gq * 128:(gq + 1) * 128], scalar=self.vcol("gm_ln_b", g), in1=bsb[:, g, :],
                    op0=ALU.mult, op1=ALU.add), r=[pt, f"bsb{hf}"], w=[f"Cc{g}"])
        R.barrier()
        self.t_off = mark
        A1 = self.tmp("gU", [128, NCH, T], BF16)
        v32 = self.tmp("v32", [128, 1024], F32)
        vh = [self.tmp(f"vh{j}", [128, 1024], BF16) for j in range(2)]
        tmx = [self.tmp("tmx0", [128, 4, 128], F32)] * 2
        st6 = self.tmp("st6", [128, 2, 6], F32)
        mv = self.tmp("mv", [128, 2], F32)
        rsd = self.tmp("rsd", [128, 1], F32)
        vs32 = v32
        vns = self.tmp("vns", [128, NCH, NS], F32)
        tms = self.tmp("tms", [128, NS], F32)
        w_in = self.din["gm_w_in"][0].rearrange("(k p) n -> p k n", p=128)
        w_out = self.din["gm_w_out"][0].rearrange("(k p) n -> p k n", p=128)
        U = [self.wload(w_in[:, :, j * 512:(j + 1) * 512], "p (k n) -> p k n", k=8) for j in range(2)]
        V = [self.wload(w_in[:, :, 1024 + j * 512:1024 + (j + 1) * 512], "p (k n) -> p k n", k=8) for j in range(2)]
        self.norm("gm_norm", 0)

        def uphase(j, i):
            c0, c1 = TB[i], TB[i + 1]
            n = c1 - c0
            wv, wt = U[j]
            for f in range(4):
                fc = 4 * j + f
                b, ps, pt = self.bank()
                for k in range(NCH):
                    R.op("pe", lambda e, f=f, k=k, ps=ps: e.matmul(ps[:, 0:n], lhsT=wv[:, k, f * 128:(f + 1) * 128],
                                                                   rhs=H[:, k, c0:c1], start=(k == 0), stop=(k == NCH - 1)),
                         r=[wt, self.ht(k, i)], w=[pt])
                R.op("act", lambda e, ps=ps, fc=fc: e.activation(out=A1[:, fc, c0:c1], in_=ps[:, 0:n], func=AF.Gelu_apprx_tanh,
                                                                 bias=self.vcol("gm_b_in", fc)), r=[pt], w=[self.at(fc, i)])

        for j in range(2):
            for i in range(NTT):
                uphase(j, i)

        def vtok(c0, m, i):
            banks = []
            for hf in range(2):
                wv, wt = V[hf]
                b, ps, pt = self.bank()
                for k in range(NCH):
                    R.op("pe", lambda e, k=k, ps=ps, wv=wv: e.matmul(ps[0:m, :], lhsT=H[:, k, c0:c0 + m], rhs=wv[:, k, :],
                                                                      start=(k == 0), stop=False),
                         r=[wt, self.ht(k, i)], w=[pt])
                R.op("pe", lambda e, ps=ps, hf=hf: e.matmul(ps[0:m, :], lhsT=self.ones_row_b[0:1, 0:m],
                                                            rhs=self.binv_row[0:1, 512 * hf:512 * hf + 512],
                                                            start=False, stop=True), r=["binv"], w=[pt])
                banks.append((ps, pt))
            return banks

        def lnorm(src, m, srct):
            for hf in range(2):
                R.op("dve", lambda e, hf=hf: e.bn_stats(out=st6[0:m, hf, :], in_=src[0:m, 512 * hf:512 * hf + 512]),
                     r=srct, w=[f"st6{hf}"])
            R.op("dve", lambda e: e.bn_aggr(out=mv[0:m, :], in_=st6[0:m, :, :].rearrange("p a b -> p (a b)")),
                 r=["st60", "st61"], w=["mv"])
            R.op("act", lambda e: e.activation(out=rsd[0:m, :], in_=mv[0:m, 1:2], func=AF.Sqrt, bias=self.eps_col[0:m, :],
                                               scale=1.0), r=["mv"], w=["rsd"])
            R.op("dve", lambda e: e.reciprocal(out=rsd[0:m, :], in_=rsd[0:m, :]), r=["rsd"], w=["rsd"])

        def vchunk(n_):
            c0 = 128 * n_
            i = c0 // 512
            banks = vtok(c0, 128, i)
            for hf, (ps, pt) in enumerate(banks):
                R.op("act", lambda e, ps=ps, hf=hf: e.activation(out=v32[:, 512 * hf:512 * hf + 512], in_=ps[:],
                                                                 func=AF.Gelu_apprx_tanh), r=[pt], w=[f"v32{hf}"])
            lnorm(v32, 128, ["v320", "v321"])
            vhb = vh[n_ % 2]
            vht = f"vh{n_ % 2}"
            R.op("dve", lambda e: e.tensor_scalar(out=vhb[:], in0=v32[:], scalar1=mv[:, 0:1], scalar2=rsd[:, 0:1],
                                                  op0=ALU.subtract, op1=ALU.mult), r=["v320", "v321", "mv", "rsd"], w=[vht])
            for hf in range(2):
                b, ps, pt = self.bank()
                for gq in range(4):
                    g = 4 * hf + gq
                    R.op("pe", lambda e, ps=ps, g=g, gq=gq: e.matmul(ps[:, gq * 128:(gq + 1) * 128],
                                                                      lhsT=vhb[:, g * 128:(g + 1) * 128], rhs=wsm[:, g, :],
                                                                      start=True, stop=True), r=[vht, "wsm"], w=[pt])
                tm = tmx[hf]
                tmt = "tmx0"
                for gq in range(4):
                    g = 4 * hf + gq
                    R.op("dve", lambda e, ps=ps, g=g, gq=gq, tm=tm: e.scalar_tensor_tensor(
                        out=tm[:, gq, :], in0=ps[:, gq * 128:(gq + 1) * 128], scalar=self.vcol("gm_ln_g", g), in1=Cc[:, g, :],
                        op0=ALU.mult, op1=ALU.add), r=[pt, f"Cc{g}"], w=[tmt + f"_{gq}"])
                R.op("dve", lambda e, tm=tm, hf=hf: e.tensor_tensor(out=A1[:, 4 * hf:4 * hf + 4, c0:c0 + 128], in0=tm[:],
                                                                      in1=A1[:, 4 * hf:4 * hf + 4, c0:c0 + 128], op=ALU.mult),
                     r=[tmt + f"_{gq}" for gq in range(4)] + [self.at(4 * hf + gq, i) for gq in range(4)],
                     w=[self.at(4 * hf + gq, i) for gq in range(4)])

        for n_ in range(16):
            vchunk(n_)

        banks = vtok(SEQ, NS, 4)
        for hf, (ps, pt) in enumerate(banks):
            R.op("act", lambda e, ps=ps, hf=hf: e.activation(out=vs32[0:NS, 512 * hf:512 * hf + 512], in_=ps[0:NS, :],
                                                             func=AF.Gelu_apprx_tanh), r=[pt], w=[f"v32{hf}"])
        lnorm(vs32, NS, ["v320", "v321"])
        R.op("dve", lambda e: e.tensor_scalar(out=vs32[0:NS, :], in0=vs32[0:NS, :], scalar1=mv[0:NS, 0:1], scalar2=rsd[0:NS, 0:1],
                                              op0=ALU.subtract, op1=ALU.mult), r=["v320", "v321", "mv", "rsd"], w=["v320", "v321"])
        for g in range(NCH):
            b, ps, pt = self.bank()
            R.op("pe", lambda e, ps=ps, g=g: e.matmul(ps[:, 0:NS], lhsT=vs32[0:NS, g * 128:(g + 1) * 128],
                                                      rhs=self.ident_f[0:NS, 0:NS], start=True, stop=True),
                 r=["v320", "v321", "ident"], w=[pt])
            R.op("dve", lambda e, ps=ps, g=g: e.tensor_scalar(out=vns[:, g, :], in0=ps[:, 0:NS],
                                                              scalar1=self.vcol("gm_ln_g", g), scalar2=self.vcol("gm_ln_b", g),
                                                              op0=ALU.mult, op1=ALU.add), r=[pt], w=[f"vns{g}"])
            R.op("dve", lambda e, g=g: e.tensor_scalar(out=tms[:], in0=vns[:, g, :], scalar1=self.vcol("gm_ws00", g),
                                                       scalar2=self.vcol("gm_bs0", g), op0=ALU.mult, op1=ALU.add),
                 r=[f"vns{g}"], w=["tms"])
            R.op("dve", lambda e, g=g: e.tensor_tensor(out=A1[:, g, SEQ:T], in0=tms[:], in1=A1[:, g, SEQ:T], op=ALU.mult),
                 r=["tms", self.at(g, 4)], w=[self.at(g, 4)])
        R.dma("sp", self.dout["gvT"].rearrange("(c p) b -> p c b", p=128), vns[:],
              r=[f"vns{g}" for g in range(NCH)], w=["o_gv"], dsem="o_gv")

        O = [self.wload(w_out[:, 4 * j:4 * j + 4, :], "p (k n) -> p k n", k=4) for j in range(2)]

        def outp(i):
            c0, c1 = TB[i], TB[i + 1]
            n = c1 - c0
            for oc in range(NCH):
                b, ps, pt = self.bank()
                for k in range(NCH):
                    wv, wt = O[k // 4]
                    R.op("pe", lambda e, k=k, ps=ps, wv=wv, oc=oc: e.matmul(ps[:, 0:n], lhsT=wv[:, k % 4, oc * 128:(oc + 1) * 128],
                                                                      rhs=A1[:, k, c0:c1], start=(k == 0), stop=(k == NCH - 1)),
                         r=[wt, self.at(k, i)], w=[pt])
                self.add_to_xres(oc, i, ps, pt)

        for i in range(NTT):
            outp(i)

    def lru_mixer(self):
        R = self.R
        xres, H = self.xres, self.H
        XO = 3
        xbp = self.tmp("xbp", [128, XO + SEQ], F32)
        xbs = self.tmp("xbs", [128, NS], F32)
        xc32 = self.tmp("xc32", [128, T], F32)
        xcb = self.tmp("xcb", [128, T], BF16)
        gate = self.tmp("gate", [128, T], BF16)
        R1 = self.tmp("R1", [128, T], F32)
        I1 = self.tmp("I1", [128, T], F32)
        T2 = self.tmp("T2", [128, T], F32)
        Gc = xcb
        scT = self.tmp("scT", [128, NCH, 3, NS], F32)
        slT = self.tmp("slT", [128, NCH, NS], F32)
        convp = self.tmp("convp", [128, NCH, 3], F32)
        convs = self.tmp("convs", [128, NCH, 3, NS], F32)
        lrup = self.tmp("lrup", [128, NCH], F32)
        lrus = self.tmp("lrus", [128, NCH, NS], F32)
        nsp8 = self.tmp("nsp8", [128, NCH], F32)
        tsm = self.tmp("tsm", [128, NS], F32)
        R.dma("sp", scT[:], self.din["scT"].rearrange("(c p) j b -> p c j b", p=128), w=["scT"], dsem="scT")
        R.dma("sp", slT[:], self.din["slT"].rearrange("(c p) b -> p c b", p=128), w=["slT"], dsem="slT")
        lam = self.vecs[:, VEC_OFF["lru_lam"]:VEC_OFF["lru_lam"] + 8]
        R.op("act", lambda e: e.activation(out=nsp8[:], in_=lam, func=AF.Exp, scale=-1.0), w=["nsp8"])
        R.op("act", lambda e: e.activation(out=nsp8[:], in_=nsp8[:], func=AF.Ln, bias=self.one_col[:], scale=1.0),
             r=["nsp8"], w=["nsp8"])
        R.op("dve", lambda e: e.tensor_scalar(out=nsp8[:], in0=nsp8[:], scalar1=-8.0, scalar2=None, op0=ALU.mult),
             r=["nsp8"], w=["nsp8"])
        R.op("dve", lambda e: e.memset(xbp[:, 0:XO], 0.0), w=["xbp_h"])
        R.op("act", lambda e: e.activation(out=convs[:, :, 0:2, :], in_=scT[:, :, 1:3, :], func=AF.Copy), r=["scT"], w=["convs_a"])
        self.norm("lru_norm", 0)
        w_in = self.din["lru_w_in"][0].rearrange("(k p) n -> p k n", p=128)
        w_out = self.din["lru_w_out"][0].rearrange("(k p) n -> p k n", p=128)
        wa_d = self.din["lru_w_a"][0].rearrange("h i j -> i h j")
        wx_d = self.din["lru_w_x"][0].rearrange("h i j -> i h j")
        Wg = {}
        Wx_ = {}
        Wg[0] = self.wload(w_in[:, :, 0:512], "p (k n) -> p k n", k=8)
        Wx_[0] = self.wload(w_in[:, :, 1024:1536], "p (k n) -> p k n", k=8)
        (wa, wxx), wat = self.wload_parts([(wa_d, 0, "p (h j) -> p h j", dict(h=8)),
                                          (wx_d, 1024, "p (h j) -> p h j", dict(h=8))])
        Wo = [self.wload(w_out[:, 4 * j:4 * j + 4, :], "p (k n) -> p k n", k=4) for j in range(2)]

        def chunk(c):
            hh = c // 4
            cc = c % 4
            wg, wgt = Wg[hh]
            wxv, wxt = Wx_[hh]
            cw = [self.vcol("lru_conv_w", j * 8 + c) for j in range(4)]
            cb = self.vcol("lru_conv_b", c)

            def projt(i):
                c0, c1 = TB[i], TB[i + 1]
                n = c1 - c0
                b, ps, pt = self.bank()
                for k in range(NCH):
                    R.op("pe", lambda e, k=k, ps=ps: e.matmul(ps[:, 0:n], lhsT=wg[:, k, cc * 128:(cc + 1) * 128], rhs=H[:, k, c0:c1],
                                                              start=(k == 0), stop=(k == NCH - 1)), r=[wgt, self.ht(k, i)], w=[pt])
                R.op("act", lambda e, ps=ps: e.activation(out=gate[:, c0:c1], in_=ps[:, 0:n], func=AF.Gelu_apprx_tanh),
                     r=[pt], w=[f"gate{i}"])
                b, ps, pt = self.bank()
                for k in range(NCH):
                    R.op("pe", lambda e, k=k, ps=ps: e.matmul(ps[:, 0:n], lhsT=wxv[:, k, cc * 128:(cc + 1) * 128], rhs=H[:, k, c0:c1],
                                                              start=(k == 0), stop=(k == NCH - 1)), r=[wxt, self.ht(k, i)], w=[pt])
                if i < 4:
                    R.op("act", lambda e, ps=ps: e.activation(out=xbp[:, XO + c0:XO + c1], in_=ps[:, 0:n], func=AF.Copy),
                         r=[pt], w=[f"xbp{i}"])
                else:
                    R.op("act", lambda e, ps=ps: e.activation(out=xbs[:], in_=ps[:, 0:n], func=AF.Copy), r=[pt], w=["xbs"])

            for i in range(NTT):
                projt(i)
            xbt = [f"xbp{i}" for i in range(4)] + ["xbp_h"]
            R.op("act", lambda e: e.activation(out=convp[:, c, :], in_=xbp[:, XO + SEQ - 3:XO + SEQ], func=AF.Copy),
                 r=["xbp3"], w=[f"convp{c}"])
            R.op("act", lambda e: e.activation(out=convs[:, c, 2, :], in_=xbs[:], func=AF.Copy), r=["xbs"], w=[f"convs_b{c}"])
            R.op("dve", lambda e: e.tensor_scalar(out=xc32[:, 0:SEQ], in0=xbp[:, XO:XO + SEQ], scalar1=cw[3], scalar2=cb,
                                                  op0=ALU.mult, op1=ALU.add), r=xbt, w=["xc_p"])
            for s_ in (1, 2, 3):
                R.op("dve", lambda e, s_=s_: e.scalar_tensor_tensor(out=xc32[:, 0:SEQ], in0=xbp[:, XO - s_:XO - s_ + SEQ],
                                                                     scalar=cw[3 - s_], in1=xc32[:, 0:SEQ], op0=ALU.mult, op1=ALU.add),
                     r=xbt + ["xc_p"], w=["xc_p"])
            R.op("dve", lambda e: e.tensor_scalar(out=xc32[:, SEQ:T], in0=xbs[:], scalar1=cw[3], scalar2=cb,
                                                  op0=ALU.mult, op1=ALU.add), r=["xbs"], w=["xc_s"])
            for j in range(3):
                R.op("dve", lambda e, j=j: e.scalar_tensor_tensor(out=xc32[:, SEQ:T], in0=scT[:, c, j, :], scalar=cw[j],
                                                                   in1=xc32[:, SEQ:T], op0=ALU.mult, op1=ALU.add),
                     r=["scT", "xc_s"], w=["xc_s"])
            R.op("act", lambda e: e.activation(out=xcb[:], in_=xc32[:], func=AF.Copy), r=["xc_p", "xc_s"], w=["xcb"])

            def gates(i):
                c0, c1 = TB[i], TB[i + 1]
                n = c1 - c0
                b, ps, pt = self.bank()
                R.op("pe", lambda e, ps=ps: e.matmul(ps[:, 0:n], lhsT=wa[:, c, :], rhs=xcb[:, c0:c1], start=True, stop=True),
                     r=[wat, "xcb"], w=[pt])
                R.op("act", lambda e, ps=ps: e.activation(out=R1[:, c0:c1], in_=ps[:, 0:n], func=AF.Sigmoid,
                                                          bias=self.vcol("lru_b_a", c)), r=[pt], w=[f"R1_{i}"])
                b, ps, pt = self.bank()
                R.op("pe", lambda e, ps=ps: e.matmul(ps[:, 0:n], lhsT=wxx[:, c, :], rhs=xcb[:, c0:c1], start=True, stop=True),
                     r=[wat, "xcb"], w=[pt])
                R.op("act", lambda e, ps=ps: e.activation(out=I1[:, c0:c1], in_=ps[:, 0:n], func=AF.Sigmoid,
                                                          bias=self.vcol("lru_b_x", c)), r=[pt], w=[f"I1_{i}"])

            for i in range(NTT):
                gates(i)
            r1t = [f"R1_{i}" for i in range(NTT)]
            i1t = [f"I1_{i}" for i in range(NTT)]
            R.op("act", lambda e: e.activation(out=R1[:], in_=R1[:], func=AF.Exp, scale=nsp8[:, c:c + 1]),
                 r=r1t + ["nsp8"], w=["a"])
            R.op("act", lambda e: e.activation(out=T2[:], in_=R1[:], func=AF.Square), r=["a"], w=["T2"])
            R.op("act", lambda e: e.activation(out=T2[:], in_=T2[:], func=AF.Sqrt, bias=self.one_col[:], scale=-1.0),
                 r=["T2"], w=["T2"])
            R.op("dve", lambda e: e.memset(T2[:, 0:1], 1.0), r=["T2"], w=["T2"])
            R.op("dve", lambda e: e.tensor_tensor(out=I1[:], in0=I1[:], in1=xc32[:], op=ALU.mult),
                 r=i1t + ["xc_p", "xc_s"], w=["b1"])
            R.op("dve", lambda e: e.tensor_tensor(out=I1[:], in0=I1[:], in1=T2[:], op=ALU.mult), r=["b1", "T2"], w=["b1"])
            R.op("dve", lambda e: e.tensor_tensor(out=tsm[:], in0=R1[:, SEQ:T], in1=slT[:, c, :], op=ALU.mult),
                 r=["a", "slT"], w=["tsm"])
            R.op("dve", lambda e: e.tensor_tensor(out=I1[:, SEQ:T], in0=I1[:, SEQ:T], in1=tsm[:], op=ALU.add),
                 r=["b1", "tsm"], w=["b1"])
            R.op("dve", lambda e: e.tensor_tensor_scan(out=T2[:, 0:SEQ], data0=R1[:, 0:SEQ], data1=I1[:, 0:SEQ], initial=0.0,
                                                       op0=ALU.mult, op1=ALU.add), r=["a", "b1", "T2"], w=["hs"])
            R.op("act", lambda e: e.activation(out=T2[:, SEQ:T], in_=I1[:, SEQ:T], func=AF.Copy), r=["b1", "hs"], w=["hs_s"])
            R.op("act", lambda e: e.activation(out=lrup[:, c:c + 1], in_=T2[:, SEQ - 1:SEQ], func=AF.Copy), r=["hs"], w=[f"lrup{c}"])
            R.op("act", lambda e: e.activation(out=lrus[:, c, :], in_=T2[:, SEQ:T], func=AF.Copy), r=["hs_s"], w=[f"lrus{c}"])
            R.op("dve", lambda e: e.tensor_tensor(out=Gc[:], in0=T2[:], in1=gate[:], op=ALU.mult),
                 r=["hs", "hs_s", "xcb"] + [f"gate{i}" for i in range(NTT)], w=["Gc"])

            def outp(i):
                c0, c1 = TB[i], TB[i + 1]
                n = c1 - c0
                wo, wot = Wo[c // 4]
                for oc in range(NCH):
                    b, ps, pt = self.bank()
                    R.op("pe", lambda e, ps=ps, oc=oc: e.matmul(ps[:, 0:n], lhsT=wo[:, c % 4, oc * 128:(oc + 1) * 128], rhs=Gc[:, c0:c1],
                                                                start=True, stop=True), r=[wot, "Gc"], w=[pt])
                    self.add_to_xres(oc, i, ps, pt)

            for i in range(NTT):
                outp(i)

        for c in range(NCH):
            if c == 4:
                Wg[1] = self.wload(w_in[:, :, 512:1024], "p (k n) -> p k n", k=8)
                Wx_[1] = self.wload(w_in[:, :, 1536:2048], "p (k n) -> p k n", k=8)
            chunk(c)
        R.dma("sp", self.dout["convpT"].rearrange("(c p) j -> p c j", p=128), convp[:],
              r=[f"convp{c}" for c in range(NCH)], w=["o_convp"], dsem="o_convp")
        R.dma("sp", self.dout["convsT"].rearrange("(c p) j b -> p c j b", p=128), convs[:],
              r=["convs_a"] + [f"convs_b{c}" for c in range(NCH)], w=["o_convs"], dsem="o_convs")
        R.dma("sp", self.dout["lrupT"], lrup[:],
              r=[f"lrup{c}" for c in range(NCH)], w=["o_lrup"], dsem="o_lrup")
        R.dma("sp", self.dout["lrusT"].rearrange("(c p) b -> p c b", p=128), lrus[:],
              r=[f"lrus{c}" for c in range(NCH)], w=["o_lrus"], dsem="o_lrus")

    def ret_mixer(self):
        R = self.R
        xres, H = self.xres, self.H
        DKS = 128 ** -0.5
        lg = np.log1p(-np.exp2(-5.0 - np.arange(8, dtype=np.float32))).astype(np.float32)
        cdec = [float(np.exp(lg[h] * np.float32(128.0))) for h in range(8)]
        gam = [float(np.exp(lg[h] * np.float32(1.0))) for h in range(8)]
        tmp = self.tmp
        G2 = tmp("G2", [128, 4, T], BF16)
        qs = tmp("qs", [128, 512], BF16)
        ks = tmp("ks", [128, 512], BF16)
        tA = tmp("tA", [128, 512], F32)
        tB = tmp("tB", [128, 512], F32)
        qT = [tmp(f"qT{p}", [128, 512], BF16) for p in range(2)]
        qd = [tmp(f"qd{p}", [128, 512], BF16) for p in range(2)]
        kT = [tmp(f"kT{p}", [128, 512], BF16) for p in range(2)]
        kdt = [tmp(f"kdt{p}", [128, 4, 128], BF16) for p in range(2)]
        vtk = [tmp(f"vtk{p}", [128, 4, 256], BF16) for p in range(2)]
        gs = [tmp(f"gs{p}", [128, 2, 512], BF16) for p in range(2)]
        sc = tmp("sc", [128, 4, 128], BF16)
        ob = tmp("ob", [128, 2, 512], BF16)
        osq = tmp("osq", [128, 2, 512], BF16)
        mu = tmp("mu", [128, 512], F32)
        rs = tB
        tn = tA
        S32 = tmp("S32", [128, 256], F32)
        Sbf = [tmp(f"Sbf{j}", [128, 256], BF16) for j in range(5)]
        dmk = tmp("dmk", [128, 128], F32)
        qdc = tmp("qdc", [128, 128], F32)
        kdc = tmp("kdc", [128, 8], F32)
        gne = tmp("gne", [128, 1], F32)
        permM = tmp("permM", [128, 128], BF16)
        identb = tmp("identb", [128, 128], BF16)
        Sst = [tmp(f"Sst{j}", [128, 2, 256], F32) for j in range(2)]
        Km = tmp("Km", [NS, 8, 128], BF16)
        ktk_s = tmp("ktk_s", [NS, 128], BF16)
        vtk_s = tmp("vtk_s", [NS, 256], BF16)
        qs32 = tmp("qs32", [128, NS], F32)
        ks32 = tmp("ks32", [128, NS], F32)
        qds32 = tmp("qds32", [128, NS], F32)
        prodb = tmp("prodb", [128, NS], BF16)
        dots = tmp("dots", [128, NS], F32)
        vTs = tmp("vTs", [128, 2, NS], F32)
        os_ = tmp("os_", [128, 2, NS], F32)
        gss = tmp("gss", [128, 2, NS], BF16)
        crs = tmp("crs", [128, 2, NS], F32)
        R.dma("pool", permM[:], self.din["permM"], w=["permM"], dsem="permM")
        R.dma("pool", identb[:], self.din["ident"], w=["identb"], dsem="identb")
        R.dma("sp", kdc[:], self.din["kdec"], w=["kdc"], dsem="kdc")
        R.op("dve", lambda e: e.memset(gne[:], GN_EPS), w=["gne"])
        self.norm("ret_norm", 0)
        R.barrier()
        ctab = [self.rstd[:, 0:512], self.rstd[:, 1024:1536]]
        stab = [self.rstd[:, 512:1024], self.rstd[:, 1536:2048]]
        w_in = self.din["ret_w_in"][0].rearrange("(k p) n -> p k n", p=128)
        w_out = self.din["ret_w_out"][0].rearrange("(f p) n -> p f n", p=128)
        GRP = [(0, 512), (512, 1024), (1024, 1536), (1536, 2048), (2048, 2064)]
        self._rope_i = 0

        def head(hh, Wo):
            i = self.w_i
            self.w_i = (self.w_i + 1) % self.NW
            slot = self.ws[i]
            WA = slot[:, 0:4096].rearrange("p (k n) -> p k n", k=8)
            wat = f"ws{i}"
            for (lo, hi, src0) in ((0, 128, 128 * hh), (128, 256, 1024 + 128 * hh), (256, 512, 2048 + 256 * hh)):
                R.dma("pool", WA[:, :, lo:hi], w_in[:, :, src0:src0 + (hi - lo)], w=[wat], dsem=wat)
            WB, wbt = self.wload(w_in[:, :, 4096 + 256 * hh:4096 + 256 * hh + 256], "p (k n) -> p k n", k=8)
            R.dma("sp", dmk[:], self.din["dmaskT"][hh], w=["dmk"], dsem="dmk")
            R.dma("sp", qdc[:], self.din["qdecb"][hh], w=["qdc"], dsem="qdc")
            R.op("dve", lambda e: e.memset(S32[:], 0.0), w=["S32"])
            R.op("dve", lambda e: e.memset(Sbf[4][:], 0.0), w=["Sbf4"])
            g2s = (hh % 2) * 2

            def rope(src_bf, srct, psP, pt, n, j, out_ap, outt):
                R.op("dve", lambda e: e.tensor_tensor(out=tA[:, 0:n], in0=src_bf[:, 0:n], in1=ctab[j][:, 0:n], op=ALU.mult),
                     r=[srct, f"ctab{j}"], w=["tA"])
                R.op("dve", lambda e: e.tensor_tensor(out=tB[:, 0:n], in0=psP[:, 0:n], in1=stab[j][:, 0:n], op=ALU.mult),
                     r=[pt, f"stab{j}"], w=["tB"])
                R.op("dve", lambda e: e.tensor_tensor(out=out_ap, in0=tA[:, 0:n], in1=tB[:, 0:n], op=ALU.add),
                     r=["tA", "tB"], w=[outt])

            def qk_proj(col_lo, c0, c1, i, dst_s, dst_t, scale):
                n = c1 - c0
                b, ps, pt = self.bank()
                for k in range(NCH):
                    R.op("pe", lambda e, k=k, ps=ps: e.matmul(ps[:, 0:n], lhsT=WA[:, k, col_lo:col_lo + 128], rhs=H[:, k, c0:c1],
                                                              start=(k == 0), stop=(k == NCH - 1)), r=[wat, self.ht(k, i)], w=[pt])
                R.op("act", lambda e, ps=ps: e.activation(out=dst_s[:, 0:n], in_=ps[:, 0:n], func=AF.Copy, scale=scale),
                     r=[pt], w=[dst_t])

            def perm(dst_s, dst_t, n):
                b, ps2, pt2 = self.bank()
                R.op("pe", lambda e, ps2=ps2: e.matmul(ps2[:, 0:n], lhsT=permM[:], rhs=dst_s[:, 0:n], start=True, stop=True),
                     r=["permM", dst_t], w=[pt2])
                return ps2, pt2

            def gnorm(src, srct, n, cols0, i, p):
                for vc in range(2):
                    R.op("act", lambda e, vc=vc: e.activation(out=osq[:, vc, 0:n], in_=src[vc], func=AF.Square), r=[srct[vc]], w=[f"osq{vc}"])
                    R.op("act", lambda e, vc=vc: e.activation(out=ob[:, vc, 0:n], in_=src[vc], func=AF.Copy), r=[srct[vc]], w=[f"ob{vc}"])
                b, p1, p1t = self.bank()
                for vc in range(2):
                    R.op("pe", lambda e, vc=vc: e.matmul(p1[:, 0:n], lhsT=self.ones_bf[:], rhs=ob[:, vc, 0:n], start=(vc == 0), stop=(vc == 1)),
                         r=[f"ob{vc}"], w=[p1t])
                b, p2, p2t = self.bank()
                for vc in range(2):
                    R.op("pe", lambda e, vc=vc: e.matmul(p2[:, 0:n], lhsT=self.ones_bf[:], rhs=osq[:, vc, 0:n], start=(vc == 0), stop=(vc == 1)),
                         r=[f"osq{vc}"], w=[p2t])
                R.op("act", lambda e: e.activation(out=mu[:, 0:n], in_=p1[:, 0:n], func=AF.Copy, scale=1.0 / 256), r=[p1t], w=["mu"])
                R.op("dve", lambda e: e.tensor_tensor(out=rs[:, 0:n], in0=mu[:, 0:n], in1=mu[:, 0:n], op=ALU.mult), r=["mu"], w=["tB"])
                R.op("dve", lambda e: e.scalar_tensor_tensor(out=rs[:, 0:n], in0=p2[:, 0:n], scalar=1.0 / 256, in1=rs[:, 0:n],
                                                             op0=ALU.mult, op1=ALU.subtract), r=[p2t, "tB"], w=["tB"])
                R.op("act", lambda e: e.activation(out=rs[:, 0:n], in_=rs[:, 0:n], func=AF.Sqrt, bias=gne[:], scale=1.0),
                     r=["tB", "gne"], w=["tB"])
                R.op("dve", lambda e: e.reciprocal(out=rs[:, 0:n], in_=rs[:, 0:n]), r=["tB"], w=["tB"])
                for vc in range(2):
                    R.op("dve", lambda e, vc=vc: e.tensor_tensor(out=tn[:, 0:n], in0=src[vc], in1=mu[:, 0:n], op=ALU.subtract),
                         r=[srct[vc], "mu"], w=["tA"])
                    R.op("dve", lambda e, vc=vc: e.tensor_tensor(out=tn[:, 0:n], in0=tn[:, 0:n], in1=rs[:, 0:n], op=ALU.mult),
                         r=["tA", "tB"], w=["tA"])
                    R.op("dve", lambda e, vc=vc: e.tensor_scalar(out=tn[:, 0:n], in0=tn[:, 0:n],
                                                                 scalar1=self.vcol("ret_gn_g", 2 * hh + vc),
                                                                 scalar2=self.vcol("ret_gn_b", 2 * hh + vc), op0=ALU.mult, op1=ALU.add),
                         r=["tA"], w=["tA"])
                    gsrc = gss[:, vc, :] if p < 0 else gs[p][:, vc, 0:n]
                    gtok = f"gss{vc}" if p < 0 else f"gs{p}_{vc}"
                    R.op("dve", lambda e, vc=vc, gsrc=gsrc: e.tensor_tensor(out=G2[:, g2s + vc, cols0:cols0 + n], in0=tn[:, 0:n],
                                                                            in1=gsrc, op=ALU.mult),
                         r=["tA", gtok], w=[f"g2_{g2s + vc}_{i}"])

            def P1(gi):
                c0, c1 = GRP[gi]
                n = c1 - c0
                i = gi
                p = gi % 2
                samp = (gi == 4)
                j = self._rope_i % 2
                self._rope_i += 1
                R.dma("sp", ctab[j][:, 0:n], self.din["ropeC"][:, c0:c1], w=[f"ctab{j}"], dsem=f"ctab{j}")
                R.dma("sp", stab[j][:, 0:n], self.din["ropeS"][:, c0:c1], w=[f"stab{j}"], dsem=f"stab{j}")
                qk_proj(0, c0, c1, i, qs, "qs", 1.0)
                qk_proj(128, c0, c1, i, ks, "ks", DKS)
                for vc in range(2):
                    b, ps, pt = self.bank()
                    for k in range(NCH):
                        R.op("pe", lambda e, k=k, ps=ps, vc=vc: e.matmul(ps[:, 0:n], lhsT=WB[:, k, vc * 128:(vc + 1) * 128], rhs=H[:, k, c0:c1],
                                                                         start=(k == 0), stop=(k == NCH - 1)), r=[wbt, self.ht(k, i)], w=[pt])
                    if samp:
                        R.op("act", lambda e, ps=ps, vc=vc: e.activation(out=gss[:, vc, :], in_=ps[:, 0:n], func=AF.Silu), r=[pt], w=[f"gss{vc}"])
                    else:
                        R.op("act", lambda e, ps=ps, vc=vc: e.activation(out=gs[p][:, vc, 0:n], in_=ps[:, 0:n], func=AF.Silu), r=[pt], w=[f"gs{p}_{vc}"])
                if not samp:
                    for half in range(2):
                        b, ps, pt = self.bank()
                        for jq in range(2):
                            jj = 2 * half + jq
                            for k in range(NCH):
                                R.op("pe", lambda e, k=k, ps=ps, jj=jj, jq=jq: e.matmul(
                                    ps[:, jq * 256:(jq + 1) * 256], lhsT=H[:, k, c0 + jj * 128:c0 + (jj + 1) * 128], rhs=WA[:, k, 256:512],
                                    start=(k == 0), stop=(k == NCH - 1)), r=[wat, self.ht(k, i)], w=[pt])
                        R.op("act", lambda e, ps=ps, half=half: e.activation(out=vtk[p][:, 2 * half:2 * half + 2, :].rearrange("p j v -> p (j v)"),
                                                                             in_=ps[:], func=AF.Copy), r=[pt], w=[f"vtk{p}_{half}"])
                psP, ptP = perm(qs, "qs", n)
                rope(qs, "qs", psP, ptP, n, j, (qs32[:] if samp else qT[p][:, 0:n]), ("qs32" if samp else f"qT{p}"))
                psP, ptP = perm(ks, "ks", n)
                rope(ks, "ks", psP, ptP, n, j, (ks32[:] if samp else kT[p][:, 0:n]), ("ks32" if samp else f"kT{p}"))
                if not samp:
                    R.op("dve", lambda e: e.tensor_tensor(out=qd[p][:].rearrange("p (j t) -> p j t", j=4),
                                                          in0=qT[p][:].rearrange("p (j t) -> p j t", j=4),
                                                          in1=qdc[:].unsqueeze(1).to_broadcast([128, 4, 128]), op=ALU.mult),
                         r=[f"qT{p}", "qdc"], w=[f"qd{p}"])
                else:
                    R.op("dve", lambda e: e.tensor_scalar(out=qds32[:], in0=qs32[:], scalar1=gam[hh], scalar2=None, op0=ALU.mult),
                         r=["qs32"], w=["qds32"])
                    R.op("dve", lambda e: e.tensor_tensor(out=prodb[:], in0=qs32[:], in1=ks32[:], op=ALU.mult), r=["qs32", "ks32"], w=["prodb"])
                    R.op("act", lambda e: e.activation(out=ks[:, 0:NS], in_=ks32[:], func=AF.Copy), r=["ks32"], w=["ks"])
                    bd, pd, pdt = self.bank()
                    R.op("pe", lambda e: e.matmul(pd[:, 0:NS], lhsT=self.ones_bf[:], rhs=prodb[:], start=True, stop=True), r=["prodb"], w=[pdt])
                    R.op("act", lambda e: e.activation(out=dots[:], in_=pd[:, 0:NS], func=AF.Copy), r=[pdt], w=["dots"])
                    for vc in range(2):
                        b, ps, pt = self.bank()
                        for k in range(NCH):
                            R.op("pe", lambda e, k=k, ps=ps, vc=vc: e.matmul(ps[:, 0:NS], lhsT=WA[:, k, 256 + vc * 128:256 + (vc + 1) * 128],
                                                                             rhs=H[:, k, c0:c1], start=(k == 0), stop=(k == NCH - 1)),
                                 r=[wat, self.ht(k, i)], w=[pt])
                        R.op("act", lambda e, ps=ps, vc=vc: e.activation(out=vTs[:, vc, :], in_=ps[:, 0:NS], func=AF.Copy), r=[pt], w=[f"vTs{vc}"])
                    b, ps, pt = self.bank()
                    for k in range(NCH):
                        R.op("pe", lambda e, k=k, ps=ps: e.matmul(ps[0:NS, 0:256], lhsT=H[:, k, c0:c1], rhs=WA[:, k, 256:512],
                                                                  start=(k == 0), stop=(k == NCH - 1)), r=[wat, self.ht(k, i)], w=[pt])
                    R.op("act", lambda e, ps=ps: e.activation(out=vtk_s[:], in_=ps[0:NS, 0:256], func=AF.Copy), r=[pt], w=["vtk_s"])
                    b, ps, pt = self.bank()
                    R.op("pe", lambda e, ps=ps: e.matmul(ps[0:NS, 0:128], lhsT=ks[:, 0:NS], rhs=identb[:], start=True, stop=True),
                         r=["ks", "identb"], w=[pt])
                    R.op("act", lambda e, ps=ps: e.activation(out=ktk_s[:], in_=ps[0:NS, 0:128], func=AF.Copy), r=[pt], w=["ktk_s"])

            def P1c(gi):
                p = gi % 2
                b, ps, pt = self.bank()
                for jj in range(4):
                    R.op("pe", lambda e, jj=jj, ps=ps: e.matmul(ps[:, jj * 128:(jj + 1) * 128], lhsT=kT[p][:, jj * 128:(jj + 1) * 128], rhs=identb[:],
                                                                start=True, stop=True), r=[f"kT{p}", "identb"], w=[pt])
                R.op("act", lambda e, ps=ps: e.activation(out=kdt[p][:].rearrange("p j d -> p (j d)"), in_=ps[:], func=AF.Copy,
                                                          scale=kdc[:, hh:hh + 1]), r=[pt, "kdc"], w=[f"kdt{p}"])

            def P2(gi):
                c0, c1 = GRP[gi]
                n = c1 - c0
                i = gi
                p = gi % 2
                b, ps, pt = self.bank()
                for jj in range(4):
                    R.op("pe", lambda e, jj=jj, ps=ps: e.matmul(ps[:, jj * 128:(jj + 1) * 128], lhsT=kT[p][:, jj * 128:(jj + 1) * 128],
                                                                rhs=qT[p][:, jj * 128:(jj + 1) * 128], start=True, stop=True),
                         r=[f"kT{p}", f"qT{p}"], w=[pt])
                R.op("dve", lambda e, ps=ps: e.tensor_tensor(out=sc[:], in0=ps[:].rearrange("p (j t) -> p j t", j=4),
                                                             in1=dmk[:].unsqueeze(1).to_broadcast([128, 4, 128]), op=ALU.mult),
                     r=[pt, "dmk"], w=["sc"])
                sidx = [0, 1, 2, 3 + p]
                for jj in range(4):
                    b, psS, psSt = self.bank()
                    R.op("pe", lambda e, jj=jj, psS=psS: e.matmul(psS[:, 0:256], lhsT=kdt[p][:, jj, :], rhs=vtk[p][:, jj, :], start=True, stop=True),
                         r=[f"kdt{p}", f"vtk{p}_{jj // 2}"], w=[psSt])
                    R.op("dve", lambda e, jj=jj, psS=psS: e.scalar_tensor_tensor(out=Sbf[sidx[jj]][:], in0=S32[:], scalar=cdec[hh], in1=psS[:, 0:256],
                                                                                 op0=ALU.mult, op1=ALU.add), r=[psSt, "S32"], w=[f"Sbf{sidx[jj]}"])
                    R.op("dve", lambda e, psS=psS: e.scalar_tensor_tensor(out=S32[:], in0=S32[:], scalar=cdec[hh], in1=psS[:, 0:256],
                                                                          op0=ALU.mult, op1=ALU.add), r=[psSt, "S32"], w=["S32"])
                bo = [self.bank(), self.bank()]
                for jj in range(4):
                    sprev = (jj - 1) if jj > 0 else (3 + (1 - p))
                    for vc in range(2):
                        _, po, pot = bo[vc]
                        R.op("pe", lambda e, jj=jj, vc=vc, po=po: e.matmul(po[:, jj * 128:(jj + 1) * 128], lhsT=vtk[p][:, jj, vc * 128:(vc + 1) * 128],
                                                                           rhs=sc[:, jj, :], start=True, stop=False),
                             r=[f"vtk{p}_{jj // 2}", "sc"], w=[pot])
                        R.op("pe", lambda e, jj=jj, vc=vc, po=po, sprev=sprev: e.matmul(
                            po[:, jj * 128:(jj + 1) * 128], lhsT=Sbf[sprev][:, vc * 128:(vc + 1) * 128],
                            rhs=qd[p][:, jj * 128:(jj + 1) * 128], start=False, stop=True),
                            r=[f"Sbf{sprev}", f"qd{p}"], w=[pot])
                gnorm([bo[0][1][:, 0:n], bo[1][1][:, 0:n]], [bo[0][2], bo[1][2]], n, c0, i, p)

            def sbatch(sbi):
                St = Sst[sbi % 2]
                Stt = f"Sst{sbi % 2}"
                b0 = 2 * sbi
                if sbi % 4 == 0:
                    hf = sbi // 4
                    R.op("dve", lambda e, hf=hf: e.tensor_tensor(
                        out=Km[:], in0=ktk_s[:].unsqueeze(1).to_broadcast([NS, 8, 128]),
                        in1=self.ident_f[0:NS, 8 * hf:8 * hf + 8].unsqueeze(2).to_broadcast([NS, 8, 128]), op=ALU.mult),
                        r=["ktk_s", "ident"], w=["Km"])
                R.dma("sp", St[:], self.din["sret"][b0:b0 + 2, hh].rearrange("b d v -> d b v"), w=[Stt], dsem=Stt)
                b, po, pot = self.bank()
                for vc in range(2):
                    for bi in range(2):
                        bb = b0 + bi
                        R.op("pe", lambda e, bi=bi, bb=bb, vc=vc: e.matmul(
                            po[:, 2 * vc + bi:2 * vc + bi + 1], lhsT=St[:, bi, vc * 128:(vc + 1) * 128], rhs=qds32[:, bb:bb + 1],
                            start=True, stop=True), r=[Stt, "qds32"], w=[pot])
                R.op("act", lambda e: e.activation(out=crs[:, :, b0:b0 + 2], in_=po[:, 0:4].rearrange("p (v b) -> p v b", v=2), func=AF.Copy),
                     r=[pot], w=[f"crs{sbi}"])
                for bi in range(2):
                    bb = b0 + bi
                    b, psS, psSt = self.bank()
                    R.op("pe", lambda e, bb=bb, psS=psS: e.matmul(psS[:, 0:256], lhsT=Km[:, bb % 8, :], rhs=vtk_s[:], start=True, stop=True),
                         r=["Km", "vtk_s"], w=[psSt])
                    R.op("dve", lambda e, bi=bi, psS=psS: e.scalar_tensor_tensor(
                        out=St[:, bi, :], in0=St[:, bi, :], scalar=gam[hh], in1=psS[:, 0:256], op0=ALU.mult, op1=ALU.add),
                        r=[psSt, Stt], w=[Stt])
                R.dma("act", self.dout["rets"][b0:b0 + 2, hh].rearrange("b d v -> d b v"), St[:], r=[Stt], w=[f"o_rets{sbi % 2}"],
                      dsem=f"o_rets{sbi % 2}")

            def sfin():
                for vc in range(2):
                    R.op("dve", lambda e, vc=vc: e.tensor_tensor(out=os_[:, vc, :], in0=vTs[:, vc, :], in1=dots[:], op=ALU.mult),
                         r=[f"vTs{vc}", "dots"], w=[f"os{vc}"])
                    R.op("dve", lambda e, vc=vc: e.tensor_tensor(out=os_[:, vc, :], in0=os_[:, vc, :], in1=crs[:, vc, :], op=ALU.add),
                         r=[f"os{vc}"] + [f"crs{k}" for k in range(8)], w=[f"os{vc}"])
                gnorm([os_[:, 0, :], os_[:, 1, :]], ["os0", "os1"], NS, SEQ, 4, -1)

            P1(4)
            P1(0)
            P1c(0)
            for gi in range(4):
                if gi + 1 < 4:
                    P1(gi + 1)
                P2(gi)
                sbatch(2 * gi)
                sbatch(2 * gi + 1)
                if gi + 1 < 4:
                    P1c(gi + 1)
                if gi == 3:
                    R.dma("sp", self.dout["retp"][hh], S32[:], r=["S32"], w=["o_retp"], dsem="o_retp")
            sfin()
            if hh % 2 == 1:
                wo, wot = Wo
                for i in range(NTT):
                    self._ret_out(i, wo, wot, G2)

        for hh in range(8):
            head(hh, (self.wload(w_out[:, 4 * (hh // 2):4 * (hh // 2) + 4, :], "p (f n) -> p f n", f=4) if hh % 2 == 1 else None))

    def _ret_out(self, i, wo, wot, G2):
        R = self.R
        c0, c1 = TB[i], TB[i + 1]
        n = c1 - c0
        for oc in range(NCH):
            b, ps, pt = self.bank()
            for j in range(4):
                R.op("pe", lambda e, j=j, ps=ps, oc=oc: e.matmul(ps[:, 0:n], lhsT=wo[:, j, oc * 128:(oc + 1) * 128], rhs=G2[:, j, c0:c1],
                                                                 start=(j == 0), stop=(j == 3)), r=[wot, f"g2_{j}_{i}"], w=[pt])
            self.add_to_xres(oc, i, ps, pt)

    def build(self):
        nc, R = self.nc, self.R
        inp, outp = self.inp, self.outp
        mix = [m for m in self.mixers if m < self.nlayers]
        xT = inp("xT", [D, T])
        vecs_d = inp("vecs", [128, NVEC])
        ident_d = inp("ident", [128, 128])
        inp("mlp_w1", [4, D, DFF])
        inp("mlp_w2", [4, DFF, D])
        yT = outp("yT", [D, T])
        if 0 in mix:
            inp("pool_w", [1, 4, 256, 256])
            inp("spT", [D, NS, 15])
            inp("rc16", [128, 4, 16])
            outp("poolpT", [D, 15])
            outp("poolsT", [D, NS, 15])
        if 1 in mix:
            inp("gm_w_in", [1, D, 2 * D])
            inp("gm_w_out", [1, D, D])
            inp("gm_b_in", [1, 2 * D])
            inp("gm_b_s", [1, 8, 128])
            inp("gm_wsT", [128, 8, 128])
            inp("trilT", [128, 128])
            outp("gvT", [D, NS])
        if 2 in mix:
            inp("ret_w_in", [1, D, 6144])
            inp("ret_w_out", [1, 2048, D])
            inp("sret", [NS, 8, 128, 256])
            inp("permM", [128, 128])
            inp("kdec", [128, 8])
            inp("dmaskT", [8, 128, 128])
            inp("qdecb", [8, 128, 128])
            inp("ropeC", [128, T])
            inp("ropeS", [128, T])
            outp("retp", [8, 128, 256])
            outp("rets", [NS, 8, 128, 256])
        if 3 in mix:
            inp("lru_w_in", [1, D, 2 * D])
            inp("lru_w_out", [1, D, D])
            inp("lru_w_a", [1, 8, 128, 128])
            inp("lru_w_x", [1, 8, 128, 128])
            inp("scT", [D, 3, NS])
            inp("slT", [D, NS])
            outp("convpT", [D, 3])
            outp("convsT", [D, 3, NS])
            outp("lrupT", [128, NCH])
            outp("lrusT", [D, NS])
        self.xres = self.sb("xres", [128, NCH, T], F32)
        self.H = self.sb("H", [128, NCH, T], BF16)
        self.NW = 5
        self.ws = [self.sb(f"wslot{i}", [128, 4096], BF16) for i in range(self.NW)]
        self.w_i = 0
        self.vecs = self.sb("vecs_sb", [128, NVEC], F32)
        self.ones_bf = self.sb("ones_bf", [128, 128], BF16)
        self.ident_f = self.sb("ident_f", [128, 128], F32)
        self.eps_col = self.sb("eps_col", [128, 1], F32)
        self.one_col = self.sb("one_col", [128, 1], F32)
        self.rstd = self.sb("rstd", [128, T], F32)
        self.sq = [self.sb(f"sq{i}", [128, 512], BF16) for i in range(2)]
        self.ones_row_b = self.sb("ones_row_b", [1, 128], BF16)
        self.ones_row_f = self.sb("ones_row_f", [1, 128], F32)
        self.ps = [nc.alloc_psum_tensor(f"psb{i}", [128, 512], F32) for i in range(8)]
        self.t_base = (nc.sbuf_base + 63) // 64 * 64
        self.t_end = nc.sbuf_top
        self.t_off = self.t_base
        self.tmp_n = 0

        R.op("dve", lambda e: e.memset(self.ones_bf[:], 1.0), w=["ones"])
        R.op("dve", lambda e: e.memset(self.eps_col[:], EPS), w=["epsc"])
        R.op("dve", lambda e: e.memset(self.one_col[:], 1.0), w=["onec"])
        R.op("dve", lambda e: e.memset(self.ones_row_b[:], 1.0), w=["onesrb"])
        R.op("dve", lambda e: e.memset(self.ones_row_f[:], 1.0), w=["onesrf"])
        R.dma("sp", self.vecs[:], vecs_d, w=["vecs"], dsem="vecs")
        R.dma("sp", self.ident_f[:], ident_d, w=["ident"], dsem="vecs")
        xv = xT.rearrange("(c p) t -> p c t", p=128)
        for c in range(NCH):
            R.dma("sp", self.xres[:, c, :], xv[:, c, :], w=self.xtoks(c), dsem="xin")

        fns = {0: self.pool_mixer, 1: self.gmlp_mixer, 2: self.ret_mixer, 3: self.lru_mixer}
        for li in range(self.nlayers):
            m = li % 4
            if m in mix:
                self.tmp_reset()
                fns[m]()
            if not NOMLP:
                self.tmp_reset()
                self.mlp(li)

        self.tmp_reset()
        self.norm("final_norm", 0, out_f32_inplace=True)
        yv = yT.rearrange("(c p) t -> p c t", p=128)
        for c in range(NCH):
            R.dma("sp", yv[:, c, :], self.xres[:, c, :], r=self.xtoks(c), w=[f"yout{c}"], dsem="yout")
        R.emit()

    def host_const(self, n, inputs, core):
        f = lambda k: np.asarray(inputs[k], np.float32)
        sl = slice(core * NS, (core + 1) * NS)
        if n == "ident":
            return np.eye(128, dtype=np.float32)
        if n == "spT":
            return np.ascontiguousarray(f("state_pool")[0, sl].transpose(2, 0, 1))
        if n == "rc16":
            t = np.arange(16, dtype=np.float32)
            rc = np.stack([1.0 / np.minimum(t + 1, w) for w in (2, 4, 8, 16)], 0).astype(np.float32)
            return np.ascontiguousarray(np.broadcast_to(rc[None], (128, 4, 16)))
        if n == "sret":
            return np.ascontiguousarray(f("state_ret")[0, sl])
        if n == "permM":
            k = np.arange(128)
            return (k[:, None] == ((k[None, :] + 64) % 128)).astype(np.float32)
        if n in ("kdec", "dmaskT", "qdecb"):
            lg = np.log1p(-np.exp2(-5.0 - np.arange(8, dtype=np.float32))).astype(np.float32)
            idx = np.arange(128, dtype=np.float32)
            if n == "kdec":
                return np.ascontiguousarray(np.exp(lg[None, :] * (127.0 - idx)[:, None]).astype(np.float32))
            if n == "qdecb":
                qd_ = np.exp(lg[:, None] * (idx + 1.0)[None, :]).astype(np.float32)
                return np.ascontiguousarray(np.broadcast_to(qd_[:, None, :], (8, 128, 128)))
            diff = idx[None, :] - idx[:, None]
            dm = np.where(diff[None] >= 0, np.exp(lg[:, None, None] * np.maximum(diff, 0.0)[None]), 0.0)
            return np.ascontiguousarray(dm.astype(np.float32))
        if n in ("ropeC", "ropeS"):
            half = 64
            freqs = np.exp(np.float32(-math.log(10000.0)) * np.arange(half, dtype=np.float32) / np.float32(half)).astype(np.float32)
            pos = np.concatenate([np.arange(SEQ), np.full(NS, PAST)]).astype(np.float32)
            ang = (pos[None, :] * freqs[:, None]).astype(np.float32)
            if n == "ropeC":
                c_ = np.cos(ang).astype(np.float32)
                return np.ascontiguousarray(np.concatenate([c_, c_], 0))
            s_ = np.sin(ang).astype(np.float32)
            return np.ascontiguousarray(np.concatenate([-s_, s_], 0))
        if n == "scT":
            return np.ascontiguousarray(f("state_conv")[0, sl].transpose(2, 1, 0))
        if n == "slT":
            return np.ascontiguousarray(f("state_lru")[0, sl].T)
        if n == "gm_wsT":
            return np.ascontiguousarray(f("gm_w_s")[0].transpose(2, 0, 1))
        if n == "trilT":
            s_ = np.arange(128)
            return (s_[None, :] >= s_[:, None]).astype(np.float32)
        raise KeyError(n)


_CACHE = {}


def get_prog(key=((0, 1, 2, 3), 4)):
    if key not in _CACHE:
        _CACHE[key] = K(*key)
    return _CACHE[key]


def pack_inputs(inputs, core, prog):
    f = lambda n: np.asarray(inputs[n], np.float32)
    m = {}
    xp = f("x_prompt")[core]
    xs = f("x_sample")[core * NS:(core + 1) * NS, 0]
    m["xT"] = np.ascontiguousarray(np.concatenate([xp, xs], axis=0).T)
    vec = np.zeros((128, NVEC), np.float32)
    for n, k in VEC_SPECS:
        if n == "gm_ws00":
            v = np.broadcast_to(f("gm_w_s")[0, :, 0, 0][None, :], (128, 8))
        elif n == "gm_bs0":
            v = np.broadcast_to(f("gm_b_s")[0, :, 0][None, :], (128, 8))
        else:
            v = cols(f(n))
        vec[:, VEC_OFF[n]:VEC_OFF[n] + k] = v
    m["vecs"] = vec
    for n in prog.din:
        if n in m:
            continue
        if n in inputs:
            m[n] = f(n)
        else:
            m[n] = prog.host_const(n, inputs, core)
    return {k: m[k] for k in prog.din}


def unpack(rs, prog):
    o = {}
    g = lambda c, n: np.asarray(rs[c][n])
    C = range(NCORES)
    o["y_prompt"] = np.ascontiguousarray(np.stack([g(c, "yT")[:, :SEQ].T for c in C], 0))
    o["y_sample"] = np.ascontiguousarray(np.concatenate([g(c, "yT")[:, SEQ:].T for c in C], 0)[:, None, :])
    d = prog.dout
    if "poolpT" in d:
        o["pool_prompt"] = np.ascontiguousarray(np.stack([g(c, "poolpT").T for c in C], 0)[None])
        o["pool_sample"] = np.ascontiguousarray(np.concatenate([g(c, "poolsT").transpose(1, 2, 0) for c in C], 0)[None])
    if "gvT" in d:
        o["gmlp_v_sample"] = np.ascontiguousarray(np.concatenate([g(c, "gvT").T for c in C], 0)[None, :, None, :])
    if "retp" in d:
        o["ret_prompt"] = np.ascontiguousarray(np.stack([g(c, "retp") for c in C], 0)[None])
        o["ret_sample"] = np.ascontiguousarray(np.concatenate([g(c, "rets") for c in C], 0)[None])
    if "convpT" in d:
        o["conv_prompt"] = np.ascontiguousarray(np.stack([g(c, "convpT").T for c in C], 0)[None])
        o["conv_sample"] = np.ascontiguousarray(np.concatenate([g(c, "convsT").transpose(2, 1, 0) for c in C], 0)[None])
        o["lru_prompt"] = np.ascontiguousarray(np.stack([g(c, "lrupT").T.reshape(-1) for c in C], 0)[None])
        o["lru_sample"] = np.ascontiguousarray(np.concatenate([g(c, "lrusT").T for c in C], 0)[None])
    return o


ORDER = ["y_prompt", "y_sample", "pool_prompt", "pool_sample", "gmlp_v_sample", "ret_prompt", "ret_sample",
         "conv_prompt", "conv_sample", "lru_prompt", "lru_sample"]


def kernel(**inputs):
    prog = get_prog()
    in_maps = [pack_inputs(inputs, c, prog) for c in range(NCORES)]
    res = run_bass_kernel_spmd(prog.nc, in_maps, core_ids=list(range(NCORES)))
    o = unpack(res.results, prog)
    return tuple(o[k] for k in ORDER)
```

```python
import math
import numpy as np
import concourse.bass as bass
import concourse.mybir as mybir
from concourse.bass_utils import run_bass_kernel_spmd

F32 = mybir.dt.float32
BF16 = mybir.dt.bfloat16
AF = mybir.ActivationFunctionType
ALU = mybir.AluOpType
AX = mybir.AxisListType

NCORES = 8
D = 1024
NCH = 8
SEQ = 2048
NS = 16
T = SEQ + NS
TB = [0, 512, 1024, 1536, 2048, 2064]
NTT = 5
MB = [0, 413, 826, 1239, 1652, 2064]
DFF = 4096
EPS = 1e-6
GN_EPS = 1e-5
PAST = 16384
import os
NOMLP = bool(os.environ.get('KDEBUG_NOMLP'))


class Op:
    __slots__ = ("eng", "fn", "deps", "signal", "sig_idx", "dsem", "dval")

    def __init__(self, eng, fn, dsem=None):
        self.eng = eng
        self.fn = fn
        self.deps = []
        self.signal = False
        self.sig_idx = 0
        self.dsem = dsem
        self.dval = 0


class Rec:
    ENGS = ("pe", "act", "dve", "pool", "sp")

    def __init__(self, nc):
        self.nc = nc
        self.ops = {e: [] for e in self.ENGS}
        self.last_w = {}
        self.readers = {}
        self.dsem_tot = {}
        self.dsem_last = {}
        self.extra = {e: [] for e in self.ENGS}

    def op(self, eng, fn, r=(), w=(), dsem=None):
        o = Op(eng, fn, dsem)
        if eng == "pool" and dsem is None:
            self.extra["pool"] = list(getattr(self, "bar_deps", []))
        deps = {}
        for t in r:
            d = self.last_w.get(t)
            if d is not None:
                deps[id(d)] = d
        for t in w:
            d = self.last_w.get(t)
            if d is not None:
                deps[id(d)] = d
            for d in self.readers.get(t, ()):
                deps[id(d)] = d
        for d in self.extra[eng]:
            deps[id(d)] = d
        self.extra[eng] = []
        o.deps = list(deps.values())
        for d in o.deps:
            if d.dsem is None:
                d.signal = True
        for t in r:
            self.readers.setdefault(t, []).append(o)
        for t in w:
            self.last_w[t] = o
            self.readers[t] = []
        if dsem is not None:
            self.dsem_tot[dsem] = self.dsem_tot.get(dsem, 0) + 16
            o.dval = self.dsem_tot[dsem]
            self.dsem_last[dsem] = o
        self.ops[eng].append(o)
        return o

    def dma(self, q, out, in_, r=(), w=(), dsem=None):
        if q == "pool" and not dsem.startswith("ws"):
            self.extra["pool"] = list(getattr(self, "bar_deps", []))
        return self.op(q, lambda e: e.dma_start(out=out, in_=in_), r=r, w=w, dsem=dsem)

    def barrier(self):
        deps = []
        for e in self.ENGS:
            for o in reversed(self.ops[e]):
                if o.dsem is None:
                    deps.append(o)
                    break
        deps += [o for k, o in self.dsem_last.items() if not k.startswith("ws")]
        for e in self.ENGS:
            if e != "pool":
                self.extra[e] = list(deps)
        self.bar_deps = list(deps)
        self.last_w = {k: v for k, v in self.last_w.items() if k.startswith("ws")}
        self.readers = {k: v for k, v in self.readers.items() if k.startswith("ws")}

    def emit(self):
        nc = self.nc
        for e in self.ENGS:
            n = 0
            for o in self.ops[e]:
                if o.dsem is None and o.signal:
                    n += 1
                    o.sig_idx = n
        esem = {e: nc.alloc_semaphore("es_" + e) for e in self.ENGS}
        dsems = {k: nc.alloc_semaphore("ds_" + k) for k in self.dsem_tot}
        rec = self

        def run(ename, eng):
            waited = {}
            for o in rec.ops[ename]:
                for d in o.deps:
                    if d.dsem is not None:
                        key, sem, val = "d" + d.dsem, dsems[d.dsem], d.dval
                    else:
                        if ename == "pe" and d.eng == "pe":
                            continue
                        key, sem, val = "e" + d.eng, esem[d.eng], d.sig_idx
                    if waited.get(key, 0) >= val:
                        continue
                    eng.wait_ge(sem, val)
                    waited[key] = val
                ins = o.fn(eng)
                if o.dsem is not None:
                    ins.then_inc(dsems[o.dsem], 16)
                elif o.signal:
                    ins.then_inc(esem[ename], 1)
            if ename == "sp":
                for k, tot in rec.dsem_tot.items():
                    eng.wait_ge(dsems[k], tot)

        with nc.Block() as block:
            @block.tensor
            def _(eng):
                run("pe", eng)

            @block.scalar
            def _(eng):
                run("act", eng)

            @block.vector
            def _(eng):
                run("dve", eng)

            @block.gpsimd
            def _(eng):
                run("pool", eng)

            @block.sync
            def _(eng):
                run("sp", eng)


VEC_SPECS = [
    ("pool_norm", 8), ("pool_scale", 8), ("gm_norm", 8), ("gm_b_in", 16),
    ("ret_norm", 8), ("ret_gn_g", 16), ("ret_gn_b", 16), ("lru_norm", 8),
    ("lru_conv_w", 32), ("lru_conv_b", 8), ("lru_b_a", 8), ("lru_b_x", 8), ("lru_lam", 8),
    ("mlp_norm", 32), ("final_norm", 8), ("gm_ln_g", 8), ("gm_ln_b", 8), ("gm_ws00", 8), ("gm_bs0", 8),
]
VEC_OFF = {}
_o = 0
for _n, _k in VEC_SPECS:
    VEC_OFF[_n] = _o
    _o += _k
NVEC = _o


def cols(v):
    v = np.asarray(v, np.float32).reshape(-1, 128)
    return np.ascontiguousarray(v.T)


def tt_of(c0, c1):
    return [i for i in range(NTT) if TB[i] < c1 and TB[i + 1] > c0]


class K:
    def __init__(self, mixers=(0, 1, 2, 3), nlayers=4):
        self.mixers = mixers
        self.nlayers = nlayers
        nc = self.nc = bass.Bass("TRN2", target_bir_lowering=False)
        self.R = Rec(nc)
        self.din = {}
        self.dout = {}
        self.bank_i = 0
        self.hold = set()
        self.TBcur = TB
        self.build()

    def inp(self, name, shape, dt=F32):
        t = self.nc.dram_tensor(name, list(shape), dt, kind="ExternalInput").ap()
        self.din[name] = t
        return t

    def outp(self, name, shape, dt=F32):
        t = self.nc.dram_tensor(name, list(shape), dt, kind="ExternalOutput").ap()
        self.dout[name] = t
        return t

    def sb(self, name, shape, dt):
        return self.nc.alloc_sbuf_tensor(name, list(shape), dt)

    def tmp(self, name, shape, dt):
        nbytes = int(np.prod(shape[1:])) * (2 if dt == BF16 else 4)
        nbytes = (nbytes + 31) // 32 * 32
        off = self.t_off
        self.t_off += nbytes
        assert self.t_off <= self.t_end, (name, self.t_off, self.t_end)
        self.tmp_n += 1
        return self.nc.alloc_sbuf_tensor_at(f"{name}_{self.tmp_n}", list(shape), dt, offset=off)

    def tmp_reset(self):
        self.R.barrier()
        self.t_off = self.t_base

    def bank(self):
        while self.bank_i in self.hold:
            self.bank_i = (self.bank_i + 1) % 8
        b = self.bank_i
        self.bank_i = (self.bank_i + 1) % 8
        return b, self.ps[b], f"ps{b}"

    def wload(self, dram_view, pattern=None, **kw):
        i = self.w_i
        self.w_i = (self.w_i + 1) % self.NW
        slot = self.ws[i]
        n = int(np.prod(dram_view.shape[1:]))
        v = slot[:, 0:n]
        if pattern is not None:
            v = v.rearrange(pattern, **kw)
        self.R.dma("pool", v, dram_view, w=[f"ws{i}"], dsem=f"ws{i}")
        return v, f"ws{i}"

    def wload_parts(self, parts):
        i = self.w_i
        self.w_i = (self.w_i + 1) % self.NW
        slot = self.ws[i]
        views = []
        for dram_view, off, pattern, kw in parts:
            n = int(np.prod(dram_view.shape[1:]))
            v = slot[:, off:off + n]
            if pattern is not None:
                v = v.rearrange(pattern, **kw)
            self.R.dma("pool", v, dram_view, w=[f"ws{i}"], dsem=f"ws{i}")
            views.append(v)
        return views, f"ws{i}"

    def xt(self, c, i):
        return f"x{c}_{i}"

    def ht(self, c, i):
        return f"h{c}_{i}"

    def at(self, c, i):
        return f"a{c}_{i}"

    def vcol(self, name, j):
        o = VEC_OFF[name] + j
        return self.vecs[:, o:o + 1]

    def norm_stats(self):
        for i in range(NTT):
            self._norm_stats_tile(i)

    def _norm_stats_tile(self, i):
        R = self.R
        xres, rstd = self.xres, self.rstd
        c0, c1 = self.TBcur[i], self.TBcur[i + 1]
        n = c1 - c0
        b, ps, pt = self.bank()
        for c in range(NCH):
            sq = self.sq[c % 2]
            sqt = f"sq{c % 2}"
            R.op("act", lambda e, sq=sq, c=c: e.activation(out=sq[:, 0:n], in_=xres[:, c, c0:c1], func=AF.Square),
                 r=[self.xt(c, i)], w=[sqt])
            R.op("pe", lambda e, sq=sq, c=c: e.matmul(ps[:, 0:n], lhsT=self.ones_bf[:], rhs=sq[:, 0:n],
                                                       start=(c == 0), stop=(c == NCH - 1)), r=[sqt], w=[pt])
        R.op("act", lambda e: e.activation(out=rstd[:, c0:c1], in_=ps[:, 0:n], func=AF.Sqrt,
                                           bias=self.eps_col[:], scale=1.0 / D), r=[pt], w=[f"rstd{i}"])
        R.op("dve", lambda e: e.reciprocal(out=rstd[:, c0:c1], in_=rstd[:, c0:c1]), r=[f"rstd{i}"], w=[f"rstd{i}"])

    def norm(self, gname, goff, out_f32_inplace=False):
        self._norm_stats_tile(0)
        for i in range(NTT):
            if i + 1 < NTT:
                self._norm_stats_tile(i + 1)
            self._norm_apply_tile(i, gname, goff, out_f32_inplace)

    def _norm_apply_tile(self, i, gname, goff, out_f32_inplace):
        R = self.R
        xres, H, rstd = self.xres, self.H, self.rstd
        c0, c1 = self.TBcur[i], self.TBcur[i + 1]
        for c in range(NCH):
            g = self.vcol(gname, goff + c)
            if out_f32_inplace:
                R.op("dve", lambda e, c=c, g=g: e.scalar_tensor_tensor(
                    out=xres[:, c, c0:c1], in0=xres[:, c, c0:c1], scalar=g, in1=rstd[:, c0:c1],
                    op0=ALU.mult, op1=ALU.mult), r=[self.xt(c, i), f"rstd{i}"], w=[self.xt(c, i)])
            else:
                R.op("dve", lambda e, c=c, g=g: e.scalar_tensor_tensor(
                    out=H[:, c, c0:c1], in0=xres[:, c, c0:c1], scalar=g, in1=rstd[:, c0:c1],
                    op0=ALU.mult, op1=ALU.mult), r=[self.xt(c, i), f"rstd{i}"], w=[self.ht(c, i)])

    def xtoks(self, c):
        return [self.xt(c, i) for i in range(NTT)]

    def htoks(self, c):
        return [self.ht(c, i) for i in range(NTT)]

    def rstd_toks(self):
        return [f"rstd{i}" for i in range(NTT)]

    def add_to_xres(self, oc, i, ps, pt, scale_col=None):
        xres = self.xres
        c0, c1 = self.TBcur[i], self.TBcur[i + 1]
        n = c1 - c0
        if scale_col is None:
            self.R.op("dve", lambda e: e.tensor_tensor(out=xres[:, oc, c0:c1], in0=xres[:, oc, c0:c1],
                                                       in1=ps[:, 0:n], op=ALU.add),
                      r=[pt, self.xt(oc, i)], w=[self.xt(oc, i)])
        else:
            self.R.op("dve", lambda e: e.scalar_tensor_tensor(out=xres[:, oc, c0:c1], in0=ps[:, 0:n], scalar=scale_col,
                                                              in1=xres[:, oc, c0:c1], op0=ALU.mult, op1=ALU.add),
                      r=[pt, self.xt(oc, i)], w=[self.xt(oc, i)])

    def mlp(self, li):
        R = self.R
        xres, H = self.xres, self.H
        A1 = self.tmp("hid", [128, NCH, T], BF16)
        self.r32 = [self.tmp(f"r32_{i}", [128, 512], F32) for i in range(2)]
        self.TBcur = MB
        self.norm("mlp_norm", li * 8)
        w1 = self.din["mlp_w1"][li].rearrange("(k p) n -> p k n", p=128)
        w2 = self.din["mlp_w2"][li].rearrange("(f p) n -> p f n", p=128)

        def load_w1(p, j):
            return self.wload(w1[:, :, p * 1024 + j * 512: p * 1024 + (j + 1) * 512], "p (k n) -> p k n", k=8)

        def load_w2(p):
            return [self.wload(w2[:, p * 8 + j * 4: p * 8 + (j + 1) * 4, :], "p (f n) -> p f n", f=4)
                    for j in range(2)]

        pre = None
        for p in range(4):
            w1s = [pre if pre is not None else load_w1(p, 0), load_w1(p, 1)]
            w2s = load_w2(p)
            pre = load_w1(p + 1, 0) if p < 3 else None

            def hd(i, w1s=w1s):
                c0, c1 = self.TBcur[i], self.TBcur[i + 1]
                n = c1 - c0
                for f in range(8):
                    wv, wt = w1s[f // 4]
                    b, ps, pt = self.bank()
                    for k in range(NCH):
                        R.op("pe", lambda e, wv=wv, f=f, k=k, ps=ps: e.matmul(
                            ps[:, 0:n], lhsT=wv[:, k, (f % 4) * 128:(f % 4 + 1) * 128], rhs=H[:, k, c0:c1],
                            start=(k == 0), stop=(k == NCH - 1)), r=[wt, self.ht(k, i)], w=[pt])
                    r32 = self.r32[f % 2]
                    rt = f"r32_{f % 2}"
                    R.op("act", lambda e, ps=ps, r32=r32: e.activation(out=r32[:, 0:n], in_=ps[:, 0:n], func=AF.Relu),
                         r=[pt], w=[rt])
                    R.op("dve", lambda e, r32=r32, f=f: e.tensor_tensor(out=A1[:, f, c0:c1], in0=r32[:, 0:n],
                                                                         in1=r32[:, 0:n], op=ALU.mult),
                         r=[rt], w=[self.at(f, i)])

            def out(i, w2s=w2s):
                c0, c1 = self.TBcur[i], self.TBcur[i + 1]
                n = c1 - c0
                for oc in range(NCH):
                    b, ps, pt = self.bank()
                    for f in range(8):
                        wv, wt = w2s[f // 4]
                        R.op("pe", lambda e, wv=wv, f=f, oc=oc, ps=ps: e.matmul(
                            ps[:, 0:n], lhsT=wv[:, f % 4, oc * 128:(oc + 1) * 128], rhs=A1[:, f, c0:c1],
                            start=(f == 0), stop=(f == 7)), r=[wt, self.at(f, i)], w=[pt])
                    self.add_to_xres(oc, i, ps, pt)

            hd(0)
            for i in range(1, NTT):
                hd(i)
                out(i - 1)
            out(NTT - 1)
        self.TBcur = TB

    def pool_mixer(self):
        R = self.R
        xres, H, rstd = self.xres, self.H, self.rstd
        WINS = (2, 4, 8, 16)
        h32 = self.tmp("h32", [128, T], F32)
        pa = self.tmp("pa", [128, SEQ], F32)
        pb = self.tmp("pb", [128, SEQ], F32)
        spT = self.tmp("spT", [128, NCH, NS, 15], F32)
        pso = self.tmp("pso", [128, NCH, NS, 15], F32)
        ppo = self.tmp("ppo", [128, NCH, 15], F32)
        rc = self.tmp("rc", [128, 4, 16], F32)
        ssum = self.tmp("ssum", [128, NS], F32)
        dfix = self.tmp("dfix", [128, 16], F32)
        R.dma("sp", spT[:], self.din["spT"].rearrange("(c p) b r -> p c b r", p=128), w=["spT"], dsem="spT")
        R.dma("sp", rc[:], self.din["rc16"], w=["rc"], dsem="rc")
        wv, wt = self.wload(self.din["pool_w"][0].rearrange("g (j p) e -> p g j e", p=128),
                            "p (g j e) -> p g j e", g=4, j=2)
        self.norm_stats()
        R.op("act", lambda e: e.activation(out=pso[:, :, :, 0:14], in_=spT[:, :, :, 1:15], func=AF.Copy),
             r=["spT"], w=["pso_a"])

        def chunk(c):
            w = WINS[c // 2]
            gi = c // 2
            g = self.vcol("pool_norm", c)
            R.op("dve", lambda e: e.scalar_tensor_tensor(out=h32[:], in0=xres[:, c, :], scalar=g, in1=rstd[:],
                                                         op0=ALU.mult, op1=ALU.mult),
                 r=self.xtoks(c) + self.rstd_toks(), w=["h32"])
            R.op("act", lambda e: e.activation(out=ppo[:, c, :], in_=h32[:, SEQ - 15:SEQ], func=AF.Copy),
                 r=["h32"], w=[f"ppo{c}"])
            R.op("act", lambda e: e.activation(out=pso[:, c, :, 14:15], in_=h32[:, SEQ:T].unsqueeze(2),
                                               func=AF.Copy), r=["h32"], w=[f"pso_b{c}"])
            src, st = h32, "h32"
            bufs = [(pa, "pa"), (pb, "pb")]
            step = 1
            k = 0
            while step < w:
                dst, dt_ = bufs[k % 2]
                R.op("dve", lambda e, src=src, dst=dst, step=step: e.tensor_tensor(
                    out=dst[:, step:SEQ], in0=src[:, step:SEQ], in1=src[:, 0:SEQ - step], op=ALU.add),
                    r=[st, st + "h"], w=[dt_])
                R.op("act", lambda e, src=src, dst=dst, step=step: e.activation(
                    out=dst[:, 0:step], in_=src[:, 0:step], func=AF.Copy), r=[st, st + "h"], w=[dt_ + "h"])
                src, st = dst, dt_
                step *= 2
                k += 1
            rr = [st, st + "h", "h32"]
            R.op("dve", lambda e, src=src: e.scalar_tensor_tensor(out=H[:, c, 0:SEQ], in0=src[:, 0:SEQ], scalar=1.0 / w,
                                                                   in1=h32[:, 0:SEQ], op0=ALU.mult, op1=ALU.subtract),
                 r=rr, w=self.htoks(c)[0:4])
            R.op("dve", lambda e, src=src: e.tensor_tensor(out=dfix[:, 0:w - 1], in0=src[:, 0:w - 1], in1=rc[:, gi, 0:w - 1],
                                                            op=ALU.mult), r=rr + ["rc"], w=["dfix"])
            R.op("dve", lambda e: e.tensor_tensor(out=H[:, c, 0:w - 1], in0=dfix[:, 0:w - 1], in1=h32[:, 0:w - 1],
                                                  op=ALU.subtract), r=["dfix", "h32"], w=[self.ht(c, 0)])
            R.op("dve", lambda e: e.tensor_reduce(out=ssum[:], in_=spT[:, c, :, 15 - (w - 1):15], axis=AX.X, op=ALU.add),
                 r=["spT"], w=["ssum"])
            R.op("dve", lambda e: e.tensor_tensor(out=ssum[:], in0=ssum[:], in1=h32[:, SEQ:T], op=ALU.add),
                 r=["ssum", "h32"], w=["ssum"])
            R.op("dve", lambda e: e.scalar_tensor_tensor(out=H[:, c, SEQ:T], in0=ssum[:], scalar=1.0 / w, in1=h32[:, SEQ:T],
                                                         op0=ALU.mult, op1=ALU.subtract),
                 r=["ssum", "h32"], w=[self.ht(c, 4)])

        for c in range(NCH):
            chunk(c)
        R.dma("sp", self.dout["poolpT"].rearrange("(c p) r -> p c r", p=128), ppo[:],
              r=[f"ppo{c}" for c in range(NCH)], w=["o_poolp"], dsem="o_poolp")
        R.dma("sp", self.dout["poolsT"].rearrange("(c p) b r -> p c b r", p=128), pso[:],
              r=["pso_a"] + [f"pso_b{c}" for c in range(NCH)], w=["o_pools"], dsem="o_pools")

        def proj(i):
            c0, c1 = TB[i], TB[i + 1]
            n = c1 - c0
            for oc in range(NCH):
                gi = oc // 2
                b, ps, pt = self.bank()
                for j in range(2):
                    R.op("pe", lambda e, j=j, ps=ps, oc=oc, gi=gi: e.matmul(
                        ps[:, 0:n], lhsT=wv[:, gi, j, (oc % 2) * 128:(oc % 2 + 1) * 128], rhs=H[:, 2 * gi + j, c0:c1],
                        start=(j == 0), stop=(j == 1)), r=[wt, self.ht(2 * gi + j, i)], w=[pt])
                self.add_to_xres(oc, i, ps, pt, scale_col=self.vcol("pool_scale", oc))

        for i in range(NTT):
            proj(i)

    def gmlp_mixer(self):
        R = self.R
        nc = self.nc
        xres, H = self.xres, self.H
        Cc = self.tmp("Cc", [128, NCH, 128], F32)
        wsm = self.tmp("wsm", [128, NCH, 128], BF16)
        self.binv_row = self.tmp("binv_row", [1, 1024], BF16)
        mark = self.t_off
        bsb = self.tmp("bsb", [128, NCH, 128], F32)
        wsf = self.tmp("wsf", [128, NCH, 128], F32)
        mk = self.tmp("mk", [128, 128], F32)
        self.bs_row = self.tmp("bs_row", [1, 1024], F32)
        R.dma("sp", wsf[:], self.din["gm_wsT"], w=["wsf"], dsem="gmc0")
        R.dma("sp", mk[:], self.din["trilT"], w=["mk"], dsem="gmc1")
        R.dma("sp", self.bs_row[:], self.din["gm_b_s"].rearrange("a g t -> a (g t)"), w=["bs_row"], dsem="gmc2")
        R.dma("pool", self.binv_row[:], self.din["gm_b_in"][0:1, 1024:2048], w=["binv"], dsem="binv")
        R.op("dve", lambda e: e.tensor_tensor(out=wsm[:], in0=wsf[:], in1=mk[:].unsqueeze(1).to_broadcast([128, NCH, 128]),
                                              op=ALU.mult), r=["wsf", "mk"], w=["wsm"])
        for hf in range(2):
            b, ps, pt = self.bank()
            R.op("pe", lambda e, ps=ps, hf=hf: e.matmul(ps[:], lhsT=self.ones_bf[:], rhs=wsm[:, 4 * hf:4 * hf + 4, :],
                                                        start=True, stop=True), r=["wsm"], w=[pt])
            b2, ps2, pt2 = self.bank()
            R.op("pe", lambda e, ps2=ps2, hf=hf: e.matmul(ps2[:], lhsT=self.ones_row_f[0:1, :],
                                                          rhs=self.bs_row[0:1, 512 * hf:512 * hf + 512],
                                                          start=True, stop=True), r=["bs_row"], w=[pt2])
            R.op("act", lambda e, ps2=ps2, hf=hf: e.activation(out=bsb[:, 4 * hf:4 * hf + 4, :], in_=ps2[:], func=AF.Copy),
                 r=[pt2], w=[f"bsb{hf}"])
            for gq in range(4):
                g = 4 * hf + gq
                R.op("dve", lambda e, ps=ps, g=g, gq=gq: e.scalar_tensor_tensor(
                    out=Cc[:, g, :], in0=ps[:, gq * 128:(gq + 1) * 128], scalar=self.vcol("gm_ln_b", g), in1=bsb[:, g, :],
                    op0=ALU.mult, op1=ALU.add), r=[pt, f"bsb{hf}"], w=[f"Cc{g}"])
        R.barrier()
        self.t_off = mark
        A1 = self.tmp("gU", [128, NCH, T], BF16)
        v32 = self.tmp("v32", [128, 1024], F32)
        vh = [self.tmp(f"vh{j}", [128, 1024], BF16) for j in range(2)]
        tmx = [self.tmp("tmx0", [128, 4, 128], F32)] * 2
        st6 = self.tmp("st6", [128, 2, 6], F32)
        mv = self.tmp("mv", [128, 2], F32)
        rsd = self.tmp("rsd", [128, 1], F32)
        vs32 = v32
        vns = self.tmp("vns", [128, NCH, NS], F32)
        tms = self.tmp("tms", [128, NS], F32)
        w_in = self.din["gm_w_in"][0].rearrange("(k p) n -> p k n", p=128)
        w_out = self.din["gm_w_out"][0].rearrange("(k p) n -> p k n", p=128)
        U = [self.wload(w_in[:, :, j * 512:(j + 1) * 512], "p (k n) -> p k n", k=8) for j in range(2)]
        V = [self.wload(w_in[:, :, 1024 + j * 512:1024 + (j + 1) * 512], "p (k n) -> p k n", k=8) for j in range(2)]
        self.norm("gm_norm", 0)

        def uphase(j, i):
            c0, c1 = TB[i], TB[i + 1]
            n = c1 - c0
            wv, wt = U[j]
            for f in range(4):
                fc = 4 * j + f
                b, ps, pt = self.bank()
                for k in range(NCH):
                    R.op("pe", lambda e, f=f, k=k, ps=ps: e.matmul(ps[:, 0:n], lhsT=wv[:, k, f * 128:(f + 1) * 128],
                                                                   rhs=H[:, k, c0:c1], start=(k == 0), stop=(k == NCH - 1)),
                         r=[wt, self.ht(k, i)], w=[pt])
                R.op("act", lambda e, ps=ps, fc=fc: e.activation(out=A1[:, fc, c0:c1], in_=ps[:, 0:n], func=AF.Gelu_apprx_tanh,
                                                                 bias=self.vcol("gm_b_in", fc)), r=[pt], w=[self.at(fc, i)])

        for j in range(2):
            for i in range(NTT):
                uphase(j, i)

        def vtok(c0, m, i):
            banks = []
            for hf in range(2):
                wv, wt = V[hf]
                b, ps, pt = self.bank()
                for k in range(NCH):
                    R.op("pe", lambda e, k=k, ps=ps, wv=wv: e.matmul(ps[0:m, :], lhsT=H[:, k, c0:c0 + m], rhs=wv[:, k, :],
                                                                      start=(k == 0), stop=False),
                         r=[wt, self.ht(k, i)], w=[pt])
                R.op("pe", lambda e, ps=ps, hf=hf: e.matmul(ps[0:m, :], lhsT=self.ones_row_b[0:1, 0:m],
                                                            rhs=self.binv_row[0:1, 512 * hf:512 * hf + 512],
                                                            start=False, stop=True), r=["binv"], w=[pt])
                banks.append((ps, pt))
            return banks

        def lnorm(src, m, srct):
            for hf in range(2):
                R.op("dve", lambda e, hf=hf: e.bn_stats(out=st6[0:m, hf, :], in_=src[0:m, 512 * hf:512 * hf + 512]),
                     r=srct, w=[f"st6{hf}"])
            R.op("dve", lambda e: e.bn_aggr(out=mv[0:m, :], in_=st6[0:m, :, :].rearrange("p a b -> p (a b)")),
                 r=["st60", "st61"], w=["mv"])
            R.op("act", lambda e: e.activation(out=rsd[0:m, :], in_=mv[0:m, 1:2], func=AF.Sqrt, bias=self.eps_col[0:m, :],
                                               scale=1.0), r=["mv"], w=["rsd"])
            R.op("dve", lambda e: e.reciprocal(out=rsd[0:m, :], in_=rsd[0:m, :]), r=["rsd"], w=["rsd"])

        def vchunk(n_):
            c0 = 128 * n_
            i = c0 // 512
            banks = vtok(c0, 128, i)
            for hf, (ps, pt) in enumerate(banks):
                R.op("act", lambda e, ps=ps, hf=hf: e.activation(out=v32[:, 512 * hf:512 * hf + 512], in_=ps[:],
                                                                 func=AF.Gelu_apprx_tanh), r=[pt], w=[f"v32{hf}"])
            lnorm(v32, 128, ["v320", "v321"])
            vhb = vh[n_ % 2]
            vht = f"vh{n_ % 2}"
            R.op("dve", lambda e: e.tensor_scalar(out=vhb[:], in0=v32[:], scalar1=mv[:, 0:1], scalar2=rsd[:, 0:1],
                                                  op0=ALU.subtract, op1=ALU.mult), r=["v320", "v321", "mv", "rsd"], w=[vht])
            for hf in range(2):
                b, ps, pt = self.bank()
                for gq in range(4):
                    g = 4 * hf + gq
                    R.op("pe", lambda e, ps=ps, g=g, gq=gq: e.matmul(ps[:, gq * 128:(gq + 1) * 128],
                                                                      lhsT=vhb[:, g * 128:(g + 1) * 128], rhs=wsm[:, g, :],
                                                                      start=True, stop=True), r=[vht, "wsm"], w=[pt])
                tm = tmx[hf]
                tmt = "tmx0"
                for gq in range(4):
                    g = 4 * hf + gq
                    R.op("dve", lambda e, ps=ps, g=g, gq=gq, tm=tm: e.scalar_tensor_tensor(
                        out=tm[:, gq, :], in0=ps[:, gq * 128:(gq + 1) * 128], scalar=self.vcol("gm_ln_g", g), in1=Cc[:, g, :],
                        op0=ALU.mult, op1=ALU.add), r=[pt, f"Cc{g}"], w=[tmt + f"_{gq}"])
                R.op("dve", lambda e, tm=tm, hf=hf: e.tensor_tensor(out=A1[:, 4 * hf:4 * hf + 4, c0:c0 + 128], in0=tm[:],
                                                                      in1=A1[:, 4 * hf:4 * hf + 4, c0:c0 + 128], op=ALU.mult),
                     r=[tmt + f"_{gq}" for gq in range(4)] + [self.at(4 * hf + gq, i) for gq in range(4)],
                     w=[self.at(4 * hf + gq, i) for gq in range(4)])

        for n_ in range(16):
            vchunk(n_)

        banks = vtok(SEQ, NS, 4)
        for hf, (ps, pt) in enumerate(banks):
            R.op("act", lambda e, ps=ps, hf=hf: e.activation(out=vs32[0:NS, 512 * hf:512 * hf + 512], in_=ps[0:NS, :],
                                                             func=AF.Gelu_apprx_tanh), r=[pt], w=[f"v32{hf}"])
        lnorm(vs32, NS, ["v320", "v321"])
        R.op("dve", lambda e: e.tensor_scalar(out=vs32[0:NS, :], in0=vs32[0:NS, :], scalar1=mv[0:NS, 0:1], scalar2=rsd[0:NS, 0:1],
                                              op0=ALU.subtract, op1=ALU.mult), r=["v320", "v321", "mv", "rsd"], w=["v320", "v321"])
        for g in range(NCH):
            b, ps, pt = self.bank()
            R.op("pe", lambda e, ps=ps, g=g: e.matmul(ps[:, 0:NS], lhsT=vs32[0:NS, g * 128:(g + 1) * 128],
                                                      rhs=self.ident_f[0:NS, 0:NS], start=True, stop=True),
                 r=["v320", "v321", "ident"], w=[pt])
            R.op("dve", lambda e, ps=ps, g=g: e.tensor_scalar(out=vns[:, g, :], in0=ps[:, 0:NS],
                                                              scalar1=self.vcol("gm_ln_g", g), scalar2=self.vcol("gm_ln_b", g),
                                                              op0=ALU.mult, op1=ALU.add), r=[pt], w=[f"vns{g}"])
            R.op("dve", lambda e, g=g: e.tensor_scalar(out=tms[:], in0=vns[:, g, :], scalar1=self.vcol("gm_ws00", g),
                                                       scalar2=self.vcol("gm_bs0", g), op0=ALU.mult, op1=ALU.add),
                 r=[f"vns{g}"], w=["tms"])
            R.op("dve", lambda e, g=g: e.tensor_tensor(out=A1[:, g, SEQ:T], in0=tms[:], in1=A1[:, g, SEQ:T], op=ALU.mult),
                 r=["tms", self.at(g, 4)], w=[self.at(g, 4)])
        R.dma("sp", self.dout["gvT"].rearrange("(c p) b -> p c b", p=128), vns[:],
              r=[f"vns{g}" for g in range(NCH)], w=["o_gv"], dsem="o_gv")

        O = [self.wload(w_out[:, 4 * j:4 * j + 4, :], "p (k n) -> p k n", k=4) for j in range(2)]

        def outp(i):
            c0, c1 = TB[i], TB[i + 1]
            n = c1 - c0
            for oc in range(NCH):
                b, ps, pt = self.bank()
                for k in range(NCH):
                    wv, wt = O[k // 4]
                    R.op("pe", lambda e, k=k, ps=ps, wv=wv, oc=oc: e.matmul(ps[:, 0:n], lhsT=wv[:, k % 4, oc * 128:(oc + 1) * 128],
                                                                      rhs=A1[:, k, c0:c1], start=(k == 0), stop=(k == NCH - 1)),
                         r=[wt, self.at(k, i)], w=[pt])
                self.add_to_xres(oc, i, ps, pt)

        for i in range(NTT):
            outp(i)

    def lru_mixer(self):
        R = self.R
        xres, H = self.xres, self.H
        XO = 3
        xbp = self.tmp("xbp", [128, XO + SEQ], F32)
        xbs = self.tmp("xbs", [128, NS], F32)
        xc32 = self.tmp("xc32", [128, T], F32)
        xcb = self.tmp("xcb", [128, T], BF16)
        gate = self.tmp("gate", [128, T], BF16)
        R1 = self.tmp("R1", [128, T], F32)
        I1 = self.tmp("I1", [128, T], F32)
        T2 = self.tmp("T2", [128, T], F32)
        Gc = xcb
        scT = self.tmp("scT", [128, NCH, 3, NS], F32)
        slT = self.tmp("slT", [128, NCH, NS], F32)
        convp = self.tmp("convp", [128, NCH, 3], F32)
        convs = self.tmp("convs", [128, NCH, 3, NS], F32)
        lrup = self.tmp("lrup", [128, NCH], F32)
        lrus = self.tmp("lrus", [128, NCH, NS], F32)
        nsp8 = self.tmp("nsp8", [128, NCH], F32)
        tsm = self.tmp("tsm", [128, NS], F32)
        R.dma("sp", scT[:], self.din["scT"].rearrange("(c p) j b -> p c j b", p=128), w=["scT"], dsem="scT")
        R.dma("sp", slT[:], self.din["slT"].rearrange("(c p) b -> p c b", p=128), w=["slT"], dsem="slT")
        lam = self.vecs[:, VEC_OFF["lru_lam"]:VEC_OFF["lru_lam"] + 8]
        R.op("act", lambda e: e.activation(out=nsp8[:], in_=lam, func=AF.Exp, scale=-1.0), w=["nsp8"])
        R.op("act", lambda e: e.activation(out=nsp8[:], in_=nsp8[:], func=AF.Ln, bias=self.one_col[:], scale=1.0),
             r=["nsp8"], w=["nsp8"])
        R.op("dve", lambda e: e.tensor_scalar(out=nsp8[:], in0=nsp8[:], scalar1=-8.0, scalar2=None, op0=ALU.mult),
             r=["nsp8"], w=["nsp8"])
        nsp4 = self.tmp("nsp4", [128, NCH], F32)
        hba = self.tmp("hba", [128, NCH], F32)
        hbx = self.tmp("hbx", [128, NCH], F32)
        qcol = self.tmp("qcol", [128, 1], F32)
        R.op("dve", lambda e: e.tensor_scalar(out=nsp4[:], in0=nsp8[:], scalar1=0.5, scalar2=None, op0=ALU.mult), r=["nsp8"], w=["nsp4"])
        ba_ = self.vecs[:, VEC_OFF["lru_b_a"]:VEC_OFF["lru_b_a"] + 8]
        bx_ = self.vecs[:, VEC_OFF["lru_b_x"]:VEC_OFF["lru_b_x"] + 8]
        R.op("dve", lambda e: e.tensor_scalar(out=hba[:], in0=ba_, scalar1=0.5, scalar2=None, op0=ALU.mult), w=["hba"])
        R.op("dve", lambda e: e.tensor_scalar(out=hbx[:], in0=bx_, scalar1=0.5, scalar2=None, op0=ALU.mult), w=["hbx"])
        R.op("dve", lambda e: e.memset(qcol[:], 0.25), w=["qcol"])
        R.op("dve", lambda e: e.memset(xbp[:, 0:XO], 0.0), w=["xbp_h"])
        R.op("act", lambda e: e.activation(out=convs[:, :, 0:2, :], in_=scT[:, :, 1:3, :], func=AF.Copy), r=["scT"], w=["convs_a"])
        self.norm("lru_norm", 0)
        w_in = self.din["lru_w_in"][0].rearrange("(k p) n -> p k n", p=128)
        w_out = self.din["lru_w_out"][0].rearrange("(k p) n -> p k n", p=128)
        wa_d = self.din["lru_w_a"][0].rearrange("h i j -> i h j")
        wx_d = self.din["lru_w_x"][0].rearrange("h i j -> i h j")
        Wg = {}
        Wx_ = {}
        Wg[0] = self.wload(w_in[:, :, 0:512], "p (k n) -> p k n", k=8)
        Wx_[0] = self.wload(w_in[:, :, 1024:1536], "p (k n) -> p k n", k=8)
        (wa, wxx), wat = self.wload_parts([(wa_d, 0, "p (h j) -> p h j", dict(h=8)),
                                          (wx_d, 1024, "p (h j) -> p h j", dict(h=8))])
        Wo = [self.wload(w_out[:, 4 * j:4 * j + 4, :], "p (k n) -> p k n", k=4) for j in range(2)]

        def P(c):
            hh, cc = c // 4, c % 4
            return dict(wg=Wg[hh][0], wgt=Wg[hh][1], wxv=Wx_[hh][0], wxt=Wx_[hh][1], cc=cc,
                        cw=[self.vcol("lru_conv_w", j * 8 + c) for j in range(4)], cb=self.vcol("lru_conv_b", c),
                        wo=Wo[c // 4][0], wot=Wo[c // 4][1])

        def S0(c, i):
            q = P(c)
            wg, wgt, wxv, wxt, cc = q["wg"], q["wgt"], q["wxv"], q["wxt"], q["cc"]
            c0, c1 = TB[i], TB[i + 1]
            n = c1 - c0
            b, ps, pt = self.bank()
            for k in range(NCH):
                R.op("pe", lambda e, k=k, ps=ps: e.matmul(ps[:, 0:n], lhsT=wg[:, k, cc * 128:(cc + 1) * 128], rhs=H[:, k, c0:c1],
                                                          start=(k == 0), stop=(k == NCH - 1)), r=[wgt, self.ht(k, i)], w=[pt])
            R.op("act", lambda e, ps=ps: e.activation(out=gate[:, c0:c1], in_=ps[:, 0:n], func=AF.Gelu_apprx_tanh),
                 r=[pt], w=[f"gate{i}"])
            b, ps, pt = self.bank()
            for k in range(NCH):
                R.op("pe", lambda e, k=k, ps=ps: e.matmul(ps[:, 0:n], lhsT=wxv[:, k, cc * 128:(cc + 1) * 128], rhs=H[:, k, c0:c1],
                                                          start=(k == 0), stop=(k == NCH - 1)), r=[wxt, self.ht(k, i)], w=[pt])
            if i < 4:
                R.op("act", lambda e, ps=ps: e.activation(out=xbp[:, XO + c0:XO + c1], in_=ps[:, 0:n], func=AF.Copy),
                     r=[pt], w=[f"xbp{i}"])
            else:
                R.op("act", lambda e, ps=ps: e.activation(out=xbs[:], in_=ps[:, 0:n], func=AF.Copy), r=[pt], w=["xbs"])

        def S1(c, i):
            q = P(c)
            cw, cb = q["cw"], q["cb"]
            c0, c1 = TB[i], TB[i + 1]
            if i < 4:
                prev = f"xbp{i - 1}" if i > 0 else "xbp_h"
                R.op("act", lambda e: e.activation(out=xc32[:, c0:c1], in_=xbp[:, XO + c0:XO + c1], func=AF.Identity, scale=cw[3], bias=cb),
                     r=[f"xbp{i}"], w=[f"xc{i}"])
                for s_ in (1, 2, 3):
                    R.op("dve", lambda e, s_=s_: e.scalar_tensor_tensor(out=xc32[:, c0:c1], in0=xbp[:, XO + c0 - s_:XO + c1 - s_],
                                                                         scalar=cw[3 - s_], in1=xc32[:, c0:c1], op0=ALU.mult, op1=ALU.add),
                         r=[f"xbp{i}", prev, f"xc{i}"], w=[f"xc{i}"])
                if i == 3:
                    R.op("act", lambda e: e.activation(out=convp[:, c, :], in_=xbp[:, XO + SEQ - 3:XO + SEQ], func=AF.Copy),
                         r=["xbp3"], w=[f"convp{c}"])
            else:
                R.op("act", lambda e: e.activation(out=convs[:, c, 2, :], in_=xbs[:], func=AF.Copy), r=["xbs"], w=[f"convs_b{c}"])
                R.op("act", lambda e: e.activation(out=xc32[:, c0:c1], in_=xbs[:], func=AF.Identity, scale=cw[3], bias=cb),
                     r=["xbs"], w=[f"xc{i}"])
                for j in range(3):
                    R.op("dve", lambda e, j=j: e.scalar_tensor_tensor(out=xc32[:, c0:c1], in0=scT[:, c, j, :], scalar=cw[j],
                                                                       in1=xc32[:, c0:c1], op0=ALU.mult, op1=ALU.add),
                         r=["scT", f"xc{i}"], w=[f"xc{i}"])
            R.op("act", lambda e: e.activation(out=xcb[:, c0:c1], in_=xc32[:, c0:c1], func=AF.Copy), r=[f"xc{i}"], w=[f"xcb{i}"])

        def S2(c, i):
            c0, c1 = TB[i], TB[i + 1]
            n = c1 - c0
            b, ps, pt = self.bank()
            R.op("pe", lambda e, ps=ps: e.matmul(ps[:, 0:n], lhsT=wa[:, c, :], rhs=xcb[:, c0:c1], start=True, stop=True),
                 r=[wat, f"xcb{i}"], w=[pt])
            R.op("act", lambda e, ps=ps: e.activation(out=R1[:, c0:c1], in_=ps[:, 0:n], func=AF.Tanh, scale=0.5,
                                                      bias=hba[:, c:c + 1]), r=[pt, "hba"], w=[f"R1_{i}"])
            b, ps, pt = self.bank()
            R.op("pe", lambda e, ps=ps: e.matmul(ps[:, 0:n], lhsT=wxx[:, c, :], rhs=xcb[:, c0:c1], start=True, stop=True),
                 r=[wat, f"xcb{i}"], w=[pt])
            R.op("act", lambda e, ps=ps: e.activation(out=I1[:, c0:c1], in_=ps[:, 0:n], func=AF.Tanh, scale=0.5,
                                                      bias=hbx[:, c:c + 1]), r=[pt, "hbx"], w=[f"I1_{i}"])
            R.op("act", lambda e: e.activation(out=R1[:, c0:c1], in_=R1[:, c0:c1], func=AF.Exp, scale=nsp4[:, c:c + 1],
                                               bias=nsp4[:, c:c + 1]), r=[f"R1_{i}", "nsp4"], w=[f"R1_{i}"])
            R.op("act", lambda e: e.activation(out=T2[:, c0:c1], in_=R1[:, c0:c1], func=AF.Square), r=[f"R1_{i}"], w=[f"T2_{i}"])

        def S2b(c, i):
            c0, c1 = TB[i], TB[i + 1]
            R.op("act", lambda e: e.activation(out=T2[:, c0:c1], in_=T2[:, c0:c1], func=AF.Sqrt, bias=qcol[:], scale=-0.25),
                 r=[f"T2_{i}", "qcol"], w=[f"T2_{i}"])

        def S3(c, i):
            c0, c1 = TB[i], TB[i + 1]
            if i == 0:
                R.op("dve", lambda e: e.memset(T2[:, 0:1], 0.5), r=["T2_0"], w=["T2_0"])
            R.op("dve", lambda e: e.scalar_tensor_tensor(out=I1[:, c0:c1], in0=I1[:, c0:c1], scalar=1.0, in1=xc32[:, c0:c1],
                                                         op0=ALU.add, op1=ALU.mult),
                 r=[f"I1_{i}", f"xc{i}"], w=[f"I1_{i}"])
            R.op("dve", lambda e: e.tensor_tensor(out=I1[:, c0:c1], in0=I1[:, c0:c1], in1=T2[:, c0:c1], op=ALU.mult),
                 r=[f"I1_{i}", f"T2_{i}"], w=[f"I1_{i}"])
            if i < 4:
                init = 0.0 if i == 0 else T2[:, c0 - 1:c0]
                R.op("dve", lambda e: e.tensor_tensor_scan(out=T2[:, c0:c1], data0=R1[:, c0:c1], data1=I1[:, c0:c1], initial=init,
                                                           op0=ALU.mult, op1=ALU.add),
                     r=[f"R1_{i}", f"I1_{i}", f"T2_{i}"] + ([f"T2_{i - 1}"] if i > 0 else []), w=[f"T2_{i}"])
                if i == 3:
                    R.op("act", lambda e: e.activation(out=lrup[:, c:c + 1], in_=T2[:, SEQ - 1:SEQ], func=AF.Copy), r=["T2_3"], w=[f"lrup{c}"])
            else:
                R.op("dve", lambda e: e.tensor_tensor(out=tsm[:], in0=R1[:, c0:c1], in1=slT[:, c, :], op=ALU.mult),
                     r=[f"R1_{i}", "slT"], w=["tsm"])
                R.op("dve", lambda e: e.tensor_tensor(out=T2[:, c0:c1], in0=I1[:, c0:c1], in1=tsm[:], op=ALU.add),
                     r=[f"I1_{i}", "tsm", f"T2_{i}"], w=[f"T2_{i}"])
                R.op("act", lambda e: e.activation(out=lrus[:, c, :], in_=T2[:, c0:c1], func=AF.Copy), r=[f"T2_{i}"], w=[f"lrus{c}"])
            R.op("dve", lambda e: e.tensor_tensor(out=Gc[:, c0:c1], in0=T2[:, c0:c1], in1=gate[:, c0:c1], op=ALU.mult),
                 r=[f"T2_{i}", f"gate{i}", f"xcb{i}"], w=[f"xcb{i}"])

        def S4(c, i):
            q = P(c)
            wo, wot = q["wo"], q["wot"]
            c0, c1 = TB[i], TB[i + 1]
            n = c1 - c0
            for oc in range(NCH):
                b, ps, pt = self.bank()
                R.op("pe", lambda e, ps=ps, oc=oc: e.matmul(ps[:, 0:n], lhsT=wo[:, c % 4, oc * 128:(oc + 1) * 128], rhs=Gc[:, c0:c1],
                                                            start=True, stop=True), r=[wot, f"xcb{i}"], w=[pt])
                self.add_to_xres(oc, i, ps, pt)

        items = [(c, i) for c in range(NCH) for i in range(NTT)]
        stages = [S0, S1, S2, S2b, S3, S4]
        loaded1 = False
        for t in range(len(items) + len(stages) - 1):
            for si, st in enumerate(stages):
                k = t - si
                if 0 <= k < len(items):
                    c, i = items[k]
                    if si == 0 and c == 4 and not loaded1:
                        Wg[1] = self.wload(w_in[:, :, 512:1024], "p (k n) -> p k n", k=8)
                        Wx_[1] = self.wload(w_in[:, :, 1536:2048], "p (k n) -> p k n", k=8)
                        loaded1 = True
                    st(c, i)
        R.dma("sp", self.dout["convpT"].rearrange("(c p) j -> p c j", p=128), convp[:],
              r=[f"convp{c}" for c in range(NCH)], w=["o_convp"], dsem="o_convp")
        R.dma("sp", self.dout["convsT"].rearrange("(c p) j b -> p c j b", p=128), convs[:],
              r=["convs_a"] + [f"convs_b{c}" for c in range(NCH)], w=["o_convs"], dsem="o_convs")
        R.dma("sp", self.dout["lrupT"], lrup[:],
              r=[f"lrup{c}" for c in range(NCH)], w=["o_lrup"], dsem="o_lrup")
        R.dma("sp", self.dout["lrusT"].rearrange("(c p) b -> p c b", p=128), lrus[:],
              r=[f"lrus{c}" for c in range(NCH)], w=["o_lrus"], dsem="o_lrus")

    def ret_mixer(self):
        R = self.R
        xres, H = self.xres, self.H
        DKS = 128 ** -0.5
        lg = np.log1p(-np.exp2(-5.0 - np.arange(8, dtype=np.float32))).astype(np.float32)
        cdec = [float(np.exp(lg[h] * np.float32(128.0))) for h in range(8)]
        gam = [float(np.exp(lg[h] * np.float32(1.0))) for h in range(8)]
        tmp = self.tmp
        G2 = tmp("G2", [128, 4, T], BF16)
        qs = tmp("qs", [128, 512], BF16)
        ks = tmp("ks", [128, 512], BF16)
        tA = tmp("tA", [128, 512], F32)
        tB = tmp("tB", [128, 512], F32)
        qT = [tmp(f"qT{p}", [128, 512], BF16) for p in range(2)]
        qd = [tmp(f"qd{p}", [128, 512], BF16) for p in range(2)]
        kT = [tmp(f"kT{p}", [128, 512], BF16) for p in range(2)]
        kdt = [tmp(f"kdt{p}", [128, 4, 128], BF16) for p in range(2)]
        vtk = [tmp(f"vtk{p}", [128, 4, 256], BF16) for p in range(2)]
        gs = [tmp(f"gs{p}", [128, 2, 512], BF16) for p in range(2)]
        sc = tmp("sc", [128, 4, 128], BF16)
        ob = tmp("ob", [128, 2, 512], BF16)
        osq = tmp("osq", [128, 2, 512], BF16)
        mu = tmp("mu", [128, 512], F32)
        rs = tB
        tn = tA
        S32 = tmp("S32", [128, 256], F32)
        Sbf = [tmp(f"Sbf{j}", [128, 256], BF16) for j in range(5)]
        dmk = tmp("dmk", [128, 128], F32)
        qdc = tmp("qdc", [128, 128], F32)
        kdc = tmp("kdc", [128, 8], F32)
        gne = tmp("gne", [128, 1], F32)
        permM = tmp("permM", [128, 128], BF16)
        identb = tmp("identb", [128, 128], BF16)
        Sst = [tmp(f"Sst{j}", [128, 2, 256], F32) for j in range(2)]
        Km = tmp("Km", [NS, 8, 128], BF16)
        ktk_s = tmp("ktk_s", [NS, 128], BF16)
        vtk_s = tmp("vtk_s", [NS, 256], BF16)
        qs32 = tmp("qs32", [128, NS], F32)
        ks32 = tmp("ks32", [128, NS], F32)
        qds32 = tmp("qds32", [128, NS], F32)
        prodb = tmp("prodb", [128, NS], BF16)
        dots = tmp("dots", [128, NS], F32)
        vTs = tmp("vTs", [128, 2, NS], F32)
        os_ = tmp("os_", [128, 2, NS], F32)
        gss = tmp("gss", [128, 2, NS], BF16)
        crs = tmp("crs", [128, 2, NS], F32)
        R.dma("pool", permM[:], self.din["permM"], w=["permM"], dsem="permM")
        R.dma("pool", identb[:], self.din["ident"], w=["identb"], dsem="identb")
        R.dma("sp", kdc[:], self.din["kdec"], w=["kdc"], dsem="kdc")
        R.op("dve", lambda e: e.memset(gne[:], GN_EPS), w=["gne"])
        self.norm("ret_norm", 0)
        R.barrier()
        ctab = [self.rstd[:, 0:512], self.rstd[:, 1024:1536]]
        stab = [self.rstd[:, 512:1024], self.rstd[:, 1536:2048]]
        w_in = self.din["ret_w_in"][0].rearrange("(k p) n -> p k n", p=128)
        w_out = self.din["ret_w_out"][0].rearrange("(f p) n -> p f n", p=128)
        GRP = [(0, 512), (512, 1024), (1024, 1536), (1536, 2048), (2048, 2064)]
        self._rope_i = 0

        def head(hh, Wo):
            i = self.w_i
            self.w_i = (self.w_i + 1) % self.NW
            slot = self.ws[i]
            WA = slot[:, 0:4096].rearrange("p (k n) -> p k n", k=8)
            wat = f"ws{i}"
            for (lo, hi, src0) in ((0, 128, 128 * hh), (128, 256, 1024 + 128 * hh), (256, 512, 2048 + 256 * hh)):
                R.dma("pool", WA[:, :, lo:hi], w_in[:, :, src0:src0 + (hi - lo)], w=[wat], dsem=wat)
            WB, wbt = self.wload(w_in[:, :, 4096 + 256 * hh:4096 + 256 * hh + 256], "p (k n) -> p k n", k=8)
            R.dma("sp", dmk[:], self.din["dmaskT"][hh], w=["dmk"], dsem="dmk")
            R.dma("sp", qdc[:], self.din["qdecb"][hh], w=["qdc"], dsem="qdc")
            R.op("dve", lambda e: e.memset(S32[:], 0.0), w=["S32"])
            R.op("dve", lambda e: e.memset(Sbf[4][:], 0.0), w=["Sbf4"])
            g2s = (hh % 2) * 2

            def rope(src_bf, srct, psP, pt, n, j, out_ap, outt):
                R.op("dve", lambda e: e.tensor_tensor(out=tA[:, 0:n], in0=src_bf[:, 0:n], in1=ctab[j][:, 0:n], op=ALU.mult),
                     r=[srct, f"ctab{j}"], w=["tA"])
                R.op("dve", lambda e: e.tensor_tensor(out=tB[:, 0:n], in0=psP[:, 0:n], in1=stab[j][:, 0:n], op=ALU.mult),
                     r=[pt, f"stab{j}"], w=["tB"])
                R.op("dve", lambda e: e.tensor_tensor(out=out_ap, in0=tA[:, 0:n], in1=tB[:, 0:n], op=ALU.add),
                     r=["tA", "tB"], w=[outt])

            def qk_proj(col_lo, c0, c1, i, dst_s, dst_t, scale):
                n = c1 - c0
                b, ps, pt = self.bank()
                for k in range(NCH):
                    R.op("pe", lambda e, k=k, ps=ps: e.matmul(ps[:, 0:n], lhsT=WA[:, k, col_lo:col_lo + 128], rhs=H[:, k, c0:c1],
                                                              start=(k == 0), stop=(k == NCH - 1)), r=[wat, self.ht(k, i)], w=[pt])
                R.op("act", lambda e, ps=ps: e.activation(out=dst_s[:, 0:n], in_=ps[:, 0:n], func=AF.Copy, scale=scale),
                     r=[pt], w=[dst_t])

            def perm(dst_s, dst_t, n):
                b, ps2, pt2 = self.bank()
                R.op("pe", lambda e, ps2=ps2: e.matmul(ps2[:, 0:n], lhsT=permM[:], rhs=dst_s[:, 0:n], start=True, stop=True),
                     r=["permM", dst_t], w=[pt2])
                return ps2, pt2

            def gnorm(src, srct, n, cols0, i, p):
                for vc in range(2):
                    R.op("act", lambda e, vc=vc: e.activation(out=osq[:, vc, 0:n], in_=src[vc], func=AF.Square), r=[srct[vc]], w=[f"osq{vc}"])
                    R.op("act", lambda e, vc=vc: e.activation(out=ob[:, vc, 0:n], in_=src[vc], func=AF.Copy), r=[srct[vc]], w=[f"ob{vc}"])
                b, p1, p1t = self.bank()
                for vc in range(2):
                    R.op("pe", lambda e, vc=vc: e.matmul(p1[:, 0:n], lhsT=self.ones_bf[:], rhs=ob[:, vc, 0:n], start=(vc == 0), stop=(vc == 1)),
                         r=[f"ob{vc}"], w=[p1t])
                b, p2, p2t = self.bank()
                for vc in range(2):
                    R.op("pe", lambda e, vc=vc: e.matmul(p2[:, 0:n], lhsT=self.ones_bf[:], rhs=osq[:, vc, 0:n], start=(vc == 0), stop=(vc == 1)),
                         r=[f"osq{vc}"], w=[p2t])
                R.op("act", lambda e: e.activation(out=mu[:, 0:n], in_=p1[:, 0:n], func=AF.Copy, scale=1.0 / 256), r=[p1t], w=["mu"])
                R.op("dve", lambda e: e.tensor_tensor(out=rs[:, 0:n], in0=mu[:, 0:n], in1=mu[:, 0:n], op=ALU.mult), r=["mu"], w=["tB"])
                R.op("dve", lambda e: e.scalar_tensor_tensor(out=rs[:, 0:n], in0=p2[:, 0:n], scalar=1.0 / 256, in1=rs[:, 0:n],
                                                             op0=ALU.mult, op1=ALU.subtract), r=[p2t, "tB"], w=["tB"])
                R.op("act", lambda e: e.activation(out=rs[:, 0:n], in_=rs[:, 0:n], func=AF.Sqrt, bias=gne[:], scale=1.0),
                     r=["tB", "gne"], w=["tB"])
                R.op("dve", lambda e: e.reciprocal(out=rs[:, 0:n], in_=rs[:, 0:n]), r=["tB"], w=["tB"])
                for vc in range(2):
                    R.op("dve", lambda e, vc=vc: e.tensor_tensor(out=tn[:, 0:n], in0=src[vc], in1=mu[:, 0:n], op=ALU.subtract),
                         r=[srct[vc], "mu"], w=["tA"])
                    R.op("dve", lambda e, vc=vc: e.tensor_tensor(out=tn[:, 0:n], in0=tn[:, 0:n], in1=rs[:, 0:n], op=ALU.mult),
                         r=["tA", "tB"], w=["tA"])
                    R.op("dve", lambda e, vc=vc: e.tensor_scalar(out=tn[:, 0:n], in0=tn[:, 0:n],
                                                                 scalar1=self.vcol("ret_gn_g", 2 * hh + vc),
                                                                 scalar2=self.vcol("ret_gn_b", 2 * hh + vc), op0=ALU.mult, op1=ALU.add),
                         r=["tA"], w=["tA"])
                    gsrc = gss[:, vc, :] if p < 0 else gs[p][:, vc, 0:n]
                    gtok = f"gss{vc}" if p < 0 else f"gs{p}_{vc}"
                    R.op("dve", lambda e, vc=vc, gsrc=gsrc: e.tensor_tensor(out=G2[:, g2s + vc, cols0:cols0 + n], in0=tn[:, 0:n],
                                                                            in1=gsrc, op=ALU.mult),
                         r=["tA", gtok], w=[f"g2_{g2s + vc}_{i}"])

            def P1(gi):
                c0, c1 = GRP[gi]
                n = c1 - c0
                i = gi
                p = gi % 2
                samp = (gi == 4)
                j = self._rope_i % 2
                self._rope_i += 1
                R.dma("sp", ctab[j][:, 0:n], self.din["ropeC"][:, c0:c1], w=[f"ctab{j}"], dsem=f"ctab{j}")
                R.dma("sp", stab[j][:, 0:n], self.din["ropeS"][:, c0:c1], w=[f"stab{j}"], dsem=f"stab{j}")
                qk_proj(0, c0, c1, i, qs, "qs", 1.0)
                qk_proj(128, c0, c1, i, ks, "ks", DKS)
                for vc in range(2):
                    b, ps, pt = self.bank()
                    for k in range(NCH):
                        R.op("pe", lambda e, k=k, ps=ps, vc=vc: e.matmul(ps[:, 0:n], lhsT=WB[:, k, vc * 128:(vc + 1) * 128], rhs=H[:, k, c0:c1],
                                                                         start=(k == 0), stop=(k == NCH - 1)), r=[wbt, self.ht(k, i)], w=[pt])
                    if samp:
                        R.op("act", lambda e, ps=ps, vc=vc: e.activation(out=gss[:, vc, :], in_=ps[:, 0:n], func=AF.Silu), r=[pt], w=[f"gss{vc}"])
                    else:
                        R.op("act", lambda e, ps=ps, vc=vc: e.activation(out=gs[p][:, vc, 0:n], in_=ps[:, 0:n], func=AF.Silu), r=[pt], w=[f"gs{p}_{vc}"])
                if not samp:
                    for half in range(2):
                        b, ps, pt = self.bank()
                        for jq in range(2):
                            jj = 2 * half + jq
                            for k in range(NCH):
                                R.op("pe", lambda e, k=k, ps=ps, jj=jj, jq=jq: e.matmul(
                                    ps[:, jq * 256:(jq + 1) * 256], lhsT=H[:, k, c0 + jj * 128:c0 + (jj + 1) * 128], rhs=WA[:, k, 256:512],
                                    start=(k == 0), stop=(k == NCH - 1)), r=[wat, self.ht(k, i)], w=[pt])
                        R.op("act", lambda e, ps=ps, half=half: e.activation(out=vtk[p][:, 2 * half:2 * half + 2, :].rearrange("p j v -> p (j v)"),
                                                                             in_=ps[:], func=AF.Copy), r=[pt], w=[f"vtk{p}_{half}"])
                psP, ptP = perm(qs, "qs", n)
                rope(qs, "qs", psP, ptP, n, j, (qs32[:] if samp else qT[p][:, 0:n]), ("qs32" if samp else f"qT{p}"))
                psP, ptP = perm(ks, "ks", n)
                rope(ks, "ks", psP, ptP, n, j, (ks32[:] if samp else kT[p][:, 0:n]), ("ks32" if samp else f"kT{p}"))
                if not samp:
                    R.op("dve", lambda e: e.tensor_tensor(out=qd[p][:].rearrange("p (j t) -> p j t", j=4),
                                                          in0=qT[p][:].rearrange("p (j t) -> p j t", j=4),
                                                          in1=qdc[:].unsqueeze(1).to_broadcast([128, 4, 128]), op=ALU.mult),
                         r=[f"qT{p}", "qdc"], w=[f"qd{p}"])
                else:
                    R.op("dve", lambda e: e.tensor_scalar(out=qds32[:], in0=qs32[:], scalar1=gam[hh], scalar2=None, op0=ALU.mult),
                         r=["qs32"], w=["qds32"])
                    R.op("dve", lambda e: e.tensor_tensor(out=prodb[:], in0=qs32[:], in1=ks32[:], op=ALU.mult), r=["qs32", "ks32"], w=["prodb"])
                    R.op("act", lambda e: e.activation(out=ks[:, 0:NS], in_=ks32[:], func=AF.Copy), r=["ks32"], w=["ks"])
                    bd, pd, pdt = self.bank()
                    R.op("pe", lambda e: e.matmul(pd[:, 0:NS], lhsT=self.ones_bf[:], rhs=prodb[:], start=True, stop=True), r=["prodb"], w=[pdt])
                    R.op("act", lambda e: e.activation(out=dots[:], in_=pd[:, 0:NS], func=AF.Copy), r=[pdt], w=["dots"])
                    for vc in range(2):
                        b, ps, pt = self.bank()
                        for k in range(NCH):
                            R.op("pe", lambda e, k=k, ps=ps, vc=vc: e.matmul(ps[:, 0:NS], lhsT=WA[:, k, 256 + vc * 128:256 + (vc + 1) * 128],
                                                                             rhs=H[:, k, c0:c1], start=(k == 0), stop=(k == NCH - 1)),
                                 r=[wat, self.ht(k, i)], w=[pt])
                        R.op("act", lambda e, ps=ps, vc=vc: e.activation(out=vTs[:, vc, :], in_=ps[:, 0:NS], func=AF.Copy), r=[pt], w=[f"vTs{vc}"])
                    b, ps, pt = self.bank()
                    for k in range(NCH):
                        R.op("pe", lambda e, k=k, ps=ps: e.matmul(ps[0:NS, 0:256], lhsT=H[:, k, c0:c1], rhs=WA[:, k, 256:512],
                                                                  start=(k == 0), stop=(k == NCH - 1)), r=[wat, self.ht(k, i)], w=[pt])
                    R.op("act", lambda e, ps=ps: e.activation(out=vtk_s[:], in_=ps[0:NS, 0:256], func=AF.Copy), r=[pt], w=["vtk_s"])
                    b, ps, pt = self.bank()
                    R.op("pe", lambda e, ps=ps: e.matmul(ps[0:NS, 0:128], lhsT=ks[:, 0:NS], rhs=identb[:], start=True, stop=True),
                         r=["ks", "identb"], w=[pt])
                    R.op("act", lambda e, ps=ps: e.activation(out=ktk_s[:], in_=ps[0:NS, 0:128], func=AF.Copy), r=[pt], w=["ktk_s"])

            def P1c(gi):
                p = gi % 2
                b, ps, pt = self.bank()
                for jj in range(4):
                    R.op("pe", lambda e, jj=jj, ps=ps: e.matmul(ps[:, jj * 128:(jj + 1) * 128], lhsT=kT[p][:, jj * 128:(jj + 1) * 128], rhs=identb[:],
                                                                start=True, stop=True), r=[f"kT{p}", "identb"], w=[pt])
                R.op("act", lambda e, ps=ps: e.activation(out=kdt[p][:].rearrange("p j d -> p (j d)"), in_=ps[:], func=AF.Copy,
                                                          scale=kdc[:, hh:hh + 1]), r=[pt, "kdc"], w=[f"kdt{p}"])

            def P2(gi):
                c0, c1 = GRP[gi]
                n = c1 - c0
                i = gi
                p = gi % 2
                b, ps, pt = self.bank()
                for jj in range(4):
                    R.op("pe", lambda e, jj=jj, ps=ps: e.matmul(ps[:, jj * 128:(jj + 1) * 128], lhsT=kT[p][:, jj * 128:(jj + 1) * 128],
                                                                rhs=qT[p][:, jj * 128:(jj + 1) * 128], start=True, stop=True),
                         r=[f"kT{p}", f"qT{p}"], w=[pt])
                R.op("dve", lambda e, ps=ps: e.tensor_tensor(out=sc[:], in0=ps[:].rearrange("p (j t) -> p j t", j=4),
                                                             in1=dmk[:].unsqueeze(1).to_broadcast([128, 4, 128]), op=ALU.mult),
                     r=[pt, "dmk"], w=["sc"])
                sidx = [0, 1, 2, 3 + p]
                for jj in range(4):
                    b, psS, psSt = self.bank()
                    R.op("pe", lambda e, jj=jj, psS=psS: e.matmul(psS[:, 0:256], lhsT=kdt[p][:, jj, :], rhs=vtk[p][:, jj, :], start=True, stop=True),
                         r=[f"kdt{p}", f"vtk{p}_{jj // 2}"], w=[psSt])
                    R.op("dve", lambda e, jj=jj, psS=psS: e.scalar_tensor_tensor(out=Sbf[sidx[jj]][:], in0=S32[:], scalar=cdec[hh], in1=psS[:, 0:256],
                                                                                 op0=ALU.mult, op1=ALU.add), r=[psSt, "S32"], w=[f"Sbf{sidx[jj]}"])
                    R.op("dve", lambda e, psS=psS: e.scalar_tensor_tensor(out=S32[:], in0=S32[:], scalar=cdec[hh], in1=psS[:, 0:256],
                                                                          op0=ALU.mult, op1=ALU.add), r=[psSt, "S32"], w=["S32"])
                bo = [self.bank(), self.bank()]
                for jj in range(4):
                    sprev = (jj - 1) if jj > 0 else (3 + (1 - p))
                    for vc in range(2):
                        _, po, pot = bo[vc]
                        R.op("pe", lambda e, jj=jj, vc=vc, po=po: e.matmul(po[:, jj * 128:(jj + 1) * 128], lhsT=vtk[p][:, jj, vc * 128:(vc + 1) * 128],
                                                                           rhs=sc[:, jj, :], start=True, stop=False),
                             r=[f"vtk{p}_{jj // 2}", "sc"], w=[pot])
                        R.op("pe", lambda e, jj=jj, vc=vc, po=po, sprev=sprev: e.matmul(
                            po[:, jj * 128:(jj + 1) * 128], lhsT=Sbf[sprev][:, vc * 128:(vc + 1) * 128],
                            rhs=qd[p][:, jj * 128:(jj + 1) * 128], start=False, stop=True),
                            r=[f"Sbf{sprev}", f"qd{p}"], w=[pot])
                gnorm([bo[0][1][:, 0:n], bo[1][1][:, 0:n]], [bo[0][2], bo[1][2]], n, c0, i, p)

            def sbatch(sbi):
                St = Sst[sbi % 2]
                Stt = f"Sst{sbi % 2}"
                b0 = 2 * sbi
                if sbi % 4 == 0:
                    hf = sbi // 4
                    R.op("dve", lambda e, hf=hf: e.tensor_tensor(
                        out=Km[:], in0=ktk_s[:].unsqueeze(1).to_broadcast([NS, 8, 128]),
                        in1=self.ident_f[0:NS, 8 * hf:8 * hf + 8].unsqueeze(2).to_broadcast([NS, 8, 128]), op=ALU.mult),
                        r=["ktk_s", "ident"], w=["Km"])
                R.dma("sp", St[:], self.din["sret"][b0:b0 + 2, hh].rearrange("b d v -> d b v"), w=[Stt], dsem=Stt)
                b, po, pot = self.bank()
                for vc in range(2):
                    for bi in range(2):
                        bb = b0 + bi
                        R.op("pe", lambda e, bi=bi, bb=bb, vc=vc: e.matmul(
                            po[:, 2 * vc + bi:2 * vc + bi + 1], lhsT=St[:, bi, vc * 128:(vc + 1) * 128], rhs=qds32[:, bb:bb + 1],
                            start=True, stop=True), r=[Stt, "qds32"], w=[pot])
                R.op("act", lambda e: e.activation(out=crs[:, :, b0:b0 + 2], in_=po[:, 0:4].rearrange("p (v b) -> p v b", v=2), func=AF.Copy),
                     r=[pot], w=[f"crs{sbi}"])
                for bi in range(2):
                    bb = b0 + bi
                    b, psS, psSt = self.bank()
                    R.op("pe", lambda e, bb=bb, psS=psS: e.matmul(psS[:, 0:256], lhsT=Km[:, bb % 8, :], rhs=vtk_s[:], start=True, stop=True),
                         r=["Km", "vtk_s"], w=[psSt])
                    R.op("dve", lambda e, bi=bi, psS=psS: e.scalar_tensor_tensor(
                        out=St[:, bi, :], in0=St[:, bi, :], scalar=gam[hh], in1=psS[:, 0:256], op0=ALU.mult, op1=ALU.add),
                        r=[psSt, Stt], w=[Stt])
                R.dma("act", self.dout["rets"][b0:b0 + 2, hh].rearrange("b d v -> d b v"), St[:], r=[Stt], w=[f"o_rets{sbi % 2}"],
                      dsem=f"o_rets{sbi % 2}")

            def sfin():
                for vc in range(2):
                    R.op("dve", lambda e, vc=vc: e.tensor_tensor(out=os_[:, vc, :], in0=vTs[:, vc, :], in1=dots[:], op=ALU.mult),
                         r=[f"vTs{vc}", "dots"], w=[f"os{vc}"])
                    R.op("dve", lambda e, vc=vc: e.tensor_tensor(out=os_[:, vc, :], in0=os_[:, vc, :], in1=crs[:, vc, :], op=ALU.add),
                         r=[f"os{vc}"] + [f"crs{k}" for k in range(8)], w=[f"os{vc}"])
                gnorm([os_[:, 0, :], os_[:, 1, :]], ["os0", "os1"], NS, SEQ, 4, -1)

            P1(4)
            P1(0)
            P1c(0)
            for gi in range(4):
                if gi + 1 < 4:
                    P1(gi + 1)
                P2(gi)
                sbatch(2 * gi)
                sbatch(2 * gi + 1)
                if gi + 1 < 4:
                    P1c(gi + 1)
                if gi == 3:
                    R.dma("sp", self.dout["retp"][hh], S32[:], r=["S32"], w=["o_retp"], dsem="o_retp")
            sfin()
            if hh % 2 == 1:
                wo, wot = Wo
                for i in range(NTT):
                    self._ret_out(i, wo, wot, G2)

        for hh in range(8):
            head(hh, (self.wload(w_out[:, 4 * (hh // 2):4 * (hh // 2) + 4, :], "p (f n) -> p f n", f=4) if hh % 2 == 1 else None))

    def _ret_out(self, i, wo, wot, G2):
        R = self.R
        c0, c1 = TB[i], TB[i + 1]
        n = c1 - c0
        for oc in range(NCH):
            b, ps, pt = self.bank()
            for j in range(4):
                R.op("pe", lambda e, j=j, ps=ps, oc=oc: e.matmul(ps[:, 0:n], lhsT=wo[:, j, oc * 128:(oc + 1) * 128], rhs=G2[:, j, c0:c1],
                                                                 start=(j == 0), stop=(j == 3)), r=[wot, f"g2_{j}_{i}"], w=[pt])
            self.add_to_xres(oc, i, ps, pt)

    def build(self):
        nc, R = self.nc, self.R
        inp, outp = self.inp, self.outp
        mix = [m for m in self.mixers if m < self.nlayers]
        xT = inp("xT", [D, T])
        vecs_d = inp("vecs", [128, NVEC])
        ident_d = inp("ident", [128, 128])
        inp("mlp_w1", [4, D, DFF])
        inp("mlp_w2", [4, DFF, D])
        yT = outp("yT", [D, T])
        if 0 in mix:
            inp("pool_w", [1, 4, 256, 256])
            inp("spT", [D, NS, 15])
            inp("rc16", [128, 4, 16])
            outp("poolpT", [D, 15])
            outp("poolsT", [D, NS, 15])
        if 1 in mix:
            inp("gm_w_in", [1, D, 2 * D])
            inp("gm_w_out", [1, D, D])
            inp("gm_b_in", [1, 2 * D])
            inp("gm_b_s", [1, 8, 128])
            inp("gm_wsT", [128, 8, 128])
            inp("trilT", [128, 128])
            outp("gvT", [D, NS])
        if 2 in mix:
            inp("ret_w_in", [1, D, 6144])
            inp("ret_w_out", [1, 2048, D])
            inp("sret", [NS, 8, 128, 256])
            inp("permM", [128, 128])
            inp("kdec", [128, 8])
            inp("dmaskT", [8, 128, 128])
            inp("qdecb", [8, 128, 128])
            inp("ropeC", [128, T])
            inp("ropeS", [128, T])
            outp("retp", [8, 128, 256])
            outp("rets", [NS, 8, 128, 256])
        if 3 in mix:
            inp("lru_w_in", [1, D, 2 * D])
            inp("lru_w_out", [1, D, D])
            inp("lru_w_a", [1, 8, 128, 128])
            inp("lru_w_x", [1, 8, 128, 128])
            inp("scT", [D, 3, NS])
            inp("slT", [D, NS])
            outp("convpT", [D, 3])
            outp("convsT", [D, 3, NS])
            outp("lrupT", [128, NCH])
            outp("lrusT", [D, NS])
        self.xres = self.sb("xres", [128, NCH, T], F32)
        self.H = self.sb("H", [128, NCH, T], BF16)
        self.NW = 5
        self.ws = [self.sb(f"wslot{i}", [128, 4096], BF16) for i in range(self.NW)]
        self.w_i = 0
        self.vecs = self.sb("vecs_sb", [128, NVEC], F32)
        self.ones_bf = self.sb("ones_bf", [128, 128], BF16)
        self.ident_f = self.sb("ident_f", [128, 128], F32)
        self.eps_col = self.sb("eps_col", [128, 1], F32)
        self.one_col = self.sb("one_col", [128, 1], F32)
        self.rstd = self.sb("rstd", [128, T], F32)
        self.sq = [self.sb(f"sq{i}", [128, 512], BF16) for i in range(2)]
        self.ones_row_b = self.sb("ones_row_b", [1, 128], BF16)
        self.ones_row_f = self.sb("ones_row_f", [1, 128], F32)
        self.ps = [nc.alloc_psum_tensor(f"psb{i}", [128, 512], F32) for i in range(8)]
        self.t_base = (nc.sbuf_base + 63) // 64 * 64
        self.t_end = nc.sbuf_top
        self.t_off = self.t_base
        self.tmp_n = 0

        R.op("dve", lambda e: e.memset(self.ones_bf[:], 1.0), w=["ones"])
        R.op("dve", lambda e: e.memset(self.eps_col[:], EPS), w=["epsc"])
        R.op("dve", lambda e: e.memset(self.one_col[:], 1.0), w=["onec"])
        R.op("dve", lambda e: e.memset(self.ones_row_b[:], 1.0), w=["onesrb"])
        R.op("dve", lambda e: e.memset(self.ones_row_f[:], 1.0), w=["onesrf"])
        R.dma("sp", self.vecs[:], vecs_d, w=["vecs"], dsem="vecs")
        R.dma("sp", self.ident_f[:], ident_d, w=["ident"], dsem="vecs")
        xv = xT.rearrange("(c p) t -> p c t", p=128)
        for c in range(NCH):
            R.dma("sp", self.xres[:, c, :], xv[:, c, :], w=self.xtoks(c), dsem="xin")

        fns = {0: self.pool_mixer, 1: self.gmlp_mixer, 2: self.ret_mixer, 3: self.lru_mixer}
        for li in range(self.nlayers):
            m = li % 4
            if m in mix:
                self.tmp_reset()
                fns[m]()
            if not NOMLP:
                self.tmp_reset()
                self.mlp(li)

        self.tmp_reset()
        self.norm("final_norm", 0, out_f32_inplace=True)
        yv = yT.rearrange("(c p) t -> p c t", p=128)
        for c in range(NCH):
            R.dma("sp", yv[:, c, :], self.xres[:, c, :], r=self.xtoks(c), w=[f"yout{c}"], dsem="yout")
        R.emit()

    def host_const(self, n, inputs, core):
        f = lambda k: np.asarray(inputs[k], np.float32)
        sl = slice(core * NS, (core + 1) * NS)
        if n == "ident":
            return np.eye(128, dtype=np.float32)
        if n == "spT":
            return np.ascontiguousarray(f("state_pool")[0, sl].transpose(2, 0, 1))
        if n == "rc16":
            t = np.arange(16, dtype=np.float32)
            rc = np.stack([1.0 / np.minimum(t + 1, w) for w in (2, 4, 8, 16)], 0).astype(np.float32)
            return np.ascontiguousarray(np.broadcast_to(rc[None], (128, 4, 16)))
        if n == "sret":
            return np.ascontiguousarray(f("state_ret")[0, sl])
        if n == "permM":
            k = np.arange(128)
            return (k[:, None] == ((k[None, :] + 64) % 128)).astype(np.float32)
        if n in ("kdec", "dmaskT", "qdecb"):
            lg = np.log1p(-np.exp2(-5.0 - np.arange(8, dtype=np.float32))).astype(np.float32)
            idx = np.arange(128, dtype=np.float32)
            if n == "kdec":
                return np.ascontiguousarray(np.exp(lg[None, :] * (127.0 - idx)[:, None]).astype(np.float32))
            if n == "qdecb":
                qd_ = np.exp(lg[:, None] * (idx + 1.0)[None, :]).astype(np.float32)
                return np.ascontiguousarray(np.broadcast_to(qd_[:, None, :], (8, 128, 128)))
            diff = idx[None, :] - idx[:, None]
            dm = np.where(diff[None] >= 0, np.exp(lg[:, None, None] * np.maximum(diff, 0.0)[None]), 0.0)
            return np.ascontiguousarray(dm.astype(np.float32))
        if n in ("ropeC", "ropeS"):
            half = 64
            freqs = np.exp(np.float32(-math.log(10000.0)) * np.arange(half, dtype=np.float32) / np.float32(half)).astype(np.float32)
            pos = np.concatenate([np.arange(SEQ), np.full(NS, PAST)]).astype(np.float32)
            ang = (pos[None, :] * freqs[:, None]).astype(np.float32)
            if n == "ropeC":
                c_ = np.cos(ang).astype(np.float32)
                return np.ascontiguousarray(np.concatenate([c_, c_], 0))
            s_ = np.sin(ang).astype(np.float32)
            return np.ascontiguousarray(np.concatenate([-s_, s_], 0))
        if n == "scT":
            return np.ascontiguousarray(f("state_conv")[0, sl].transpose(2, 1, 0))
        if n == "slT":
            return np.ascontiguousarray(f("state_lru")[0, sl].T)
        if n == "gm_wsT":
            return np.ascontiguousarray(f("gm_w_s")[0].transpose(2, 0, 1))
        if n == "trilT":
            s_ = np.arange(128)
            return (s_[None, :] >= s_[:, None]).astype(np.float32)
        raise KeyError(n)


_CACHE = {}


def get_prog(key=((0, 1, 2, 3), 4)):
    if key not in _CACHE:
        _CACHE[key] = K(*key)
    return _CACHE[key]


def pack_inputs(inputs, core, prog):
    f = lambda n: np.asarray(inputs[n], np.float32)
    m = {}
    xp = f("x_prompt")[core]
    xs = f("x_sample")[core * NS:(core + 1) * NS, 0]
    m["xT"] = np.ascontiguousarray(np.concatenate([xp, xs], axis=0).T)
    vec = np.zeros((128, NVEC), np.float32)
    for n, k in VEC_SPECS:
        if n == "gm_ws00":
            v = np.broadcast_to(f("gm_w_s")[0, :, 0, 0][None, :], (128, 8))
        elif n == "gm_bs0":
            v = np.broadcast_to(f("gm_b_s")[0, :, 0][None, :], (128, 8))
        else:
            v = cols(f(n))
        vec[:, VEC_OFF[n]:VEC_OFF[n] + k] = v
    m["vecs"] = vec
    for n in prog.din:
        if n in m:
            continue
        if n in inputs:
            m[n] = f(n)
        else:
            m[n] = prog.host_const(n, inputs, core)
    return {k: m[k] for k in prog.din}


def unpack(rs, prog):
    o = {}
    g = lambda c, n: np.asarray(rs[c][n])
    C = range(NCORES)
    o["y_prompt"] = np.ascontiguousarray(np.stack([g(c, "yT")[:, :SEQ].T for c in C], 0))
    o["y_sample"] = np.ascontiguousarray(np.concatenate([g(c, "yT")[:, SEQ:].T for c in C], 0)[:, None, :])
    d = prog.dout
    if "poolpT" in d:
        o["pool_prompt"] = np.ascontiguousarray(np.stack([g(c, "poolpT").T for c in C], 0)[None])
        o["pool_sample"] = np.ascontiguousarray(np.concatenate([g(c, "poolsT").transpose(1, 2, 0) for c in C], 0)[None])
    if "gvT" in d:
        o["gmlp_v_sample"] = np.ascontiguousarray(np.concatenate([g(c, "gvT").T for c in C], 0)[None, :, None, :])
    if "retp" in d:
        o["ret_prompt"] = np.ascontiguousarray(np.stack([g(c, "retp") for c in C], 0)[None])
        o["ret_sample"] = np.ascontiguousarray(np.concatenate([g(c, "rets") for c in C], 0)[None])
    if "convpT" in d:
        o["conv_prompt"] = np.ascontiguousarray(np.stack([g(c, "convpT").T for c in C], 0)[None])
        o["conv_sample"] = np.ascontiguousarray(np.concatenate([g(c, "convsT").transpose(2, 1, 0) for c in C], 0)[None])
        o["lru_prompt"] = np.ascontiguousarray(np.stack([g(c, "lrupT").T.reshape(-1) for c in C], 0)[None])
        o["lru_sample"] = np.ascontiguousarray(np.concatenate([g(c, "lrusT").T for c in C], 0)[None])
    return o


ORDER = ["y_prompt", "y_sample", "pool_prompt", "pool_sample", "gmlp_v_sample", "ret_prompt", "ret_sample",
         "conv_prompt", "conv_sample", "lru_prompt", "lru_sample"]


def kernel(**inputs):
    prog = get_prog()
    in_maps = [pack_inputs(inputs, c, prog) for c in range(NCORES)]
    res = run_bass_kernel_spmd(prog.nc, in_maps, core_ids=list(range(NCORES)))
    o = unpack(res.results, prog)
    return tuple(o[k] for k in ORDER)
```

```python
import math
import numpy as np
import concourse.bass as bass
import concourse.mybir as mybir
from concourse.bass_utils import run_bass_kernel_spmd

F32 = mybir.dt.float32
BF16 = mybir.dt.bfloat16
AF = mybir.ActivationFunctionType
ALU = mybir.AluOpType
AX = mybir.AxisListType

NCORES = 8
D = 1024
NCH = 8
SEQ = 2048
NS = 16
T = SEQ + NS
TB = [0, 512, 1024, 1536, 2048, 2064]
NTT = 5
MB = [0, 413, 826, 1239, 1652, 2064]
DFF = 4096
EPS = 1e-6
GN_EPS = 1e-5
PAST = 16384
import os
NOMLP = bool(os.environ.get('KDEBUG_NOMLP'))


class Op:
    __slots__ = ("eng", "fn", "deps", "signal", "sig_idx", "dsem", "dval")

    def __init__(self, eng, fn, dsem=None):
        self.eng = eng
        self.fn = fn
        self.deps = []
        self.signal = False
        self.sig_idx = 0
        self.dsem = dsem
        self.dval = 0


class Rec:
    ENGS = ("pe", "act", "dve", "pool", "sp")

    def __init__(self, nc):
        self.nc = nc
        self.ops = {e: [] for e in self.ENGS}
        self.last_w = {}
        self.readers = {}
        self.dsem_tot = {}
        self.dsem_last = {}
        self.extra = {e: [] for e in self.ENGS}

    def op(self, eng, fn, r=(), w=(), dsem=None):
        o = Op(eng, fn, dsem)
        if eng == "pool" and dsem is None:
            self.extra["pool"] = list(getattr(self, "bar_deps", []))
        deps = {}
        for t in r:
            d = self.last_w.get(t)
            if d is not None:
                deps[id(d)] = d
        for t in w:
            d = self.last_w.get(t)
            if d is not None:
                deps[id(d)] = d
            for d in self.readers.get(t, ()):
                deps[id(d)] = d
        for d in self.extra[eng]:
            deps[id(d)] = d
        self.extra[eng] = []
        o.deps = list(deps.values())
        for d in o.deps:
            if d.dsem is None:
                d.signal = True
        for t in r:
            self.readers.setdefault(t, []).append(o)
        for t in w:
            self.last_w[t] = o
            self.readers[t] = []
        if dsem is not None:
            self.dsem_tot[dsem] = self.dsem_tot.get(dsem, 0) + 16
            o.dval = self.dsem_tot[dsem]
            self.dsem_last[dsem] = o
        self.ops[eng].append(o)
        return o

    def dma(self, q, out, in_, r=(), w=(), dsem=None):
        if q == "pool" and not dsem.startswith("ws"):
            self.extra["pool"] = list(getattr(self, "bar_deps", []))
        return self.op(q, lambda e: e.dma_start(out=out, in_=in_), r=r, w=w, dsem=dsem)

    def barrier(self):
        deps = []
        for e in self.ENGS:
            for o in reversed(self.ops[e]):
                if o.dsem is None:
                    deps.append(o)
                    break
        deps += [o for k, o in self.dsem_last.items() if not k.startswith("ws")]
        for e in self.ENGS:
            if e != "pool":
                self.extra[e] = list(deps)
        self.bar_deps = list(deps)
        self.last_w = {k: v for k, v in self.last_w.items() if k.startswith("ws")}
        self.readers = {k: v for k, v in self.readers.items() if k.startswith("ws")}

    def emit(self):
        nc = self.nc
        for e in self.ENGS:
            n = 0
            for o in self.ops[e]:
                if o.dsem is None and o.signal:
                    n += 1
                    o.sig_idx = n
        esem = {e: nc.alloc_semaphore("es_" + e) for e in self.ENGS}
        dsems = {k: nc.alloc_semaphore("ds_" + k) for k in self.dsem_tot}
        rec = self

        def run(ename, eng):
            waited = {}
            for o in rec.ops[ename]:
                for d in o.deps:
                    if d.dsem is not None:
                        key, sem, val = "d" + d.dsem, dsems[d.dsem], d.dval
                    else:
                        if ename == "pe" and d.eng == "pe":
                            continue
                        key, sem, val = "e" + d.eng, esem[d.eng], d.sig_idx
                    if waited.get(key, 0) >= val:
                        continue
                    eng.wait_ge(sem, val)
                    waited[key] = val
                ins = o.fn(eng)
                if o.dsem is not None:
                    ins.then_inc(dsems[o.dsem], 16)
                elif o.signal:
                    ins.then_inc(esem[ename], 1)
            if ename == "sp":
                for k, tot in rec.dsem_tot.items():
                    eng.wait_ge(dsems[k], tot)

        with nc.Block() as block:
            @block.tensor
            def _(eng):
                run("pe", eng)

            @block.scalar
            def _(eng):
                run("act", eng)

            @block.vector
            def _(eng):
                run("dve", eng)

            @block.gpsimd
            def _(eng):
                run("pool", eng)

            @block.sync
            def _(eng):
                run("sp", eng)


VEC_SPECS = [
    ("pool_norm", 8), ("pool_scale", 8), ("gm_norm", 8), ("gm_b_in", 16),
    ("ret_norm", 8), ("ret_gn_g", 16), ("ret_gn_b", 16), ("lru_norm", 8),
    ("lru_conv_w", 32), ("lru_conv_b", 8), ("lru_b_a", 8), ("lru_b_x", 8), ("lru_lam", 8),
    ("mlp_norm", 32), ("final_norm", 8), ("gm_ln_g", 8), ("gm_ln_b", 8), ("gm_ws00", 8), ("gm_bs0", 8),
]
VEC_OFF = {}
_o = 0
for _n, _k in VEC_SPECS:
    VEC_OFF[_n] = _o
    _o += _k
NVEC = _o


def cols(v):
    v = np.asarray(v, np.float32).reshape(-1, 128)
    return np.ascontiguousarray(v.T)


def tt_of(c0, c1):
    return [i for i in range(NTT) if TB[i] < c1 and TB[i + 1] > c0]


class K:
    def __init__(self, mixers=(0, 1, 2, 3), nlayers=4):
        self.mixers = mixers
        self.nlayers = nlayers
        nc = self.nc = bass.Bass("TRN2", target_bir_lowering=False)
        self.R = Rec(nc)
        self.din = {}
        self.dout = {}
        self.bank_i = 0
        self.hold = set()
        self.TBcur = TB
        self.build()

    def inp(self, name, shape, dt=F32):
        t = self.nc.dram_tensor(name, list(shape), dt, kind="ExternalInput").ap()
        self.din[name] = t
        return t

    def outp(self, name, shape, dt=F32):
        t = self.nc.dram_tensor(name, list(shape), dt, kind="ExternalOutput").ap()
        self.dout[name] = t
        return t

    def sb(self, name, shape, dt):
        return self.nc.alloc_sbuf_tensor(name, list(shape), dt)

    def tmp(self, name, shape, dt):
        nbytes = int(np.prod(shape[1:])) * (2 if dt == BF16 else 4)
        nbytes = (nbytes + 31) // 32 * 32
        off = self.t_off
        self.t_off += nbytes
        assert self.t_off <= self.t_end, (name, self.t_off, self.t_end)
        self.tmp_n += 1
        return self.nc.alloc_sbuf_tensor_at(f"{name}_{self.tmp_n}", list(shape), dt, offset=off)

    def tmp_reset(self):
        self.R.barrier()
        self.t_off = self.t_base

    def bank(self):
        while self.bank_i in self.hold:
            self.bank_i = (self.bank_i + 1) % 8
        b = self.bank_i
        self.bank_i = (self.bank_i + 1) % 8
        return b, self.ps[b], f"ps{b}"

    def wload(self, dram_view, pattern=None, **kw):
        i = self.w_i
        self.w_i = (self.w_i + 1) % self.NW
        slot = self.ws[i]
        n = int(np.prod(dram_view.shape[1:]))
        v = slot[:, 0:n]
        if pattern is not None:
            v = v.rearrange(pattern, **kw)
        self.R.dma("pool", v, dram_view, w=[f"ws{i}"], dsem=f"ws{i}")
        return v, f"ws{i}"

    def wload_parts(self, parts):
        i = self.w_i
        self.w_i = (self.w_i + 1) % self.NW
        slot = self.ws[i]
        views = []
        for dram_view, off, pattern, kw in parts:
            n = int(np.prod(dram_view.shape[1:]))
            v = slot[:, off:off + n]
            if pattern is not None:
                v = v.rearrange(pattern, **kw)
            self.R.dma("pool", v, dram_view, w=[f"ws{i}"], dsem=f"ws{i}")
            views.append(v)
        return views, f"ws{i}"

    def xt(self, c, i):
        return f"x{c}_{i}"

    def ht(self, c, i):
        return f"h{c}_{i}"

    def at(self, c, i):
        return f"a{c}_{i}"

    def vcol(self, name, j):
        o = VEC_OFF[name] + j
        return self.vecs[:, o:o + 1]

    def norm_stats(self):
        for i in range(NTT):
            self._norm_stats_tile(i)

    def _norm_stats_tile(self, i):
        R = self.R
        xres, rstd = self.xres, self.rstd
        c0, c1 = self.TBcur[i], self.TBcur[i + 1]
        n = c1 - c0
        b, ps, pt = self.bank()
        for c in range(NCH):
            sq = self.sq[c % 2]
            sqt = f"sq{c % 2}"
            R.op("act", lambda e, sq=sq, c=c: e.activation(out=sq[:, 0:n], in_=xres[:, c, c0:c1], func=AF.Square),
                 r=[self.xt(c, i)], w=[sqt])
            R.op("pe", lambda e, sq=sq, c=c: e.matmul(ps[:, 0:n], lhsT=self.ones_bf[:], rhs=sq[:, 0:n],
                                                       start=(c == 0), stop=(c == NCH - 1)), r=[sqt], w=[pt])
        R.op("act", lambda e: e.activation(out=rstd[:, c0:c1], in_=ps[:, 0:n], func=AF.Ln,
                                           bias=self.eps_col[:], scale=1.0 / D), r=[pt], w=[f"rstd{i}"])
        R.op("act", lambda e: e.activation(out=rstd[:, c0:c1], in_=rstd[:, c0:c1], func=AF.Exp, scale=-0.5),
             r=[f"rstd{i}"], w=[f"rstd{i}"])

    def norm(self, gname, goff, out_f32_inplace=False):
        self._norm_stats_tile(0)
        for i in range(NTT):
            if i + 1 < NTT:
                self._norm_stats_tile(i + 1)
            self._norm_apply_tile(i, gname, goff, out_f32_inplace)

    def _norm_apply_tile(self, i, gname, goff, out_f32_inplace):
        R = self.R
        xres, H, rstd = self.xres, self.H, self.rstd
        c0, c1 = self.TBcur[i], self.TBcur[i + 1]
        for c in range(NCH):
            g = self.vcol(gname, goff + c)
            if out_f32_inplace:
                R.op("dve", lambda e, c=c, g=g: e.scalar_tensor_tensor(
                    out=xres[:, c, c0:c1], in0=xres[:, c, c0:c1], scalar=g, in1=rstd[:, c0:c1],
                    op0=ALU.mult, op1=ALU.mult), r=[self.xt(c, i), f"rstd{i}"], w=[self.xt(c, i)])
            else:
                R.op("dve", lambda e, c=c, g=g: e.scalar_tensor_tensor(
                    out=H[:, c, c0:c1], in0=xres[:, c, c0:c1], scalar=g, in1=rstd[:, c0:c1],
                    op0=ALU.mult, op1=ALU.mult), r=[self.xt(c, i), f"rstd{i}"], w=[self.ht(c, i)])

    def xtoks(self, c):
        return [self.xt(c, i) for i in range(NTT)]

    def htoks(self, c):
        return [self.ht(c, i) for i in range(NTT)]

    def rstd_toks(self):
        return [f"rstd{i}" for i in range(NTT)]

    def add_to_xres(self, oc, i, ps, pt, scale_col=None):
        xres = self.xres
        c0, c1 = self.TBcur[i], self.TBcur[i + 1]
        n = c1 - c0
        if scale_col is None:
            self.R.op("dve", lambda e: e.tensor_tensor(out=xres[:, oc, c0:c1], in0=xres[:, oc, c0:c1],
                                                       in1=ps[:, 0:n], op=ALU.add),
                      r=[pt, self.xt(oc, i)], w=[self.xt(oc, i)])
        else:
            self.R.op("dve", lambda e: e.scalar_tensor_tensor(out=xres[:, oc, c0:c1], in0=ps[:, 0:n], scalar=scale_col,
                                                              in1=xres[:, oc, c0:c1], op0=ALU.mult, op1=ALU.add),
                      r=[pt, self.xt(oc, i)], w=[self.xt(oc, i)])

    def mlp(self, li):
        R = self.R
        xres, H = self.xres, self.H
        A1 = self.tmp("hid", [128, NCH, T], BF16)
        self.r32 = [self.tmp(f"r32_{i}", [128, 512], F32) for i in range(2)]
        self.TBcur = MB
        self.norm("mlp_norm", li * 8)
        w1 = self.din["mlp_w1"][li].rearrange("(k p) n -> p k n", p=128)
        w2 = self.din["mlp_w2"][li].rearrange("(f p) n -> p f n", p=128)

        def load_w1(p, j):
            return self.wload(w1[:, :, p * 1024 + j * 512: p * 1024 + (j + 1) * 512], "p (k n) -> p k n", k=8)

        def load_w2(p):
            return [self.wload(w2[:, p * 8 + j * 4: p * 8 + (j + 1) * 4, :], "p (f n) -> p f n", f=4)
                    for j in range(2)]

        pre = None
        for p in range(4):
            w1s = [pre if pre is not None else load_w1(p, 0), load_w1(p, 1)]
            w2s = load_w2(p)
            pre = load_w1(p + 1, 0) if p < 3 else None

            def hd(i, w1s=w1s):
                c0, c1 = self.TBcur[i], self.TBcur[i + 1]
                n = c1 - c0
                for f in range(8):
                    wv, wt = w1s[f // 4]
                    b, ps, pt = self.bank()
                    for k in range(NCH):
                        R.op("pe", lambda e, wv=wv, f=f, k=k, ps=ps: e.matmul(
                            ps[:, 0:n], lhsT=wv[:, k, (f % 4) * 128:(f % 4 + 1) * 128], rhs=H[:, k, c0:c1],
                            start=(k == 0), stop=(k == NCH - 1)), r=[wt, self.ht(k, i)], w=[pt])
                    r32 = self.r32[f % 2]
                    rt = f"r32_{f % 2}"
                    R.op("act", lambda e, ps=ps, r32=r32: e.activation(out=r32[:, 0:n], in_=ps[:, 0:n], func=AF.Relu),
                         r=[pt], w=[rt])
                    R.op("dve", lambda e, r32=r32, f=f: e.tensor_tensor(out=A1[:, f, c0:c1], in0=r32[:, 0:n],
                                                                         in1=r32[:, 0:n], op=ALU.mult),
                         r=[rt], w=[self.at(f, i)])

            def out(i, w2s=w2s):
                c0, c1 = self.TBcur[i], self.TBcur[i + 1]
                n = c1 - c0
                for oc in range(NCH):
                    b, ps, pt = self.bank()
                    for f in range(8):
                        wv, wt = w2s[f // 4]
                        R.op("pe", lambda e, wv=wv, f=f, oc=oc, ps=ps: e.matmul(
                            ps[:, 0:n], lhsT=wv[:, f % 4, oc * 128:(oc + 1) * 128], rhs=A1[:, f, c0:c1],
                            start=(f == 0), stop=(f == 7)), r=[wt, self.at(f, i)], w=[pt])
                    self.add_to_xres(oc, i, ps, pt)

            hd(0)
            for i in range(1, NTT):
                hd(i)
                out(i - 1)
            out(NTT - 1)
        self.TBcur = TB

    def pool_mixer(self):
        R = self.R
        xres, H, rstd = self.xres, self.H, self.rstd
        WINS = (2, 4, 8, 16)
        h32 = self.tmp("h32", [128, T], F32)
        pa = self.tmp("pa", [128, SEQ], F32)
        pb = self.tmp("pb", [128, SEQ], F32)
        spT = self.tmp("spT", [128, NCH, NS, 15], F32)
        pso = self.tmp("pso", [128, NCH, NS, 15], F32)
        ppo = self.tmp("ppo", [128, NCH, 15], F32)
        rc = self.tmp("rc", [128, 4, 16], F32)
        ssum = self.tmp("ssum", [128, NS], F32)
        dfix = self.tmp("dfix", [128, 16], F32)
        R.dma("sp", spT[:], self.din["spT"].rearrange("(c p) b r -> p c b r", p=128), w=["spT"], dsem="spT")
        R.dma("sp", rc[:], self.din["rc16"], w=["rc"], dsem="rc")
        wv, wt = self.wload(self.din["pool_w"][0].rearrange("g (j p) e -> p g j e", p=128),
                            "p (g j e) -> p g j e", g=4, j=2)
        self.norm_stats()
        R.op("act", lambda e: e.activation(out=pso[:, :, :, 0:14], in_=spT[:, :, :, 1:15], func=AF.Copy),
             r=["spT"], w=["pso_a"])

        def chunk(c):
            w = WINS[c // 2]
            gi = c // 2
            g = self.vcol("pool_norm", c)
            R.op("dve", lambda e: e.scalar_tensor_tensor(out=h32[:], in0=xres[:, c, :], scalar=g, in1=rstd[:],
                                                         op0=ALU.mult, op1=ALU.mult),
                 r=self.xtoks(c) + self.rstd_toks(), w=["h32"])
            R.op("act", lambda e: e.activation(out=ppo[:, c, :], in_=h32[:, SEQ - 15:SEQ], func=AF.Copy),
                 r=["h32"], w=[f"ppo{c}"])
            R.op("act", lambda e: e.activation(out=pso[:, c, :, 14:15], in_=h32[:, SEQ:T].unsqueeze(2),
                                               func=AF.Copy), r=["h32"], w=[f"pso_b{c}"])
            src, st = h32, "h32"
            bufs = [(pa, "pa"), (pb, "pb")]
            step = 1
            k = 0
            while step < w:
                dst, dt_ = bufs[k % 2]
                R.op("dve", lambda e, src=src, dst=dst, step=step: e.tensor_tensor(
                    out=dst[:, step:SEQ], in0=src[:, step:SEQ], in1=src[:, 0:SEQ - step], op=ALU.add),
                    r=[st, st + "h"], w=[dt_])
                R.op("act", lambda e, src=src, dst=dst, step=step: e.activation(
                    out=dst[:, 0:step], in_=src[:, 0:step], func=AF.Copy), r=[st, st + "h"], w=[dt_ + "h"])
                src, st = dst, dt_
                step *= 2
                k += 1
            rr = [st, st + "h", "h32"]
            R.op("dve", lambda e, src=src: e.scalar_tensor_tensor(out=H[:, c, 0:SEQ], in0=src[:, 0:SEQ], scalar=1.0 / w,
                                                                   in1=h32[:, 0:SEQ], op0=ALU.mult, op1=ALU.subtract),
                 r=rr, w=self.htoks(c)[0:4])
            R.op("dve", lambda e, src=src: e.tensor_tensor(out=dfix[:, 0:w - 1], in0=src[:, 0:w - 1], in1=rc[:, gi, 0:w - 1],
                                                            op=ALU.mult), r=rr + ["rc"], w=["dfix"])
            R.op("dve", lambda e: e.tensor_tensor(out=H[:, c, 0:w - 1], in0=dfix[:, 0:w - 1], in1=h32[:, 0:w - 1],
                                                  op=ALU.subtract), r=["dfix", "h32"], w=[self.ht(c, 0)])
            R.op("dve", lambda e: e.tensor_reduce(out=ssum[:], in_=spT[:, c, :, 15 - (w - 1):15], axis=AX.X, op=ALU.add),
                 r=["spT"], w=["ssum"])
            R.op("dve", lambda e: e.tensor_tensor(out=ssum[:], in0=ssum[:], in1=h32[:, SEQ:T], op=ALU.add),
                 r=["ssum", "h32"], w=["ssum"])
            R.op("dve", lambda e: e.scalar_tensor_tensor(out=H[:, c, SEQ:T], in0=ssum[:], scalar=1.0 / w, in1=h32[:, SEQ:T],
                                                         op0=ALU.mult, op1=ALU.subtract),
                 r=["ssum", "h32"], w=[self.ht(c, 4)])

        for c in range(NCH):
            chunk(c)
        R.dma("sp", self.dout["poolpT"].rearrange("(c p) r -> p c r", p=128), ppo[:],
              r=[f"ppo{c}" for c in range(NCH)], w=["o_poolp"], dsem="o_poolp")
        R.dma("sp", self.dout["poolsT"].rearrange("(c p) b r -> p c b r", p=128), pso[:],
              r=["pso_a"] + [f"pso_b{c}" for c in range(NCH)], w=["o_pools"], dsem="o_pools")

        def proj(i):
            c0, c1 = TB[i], TB[i + 1]
            n = c1 - c0
            for oc in range(NCH):
                gi = oc // 2
                b, ps, pt = self.bank()
                for j in range(2):
                    R.op("pe", lambda e, j=j, ps=ps, oc=oc, gi=gi: e.matmul(
                        ps[:, 0:n], lhsT=wv[:, gi, j, (oc % 2) * 128:(oc % 2 + 1) * 128], rhs=H[:, 2 * gi + j, c0:c1],
                        start=(j == 0), stop=(j == 1)), r=[wt, self.ht(2 * gi + j, i)], w=[pt])
                self.add_to_xres(oc, i, ps, pt, scale_col=self.vcol("pool_scale", oc))

        for i in range(NTT):
            proj(i)

    def gmlp_mixer(self):
        R = self.R
        nc = self.nc
        xres, H = self.xres, self.H
        Cc = self.tmp("Cc", [128, NCH, 128], F32)
        wsm = self.tmp("wsm", [128, NCH, 128], BF16)
        self.binv_row = self.tmp("binv_row", [1, 1024], BF16)
        mark = self.t_off
        bsb = self.tmp("bsb", [128, NCH, 128], F32)
        wsf = self.tmp("wsf", [128, NCH, 128], F32)
        mk = self.tmp("mk", [128, 128], F32)
        self.bs_row = self.tmp("bs_row", [1, 1024], F32)
        R.dma("sp", wsf[:], self.din["gm_wsT"], w=["wsf"], dsem="gmc0")
        R.dma("sp", mk[:], self.din["trilT"], w=["mk"], dsem="gmc1")
        R.dma("sp", self.bs_row[:], self.din["gm_b_s"].rearrange("a g t -> a (g t)"), w=["bs_row"], dsem="gmc2")
        R.dma("pool", self.binv_row[:], self.din["gm_b_in"][0:1, 1024:2048], w=["binv"], dsem="binv")
        R.op("dve", lambda e: e.tensor_tensor(out=wsm[:], in0=wsf[:], in1=mk[:].unsqueeze(1).to_broadcast([128, NCH, 128]),
                                              op=ALU.mult), r=["wsf", "mk"], w=["wsm"])
        for hf in range(2):
            b, ps, pt = self.bank()
            R.op("pe", lambda e, ps=ps, hf=hf: e.matmul(ps[:], lhsT=self.ones_bf[:], rhs=wsm[:, 4 * hf:4 * hf + 4, :],
                                                        start=True, stop=True), r=["wsm"], w=[pt])
            b2, ps2, pt2 = self.bank()
            R.op("pe", lambda e, ps2=ps2, hf=hf: e.matmul(ps2[:], lhsT=self.ones_row_f[0:1, :],
                                                          rhs=self.bs_row[0:1, 512 * hf:512 * hf + 512],
                                                          start=True, stop=True), r=["bs_row"], w=[pt2])
            R.op("act", lambda e, ps2=ps2, hf=hf: e.activation(out=bsb[:, 4 * hf:4 * hf + 4, :], in_=ps2[:], func=AF.Copy),
                 r=[pt2], w=[f"bsb{hf}"])
            for gq in range(4):
                g = 4 * hf + gq
                R.op("dve", lambda e, ps=ps, g=g, gq=gq: e.scalar_tensor_tensor(
                    out=Cc[:, g, :], in0=ps[:, gq * 128:(gq + 1) * 128], scalar=self.vcol("gm_ln_b", g), in1=bsb[:, g, :],
                    op0=ALU.mult, op1=ALU.add), r=[pt, f"bsb{hf}"], w=[f"Cc{g}"])
        R.barrier()
        self.t_off = mark
        A1 = self.tmp("gU", [128, NCH, T], BF16)
        v32 = self.tmp("v32", [128, 1024], F32)
        vh = [self.tmp(f"vh{j}", [128, 1024], BF16) for j in range(2)]
        tmx = [self.tmp("tmx0", [128, 4, 128], F32)] * 2
        st6 = self.tmp("st6", [128, 2, 6], F32)
        mv = self.tmp("mv", [128, 2], F32)
        rsd = self.tmp("rsd", [128, 1], F32)
        vs32 = v32
        vns = self.tmp("vns", [128, NCH, NS], F32)
        tms = self.tmp("tms", [128, NS], F32)
        w_in = self.din["gm_w_in"][0].rearrange("(k p) n -> p k n", p=128)
        w_out = self.din["gm_w_out"][0].rearrange("(k p) n -> p k n", p=128)
        U = [self.wload(w_in[:, :, j * 512:(j + 1) * 512], "p (k n) -> p k n", k=8) for j in range(2)]
        V = [self.wload(w_in[:, :, 1024 + j * 512:1024 + (j + 1) * 512], "p (k n) -> p k n", k=8) for j in range(2)]
        self.norm("gm_norm", 0)

        def uphase(j, i):
            c0, c1 = TB[i], TB[i + 1]
            n = c1 - c0
            wv, wt = U[j]
            for f in range(4):
                fc = 4 * j + f
                b, ps, pt = self.bank()
                for k in range(NCH):
                    R.op("pe", lambda e, f=f, k=k, ps=ps: e.matmul(ps[:, 0:n], lhsT=wv[:, k, f * 128:(f + 1) * 128],
                                                                   rhs=H[:, k, c0:c1], start=(k == 0), stop=(k == NCH - 1)),
                         r=[wt, self.ht(k, i)], w=[pt])
                R.op("act", lambda e, ps=ps, fc=fc: e.activation(out=A1[:, fc, c0:c1], in_=ps[:, 0:n], func=AF.Gelu_apprx_tanh,
                                                                 bias=self.vcol("gm_b_in", fc)), r=[pt], w=[self.at(fc, i)])

        for j in range(2):
            for i in range(NTT):
                uphase(j, i)

        def vtok(c0, m, i):
            banks = []
            for hf in range(2):
                wv, wt = V[hf]
                b, ps, pt = self.bank()
                for k in range(NCH):
                    R.op("pe", lambda e, k=k, ps=ps, wv=wv: e.matmul(ps[0:m, :], lhsT=H[:, k, c0:c0 + m], rhs=wv[:, k, :],
                                                                      start=(k == 0), stop=False),
                         r=[wt, self.ht(k, i)], w=[pt])
                R.op("pe", lambda e, ps=ps, hf=hf: e.matmul(ps[0:m, :], lhsT=self.ones_row_b[0:1, 0:m],
                                                            rhs=self.binv_row[0:1, 512 * hf:512 * hf + 512],
                                                            start=False, stop=True), r=["binv"], w=[pt])
                banks.append((ps, pt))
            return banks

        def lnorm(src, m, srct):
            for hf in range(2):
                R.op("dve", lambda e, hf=hf: e.bn_stats(out=st6[0:m, hf, :], in_=src[0:m, 512 * hf:512 * hf + 512]),
                     r=srct, w=[f"st6{hf}"])
            R.op("dve", lambda e: e.bn_aggr(out=mv[0:m, :], in_=st6[0:m, :, :].rearrange("p a b -> p (a b)")),
                 r=["st60", "st61"], w=["mv"])
            R.op("act", lambda e: e.activation(out=rsd[0:m, :], in_=mv[0:m, 1:2], func=AF.Sqrt, bias=self.eps_col[0:m, :],
                                               scale=1.0), r=["mv"], w=["rsd"])
            R.op("dve", lambda e: e.reciprocal(out=rsd[0:m, :], in_=rsd[0:m, :]), r=["rsd"], w=["rsd"])

        def vchunk(n_):
            c0 = 128 * n_
            i = c0 // 512
            banks = vtok(c0, 128, i)
            for hf, (ps, pt) in enumerate(banks):
                R.op("act", lambda e, ps=ps, hf=hf: e.activation(out=v32[:, 512 * hf:512 * hf + 512], in_=ps[:],
                                                                 func=AF.Gelu_apprx_tanh), r=[pt], w=[f"v32{hf}"])
            lnorm(v32, 128, ["v320", "v321"])
            vhb = vh[n_ % 2]
            vht = f"vh{n_ % 2}"
            R.op("dve", lambda e: e.tensor_scalar(out=vhb[:], in0=v32[:], scalar1=mv[:, 0:1], scalar2=rsd[:, 0:1],
                                                  op0=ALU.subtract, op1=ALU.mult), r=["v320", "v321", "mv", "rsd"], w=[vht])
            for hf in range(2):
                b, ps, pt = self.bank()
                for gq in range(4):
                    g = 4 * hf + gq
                    R.op("pe", lambda e, ps=ps, g=g, gq=gq: e.matmul(ps[:, gq * 128:(gq + 1) * 128],
                                                                      lhsT=vhb[:, g * 128:(g + 1) * 128], rhs=wsm[:, g, :],
                                                                      start=True, stop=True), r=[vht, "wsm"], w=[pt])
                tm = tmx[hf]
                tmt = "tmx0"
                for gq in range(4):
                    g = 4 * hf + gq
                    R.op("dve", lambda e, ps=ps, g=g, gq=gq, tm=tm: e.scalar_tensor_tensor(
                        out=tm[:, gq, :], in0=ps[:, gq * 128:(gq + 1) * 128], scalar=self.vcol("gm_ln_g", g), in1=Cc[:, g, :],
                        op0=ALU.mult, op1=ALU.add), r=[pt, f"Cc{g}"], w=[tmt + f"_{gq}"])
                R.op("dve", lambda e, tm=tm, hf=hf: e.tensor_tensor(out=A1[:, 4 * hf:4 * hf + 4, c0:c0 + 128], in0=tm[:],
                                                                      in1=A1[:, 4 * hf:4 * hf + 4, c0:c0 + 128], op=ALU.mult),
                     r=[tmt + f"_{gq}" for gq in range(4)] + [self.at(4 * hf + gq, i) for gq in range(4)],
                     w=[self.at(4 * hf + gq, i) for gq in range(4)])

        for n_ in range(16):
            vchunk(n_)

        banks = vtok(SEQ, NS, 4)
        for hf, (ps, pt) in enumerate(banks):
            R.op("act", lambda e, ps=ps, hf=hf: e.activation(out=vs32[0:NS, 512 * hf:512 * hf + 512], in_=ps[0:NS, :],
                                                             func=AF.Gelu_apprx_tanh), r=[pt], w=[f"v32{hf}"])
        lnorm(vs32, NS, ["v320", "v321"])
        R.op("dve", lambda e: e.tensor_scalar(out=vs32[0:NS, :], in0=vs32[0:NS, :], scalar1=mv[0:NS, 0:1], scalar2=rsd[0:NS, 0:1],
                                              op0=ALU.subtract, op1=ALU.mult), r=["v320", "v321", "mv", "rsd"], w=["v320", "v321"])
        for g in range(NCH):
            b, ps, pt = self.bank()
            R.op("pe", lambda e, ps=ps, g=g: e.matmul(ps[:, 0:NS], lhsT=vs32[0:NS, g * 128:(g + 1) * 128],
                                                      rhs=self.ident_f[0:NS, 0:NS], start=True, stop=True),
                 r=["v320", "v321", "ident"], w=[pt])
            R.op("dve", lambda e, ps=ps, g=g: e.tensor_scalar(out=vns[:, g, :], in0=ps[:, 0:NS],
                                                              scalar1=self.vcol("gm_ln_g", g), scalar2=self.vcol("gm_ln_b", g),
                                                              op0=ALU.mult, op1=ALU.add), r=[pt], w=[f"vns{g}"])
            R.op("dve", lambda e, g=g: e.tensor_scalar(out=tms[:], in0=vns[:, g, :], scalar1=self.vcol("gm_ws00", g),
                                                       scalar2=self.vcol("gm_bs0", g), op0=ALU.mult, op1=ALU.add),
                 r=[f"vns{g}"], w=["tms"])
            R.op("dve", lambda e, g=g: e.tensor_tensor(out=A1[:, g, SEQ:T], in0=tms[:], in1=A1[:, g, SEQ:T], op=ALU.mult),
                 r=["tms", self.at(g, 4)], w=[self.at(g, 4)])
        R.dma("sp", self.dout["gvT"].rearrange("(c p) b -> p c b", p=128), vns[:],
              r=[f"vns{g}" for g in range(NCH)], w=["o_gv"], dsem="o_gv")

        O = [self.wload(w_out[:, 4 * j:4 * j + 4, :], "p (k n) -> p k n", k=4) for j in range(2)]

        def outp(i):
            c0, c1 = TB[i], TB[i + 1]
            n = c1 - c0
            for oc in range(NCH):
                b, ps, pt = self.bank()
                for k in range(NCH):
                    wv, wt = O[k // 4]
                    R.op("pe", lambda e, k=k, ps=ps, wv=wv, oc=oc: e.matmul(ps[:, 0:n], lhsT=wv[:, k % 4, oc * 128:(oc + 1) * 128],
                                                                      rhs=A1[:, k, c0:c1], start=(k == 0), stop=(k == NCH - 1)),
                         r=[wt, self.at(k, i)], w=[pt])
                self.add_to_xres(oc, i, ps, pt)

        for i in range(NTT):
            outp(i)

    def lru_mixer(self):
        R = self.R
        xres, H = self.xres, self.H
        XO = 3
        xbp = self.tmp("xbp", [128, XO + SEQ], F32)
        xbs = self.tmp("xbs", [128, NS], F32)
        xc32 = self.tmp("xc32", [128, T], F32)
        xcb = self.tmp("xcb", [128, T], BF16)
        gate = self.tmp("gate", [128, T], BF16)
        R1 = self.tmp("R1", [128, T], F32)
        I1 = self.tmp("I1", [128, T], F32)
        T2 = self.tmp("T2", [128, T], F32)
        Gc = xcb
        scT = self.tmp("scT", [128, NCH, 3, NS], F32)
        slT = self.tmp("slT", [128, NCH, NS], F32)
        convp = self.tmp("convp", [128, NCH, 3], F32)
        convs = self.tmp("convs", [128, NCH, 3, NS], F32)
        lrup = self.tmp("lrup", [128, NCH], F32)
        lrus = self.tmp("lrus", [128, NCH, NS], F32)
        nsp8 = self.tmp("nsp8", [128, NCH], F32)
        tsm = self.tmp("tsm", [128, NS], F32)
        R.dma("sp", scT[:], self.din["scT"].rearrange("(c p) j b -> p c j b", p=128), w=["scT"], dsem="scT")
        R.dma("sp", slT[:], self.din["slT"].rearrange("(c p) b -> p c b", p=128), w=["slT"], dsem="slT")
        lam = self.vecs[:, VEC_OFF["lru_lam"]:VEC_OFF["lru_lam"] + 8]
        R.op("act", lambda e: e.activation(out=nsp8[:], in_=lam, func=AF.Exp, scale=-1.0), w=["nsp8"])
        R.op("act", lambda e: e.activation(out=nsp8[:], in_=nsp8[:], func=AF.Ln, bias=self.one_col[:], scale=1.0),
             r=["nsp8"], w=["nsp8"])
        R.op("dve", lambda e: e.tensor_scalar(out=nsp8[:], in0=nsp8[:], scalar1=-8.0, scalar2=None, op0=ALU.mult),
             r=["nsp8"], w=["nsp8"])
        nsp4 = self.tmp("nsp4", [128, NCH], F32)
        hba = self.tmp("hba", [128, NCH], F32)
        hbx = self.tmp("hbx", [128, NCH], F32)
        qcol = self.tmp("qcol", [128, 1], F32)
        R.op("dve", lambda e: e.tensor_scalar(out=nsp4[:], in0=nsp8[:], scalar1=0.5, scalar2=None, op0=ALU.mult), r=["nsp8"], w=["nsp4"])
        ba_ = self.vecs[:, VEC_OFF["lru_b_a"]:VEC_OFF["lru_b_a"] + 8]
        bx_ = self.vecs[:, VEC_OFF["lru_b_x"]:VEC_OFF["lru_b_x"] + 8]
        R.op("dve", lambda e: e.tensor_scalar(out=hba[:], in0=ba_, scalar1=0.5, scalar2=None, op0=ALU.mult), w=["hba"])
        R.op("dve", lambda e: e.tensor_scalar(out=hbx[:], in0=bx_, scalar1=0.5, scalar2=None, op0=ALU.mult), w=["hbx"])
        R.op("dve", lambda e: e.memset(qcol[:], 0.25), w=["qcol"])
        R.op("dve", lambda e: e.memset(xbp[:, 0:XO], 0.0), w=["xbp_h"])
        R.op("act", lambda e: e.activation(out=convs[:, :, 0:2, :], in_=scT[:, :, 1:3, :], func=AF.Copy), r=["scT"], w=["convs_a"])
        self.norm("lru_norm", 0)
        w_in = self.din["lru_w_in"][0].rearrange("(k p) n -> p k n", p=128)
        w_out = self.din["lru_w_out"][0].rearrange("(k p) n -> p k n", p=128)
        wa_d = self.din["lru_w_a"][0].rearrange("h i j -> i h j")
        wx_d = self.din["lru_w_x"][0].rearrange("h i j -> i h j")
        Wg = {}
        Wx_ = {}
        Wg[0] = self.wload(w_in[:, :, 0:512], "p (k n) -> p k n", k=8)
        Wx_[0] = self.wload(w_in[:, :, 1024:1536], "p (k n) -> p k n", k=8)
        (wa, wxx), wat = self.wload_parts([(wa_d, 0, "p (h j) -> p h j", dict(h=8)),
                                          (wx_d, 1024, "p (h j) -> p h j", dict(h=8))])
        Wo = [self.wload(w_out[:, 4 * j:4 * j + 4, :], "p (k n) -> p k n", k=4) for j in range(2)]

        def P(c):
            hh, cc = c // 4, c % 4
            return dict(wg=Wg[hh][0], wgt=Wg[hh][1], wxv=Wx_[hh][0], wxt=Wx_[hh][1], cc=cc,
                        cw=[self.vcol("lru_conv_w", j * 8 + c) for j in range(4)], cb=self.vcol("lru_conv_b", c),
                        wo=Wo[c // 4][0], wot=Wo[c // 4][1])

        def S0(c, i):
            q = P(c)
            wg, wgt, wxv, wxt, cc = q["wg"], q["wgt"], q["wxv"], q["wxt"], q["cc"]
            c0, c1 = TB[i], TB[i + 1]
            n = c1 - c0
            b, ps, pt = self.bank()
            for k in range(NCH):
                R.op("pe", lambda e, k=k, ps=ps: e.matmul(ps[:, 0:n], lhsT=wg[:, k, cc * 128:(cc + 1) * 128], rhs=H[:, k, c0:c1],
                                                          start=(k == 0), stop=(k == NCH - 1)), r=[wgt, self.ht(k, i)], w=[pt])
            R.op("act", lambda e, ps=ps: e.activation(out=gate[:, c0:c1], in_=ps[:, 0:n], func=AF.Gelu_apprx_tanh),
                 r=[pt], w=[f"gate{i}"])
            b, ps, pt = self.bank()
            for k in range(NCH):
                R.op("pe", lambda e, k=k, ps=ps: e.matmul(ps[:, 0:n], lhsT=wxv[:, k, cc * 128:(cc + 1) * 128], rhs=H[:, k, c0:c1],
                                                          start=(k == 0), stop=(k == NCH - 1)), r=[wxt, self.ht(k, i)], w=[pt])
            if i < 4:
                R.op("act", lambda e, ps=ps: e.activation(out=xbp[:, XO + c0:XO + c1], in_=ps[:, 0:n], func=AF.Copy),
                     r=[pt], w=[f"xbp{i}"])
            else:
                R.op("act", lambda e, ps=ps: e.activation(out=xbs[:], in_=ps[:, 0:n], func=AF.Copy), r=[pt], w=["xbs"])

        def S1(c, i):
            q = P(c)
            cw, cb = q["cw"], q["cb"]
            c0, c1 = TB[i], TB[i + 1]
            if i < 4:
                prev = f"xbp{i - 1}" if i > 0 else "xbp_h"
                R.op("act", lambda e: e.activation(out=xc32[:, c0:c1], in_=xbp[:, XO + c0:XO + c1], func=AF.Identity, scale=cw[3], bias=cb),
                     r=[f"xbp{i}"], w=[f"xc{i}"])
                for s_ in (1, 2, 3):
                    R.op("dve", lambda e, s_=s_: e.scalar_tensor_tensor(out=xc32[:, c0:c1], in0=xbp[:, XO + c0 - s_:XO + c1 - s_],
                                                                         scalar=cw[3 - s_], in1=xc32[:, c0:c1], op0=ALU.mult, op1=ALU.add),
                         r=[f"xbp{i}", prev, f"xc{i}"], w=[f"xc{i}"])
                if i == 3:
                    R.op("act", lambda e: e.activation(out=convp[:, c, :], in_=xbp[:, XO + SEQ - 3:XO + SEQ], func=AF.Copy),
                         r=["xbp3"], w=[f"convp{c}"])
            else:
                R.op("act", lambda e: e.activation(out=convs[:, c, 2, :], in_=xbs[:], func=AF.Copy), r=["xbs"], w=[f"convs_b{c}"])
                R.op("act", lambda e: e.activation(out=xc32[:, c0:c1], in_=xbs[:], func=AF.Identity, scale=cw[3], bias=cb),
                     r=["xbs"], w=[f"xc{i}"])
                for j in range(3):
                    R.op("dve", lambda e, j=j: e.scalar_tensor_tensor(out=xc32[:, c0:c1], in0=scT[:, c, j, :], scalar=cw[j],
                                                                       in1=xc32[:, c0:c1], op0=ALU.mult, op1=ALU.add),
                         r=["scT", f"xc{i}"], w=[f"xc{i}"])
            R.op("act", lambda e: e.activation(out=xcb[:, c0:c1], in_=xc32[:, c0:c1], func=AF.Copy), r=[f"xc{i}"], w=[f"xcb{i}"])

        def S2(c, i):
            c0, c1 = TB[i], TB[i + 1]
            n = c1 - c0
            b, ps, pt = self.bank()
            R.op("pe", lambda e, ps=ps: e.matmul(ps[:, 0:n], lhsT=wa[:, c, :], rhs=xcb[:, c0:c1], start=True, stop=True),
                 r=[wat, f"xcb{i}"], w=[pt])
            R.op("act", lambda e, ps=ps: e.activation(out=R1[:, c0:c1], in_=ps[:, 0:n], func=AF.Tanh, scale=0.5,
                                                      bias=hba[:, c:c + 1]), r=[pt, "hba"], w=[f"R1_{i}"])
            b, ps, pt = self.bank()
            R.op("pe", lambda e, ps=ps: e.matmul(ps[:, 0:n], lhsT=wxx[:, c, :], rhs=xcb[:, c0:c1], start=True, stop=True),
                 r=[wat, f"xcb{i}"], w=[pt])
            R.op("act", lambda e, ps=ps: e.activation(out=I1[:, c0:c1], in_=ps[:, 0:n], func=AF.Tanh, scale=0.5,
                                                      bias=hbx[:, c:c + 1]), r=[pt, "hbx"], w=[f"I1_{i}"])
            R.op("act", lambda e: e.activation(out=R1[:, c0:c1], in_=R1[:, c0:c1], func=AF.Exp, scale=nsp4[:, c:c + 1],
                                               bias=nsp4[:, c:c + 1]), r=[f"R1_{i}", "nsp4"], w=[f"R1_{i}"])
            R.op("act", lambda e: e.activation(out=T2[:, c0:c1], in_=R1[:, c0:c1], func=AF.Square), r=[f"R1_{i}"], w=[f"T2_{i}"])

        def S2b(c, i):
            c0, c1 = TB[i], TB[i + 1]
            R.op("act", lambda e: e.activation(out=T2[:, c0:c1], in_=T2[:, c0:c1], func=AF.Sqrt, bias=qcol[:], scale=-0.25),
                 r=[f"T2_{i}", "qcol"], w=[f"T2_{i}"])

        def S3(c, i):
            c0, c1 = TB[i], TB[i + 1]
            if i == 0:
                R.op("dve", lambda e: e.memset(T2[:, 0:1], 0.5), r=["T2_0"], w=["T2_0"])
            R.op("dve", lambda e: e.scalar_tensor_tensor(out=I1[:, c0:c1], in0=I1[:, c0:c1], scalar=1.0, in1=xc32[:, c0:c1],
                                                         op0=ALU.add, op1=ALU.mult),
                 r=[f"I1_{i}", f"xc{i}"], w=[f"I1_{i}"])
            R.op("dve", lambda e: e.tensor_tensor(out=I1[:, c0:c1], in0=I1[:, c0:c1], in1=T2[:, c0:c1], op=ALU.mult),
                 r=[f"I1_{i}", f"T2_{i}"], w=[f"I1_{i}"])
            if i < 4:
                init = 0.0 if i == 0 else T2[:, c0 - 1:c0]
                R.op("dve", lambda e: e.tensor_tensor_scan(out=T2[:, c0:c1], data0=R1[:, c0:c1], data1=I1[:, c0:c1], initial=init,
                                                           op0=ALU.mult, op1=ALU.add),
                     r=[f"R1_{i}", f"I1_{i}", f"T2_{i}"] + ([f"T2_{i - 1}"] if i > 0 else []), w=[f"T2_{i}"])
                if i == 3:
                    R.op("act", lambda e: e.activation(out=lrup[:, c:c + 1], in_=T2[:, SEQ - 1:SEQ], func=AF.Copy), r=["T2_3"], w=[f"lrup{c}"])
            else:
                R.op("dve", lambda e: e.tensor_tensor(out=tsm[:], in0=R1[:, c0:c1], in1=slT[:, c, :], op=ALU.mult),
                     r=[f"R1_{i}", "slT"], w=["tsm"])
                R.op("dve", lambda e: e.tensor_tensor(out=T2[:, c0:c1], in0=I1[:, c0:c1], in1=tsm[:], op=ALU.add),
                     r=[f"I1_{i}", "tsm", f"T2_{i}"], w=[f"T2_{i}"])
                R.op("act", lambda e: e.activation(out=lrus[:, c, :], in_=T2[:, c0:c1], func=AF.Copy), r=[f"T2_{i}"], w=[f"lrus{c}"])
            R.op("dve", lambda e: e.tensor_tensor(out=Gc[:, c0:c1], in0=T2[:, c0:c1], in1=gate[:, c0:c1], op=ALU.mult),
                 r=[f"T2_{i}", f"gate{i}", f"xcb{i}"], w=[f"xcb{i}"])

        def S4(c, i):
            q = P(c)
            wo, wot = q["wo"], q["wot"]
            c0, c1 = TB[i], TB[i + 1]
            n = c1 - c0
            for oc in range(NCH):
                b, ps, pt = self.bank()
                R.op("pe", lambda e, ps=ps, oc=oc: e.matmul(ps[:, 0:n], lhsT=wo[:, c % 4, oc * 128:(oc + 1) * 128], rhs=Gc[:, c0:c1],
                                                            start=True, stop=True), r=[wot, f"xcb{i}"], w=[pt])
                self.add_to_xres(oc, i, ps, pt)

        items = [(c, i) for c in range(NCH) for i in range(NTT)]
        stages = [S0, S1, S2, S2b, S3, S4]
        loaded1 = False
        for t in range(len(items) + len(stages) - 1):
            for si, st in enumerate(stages):
                k = t - si
                if 0 <= k < len(items):
                    c, i = items[k]
                    if si == 0 and c == 4 and not loaded1:
                        Wg[1] = self.wload(w_in[:, :, 512:1024], "p (k n) -> p k n", k=8)
                        Wx_[1] = self.wload(w_in[:, :, 1536:2048], "p (k n) -> p k n", k=8)
                        loaded1 = True
                    st(c, i)
        R.dma("sp", self.dout["convpT"].rearrange("(c p) j -> p c j", p=128), convp[:],
              r=[f"convp{c}" for c in range(NCH)], w=["o_convp"], dsem="o_convp")
        R.dma("sp", self.dout["convsT"].rearrange("(c p) j b -> p c j b", p=128), convs[:],
              r=["convs_a"] + [f"convs_b{c}" for c in range(NCH)], w=["o_convs"], dsem="o_convs")
        R.dma("sp", self.dout["lrupT"], lrup[:],
              r=[f"lrup{c}" for c in range(NCH)], w=["o_lrup"], dsem="o_lrup")
        R.dma("sp", self.dout["lrusT"].rearrange("(c p) b -> p c b", p=128), lrus[:],
              r=[f"lrus{c}" for c in range(NCH)], w=["o_lrus"], dsem="o_lrus")

    def ret_mixer(self):
        R = self.R
        xres, H = self.xres, self.H
        DKS = 128 ** -0.5
        lg = np.log1p(-np.exp2(-5.0 - np.arange(8, dtype=np.float32))).astype(np.float32)
        cdec = [float(np.exp(lg[h] * np.float32(128.0))) for h in range(8)]
        gam = [float(np.exp(lg[h] * np.float32(1.0))) for h in range(8)]
        tmp = self.tmp
        G2 = tmp("G2", [128, 4, T], BF16)
        qs = tmp("qs", [128, 512], BF16)
        ks = tmp("ks", [128, 512], BF16)
        tA = tmp("tA", [128, 512], F32)
        tB = tmp("tB", [128, 512], F32)
        qT = [tmp(f"qT{p}", [128, 512], BF16) for p in range(2)]
        qd = [tmp(f"qd{p}", [128, 512], BF16) for p in range(2)]
        kT = [tmp(f"kT{p}", [128, 512], BF16) for p in range(2)]
        kdt = [tmp(f"kdt{p}", [128, 4, 128], BF16) for p in range(2)]
        vtk = [tmp(f"vtk{p}", [128, 4, 256], BF16) for p in range(2)]
        gs = [tmp(f"gs{p}", [128, 2, 512], BF16) for p in range(2)]
        sc = tmp("sc", [128, 4, 128], BF16)
        ob = tmp("ob", [128, 2, 512], BF16)
        osq = tmp("osq", [128, 2, 512], BF16)
        mu = tmp("mu", [128, 512], F32)
        rs = tB
        tn = tA
        S32 = tmp("S32", [128, 256], F32)
        Sbf = [tmp(f"Sbf{j}", [128, 256], BF16) for j in range(5)]
        dmk = tmp("dmk", [128, 128], F32)
        qdc = tmp("qdc", [128, 128], F32)
        kdc = tmp("kdc", [128, 8], F32)
        gne = tmp("gne", [128, 1], F32)
        permM = tmp("permM", [128, 128], BF16)
        identb = tmp("identb", [128, 128], BF16)
        Sst = [tmp(f"Sst{j}", [128, 2, 256], F32) for j in range(2)]
        Km = tmp("Km", [NS, 8, 128], BF16)
        ktk_s = tmp("ktk_s", [NS, 128], BF16)
        vtk_s = tmp("vtk_s", [NS, 256], BF16)
        qs32 = tmp("qs32", [128, NS], F32)
        ks32 = tmp("ks32", [128, NS], F32)
        qds32 = tmp("qds32", [128, NS], F32)
        prodb = tmp("prodb", [128, NS], BF16)
        dots = tmp("dots", [128, NS], F32)
        vTs = tmp("vTs", [128, 2, NS], F32)
        os_ = tmp("os_", [128, 2, NS], F32)
        gss = tmp("gss", [128, 2, NS], BF16)
        Cd = tmp("Cd", [128, 128], BF16)
        Co = tmp("Co", [128, 128], BF16)
        crs = tmp("crs", [128, 2, NS], F32)
        R.dma("pool", permM[:], self.din["permM"], w=["permM"], dsem="permM")
        R.dma("pool", identb[:], self.din["ident"], w=["identb"], dsem="identb")
        R.dma("pool", Cd[:], self.din["cmat"][0], w=["cmat"], dsem="cmat0")
        R.dma("pool", Co[:], self.din["cmat"][1], w=["cmat"], dsem="cmat1")
        R.dma("sp", kdc[:], self.din["kdec"], w=["kdc"], dsem="kdc")
        R.op("dve", lambda e: e.memset(gne[:], GN_EPS), w=["gne"])
        self.norm("ret_norm", 0)
        R.barrier()
        ctab = [self.rstd[:, 0:512], self.rstd[:, 1024:1536]]
        stab = [self.rstd[:, 512:1024], self.rstd[:, 1536:2048]]
        w_in = self.din["ret_w_in"][0].rearrange("(k p) n -> p k n", p=128)
        w_out = self.din["ret_w_out"][0].rearrange("(f p) n -> p f n", p=128)
        GRP = [(0, 512), (512, 1024), (1024, 1536), (1536, 2048), (2048, 2064)]
        self._rope_i = 0

        def head(hh, Wo):
            i = self.w_i
            self.w_i = (self.w_i + 1) % self.NW
            slot = self.ws[i]
            WA = slot[:, 0:4096].rearrange("p (k n) -> p k n", k=8)
            wat = f"ws{i}"
            for (lo, hi, src0) in ((0, 128, 128 * hh), (128, 256, 1024 + 128 * hh), (256, 512, 2048 + 256 * hh)):
                R.dma("pool", WA[:, :, lo:hi], w_in[:, :, src0:src0 + (hi - lo)], w=[wat], dsem=wat)
            WB, wbt = self.wload(w_in[:, :, 4096 + 256 * hh:4096 + 256 * hh + 256], "p (k n) -> p k n", k=8)
            R.dma("sp", dmk[:], self.din["dmaskT"][hh], w=["dmk"], dsem="dmk")
            R.dma("sp", qdc[:], self.din["qdecb"][hh], w=["qdc"], dsem="qdc")
            R.op("dve", lambda e: e.memset(S32[:], 0.0), w=["S32"])
            R.op("dve", lambda e: e.memset(Sbf[4][:], 0.0), w=["Sbf4"])
            g2s = (hh % 2) * 2

            def rope(src_bf, srct, psP, pt, n, j, out_ap, outt):
                R.op("dve", lambda e: e.tensor_tensor(out=tA[:, 0:n], in0=src_bf[:, 0:n], in1=ctab[j][:, 0:n], op=ALU.mult),
                     r=[srct, f"ctab{j}"], w=["tA"])
                R.op("dve", lambda e: e.tensor_tensor(out=tB[:, 0:n], in0=psP[:, 0:n], in1=stab[j][:, 0:n], op=ALU.mult),
                     r=[pt, f"stab{j}"], w=["tB"])
                R.op("dve", lambda e: e.tensor_tensor(out=out_ap, in0=tA[:, 0:n], in1=tB[:, 0:n], op=ALU.add),
                     r=["tA", "tB"], w=[outt])

            def qk_proj(col_lo, c0, c1, i, dst_s, dst_t, scale):
                n = c1 - c0
                b, ps, pt = self.bank()
                for k in range(NCH):
                    R.op("pe", lambda e, k=k, ps=ps: e.matmul(ps[:, 0:n], lhsT=WA[:, k, col_lo:col_lo + 128], rhs=H[:, k, c0:c1],
                                                              start=(k == 0), stop=(k == NCH - 1)), r=[wat, self.ht(k, i)], w=[pt])
                R.op("act", lambda e, ps=ps: e.activation(out=dst_s[:, 0:n], in_=ps[:, 0:n], func=AF.Copy, scale=scale),
                     r=[pt], w=[dst_t])

            def perm(dst_s, dst_t, n):
                b, ps2, pt2 = self.bank()
                R.op("pe", lambda e, ps2=ps2: e.matmul(ps2[:, 0:n], lhsT=permM[:], rhs=dst_s[:, 0:n], start=True, stop=True),
                     r=["permM", dst_t], w=[pt2])
                return ps2, pt2

            def gnorm(src, srct, n, cols0, i, p):
                for vc in range(2):
                    R.op("act", lambda e, vc=vc: e.activation(out=ob[:, vc, 0:n], in_=src[vc], func=AF.Copy), r=[srct[vc]], w=[f"ob{vc}"])
                pc = []
                for vc in range(2):
                    b, pcv, pct = self.bank()
                    for v2 in range(2):
                        cm = Cd if v2 == vc else Co
                        R.op("pe", lambda e, pcv=pcv, v2=v2, cm=cm: e.matmul(pcv[:, 0:n], lhsT=cm[:], rhs=ob[:, v2, 0:n],
                                                                             start=(v2 == 0), stop=(v2 == 1)), r=[f"ob{v2}", "cmat"], w=[pct])
                    pc.append((pcv, pct))
                    R.op("act", lambda e, vc=vc, pcv=pcv: e.activation(out=osq[:, vc, 0:n], in_=pcv[:, 0:n], func=AF.Square), r=[pct], w=[f"osq{vc}"])
                b, p2, p2t = self.bank()
                for vc in range(2):
                    R.op("pe", lambda e, vc=vc: e.matmul(p2[:, 0:n], lhsT=self.ones_bf[:], rhs=osq[:, vc, 0:n], start=(vc == 0), stop=(vc == 1)),
                         r=[f"osq{vc}"], w=[p2t])
                R.op("act", lambda e: e.activation(out=rs[:, 0:n], in_=p2[:, 0:n], func=AF.Ln, bias=gne[:], scale=1.0 / 256),
                     r=[p2t, "gne"], w=["tB"])
                R.op("act", lambda e: e.activation(out=rs[:, 0:n], in_=rs[:, 0:n], func=AF.Exp, scale=-0.5), r=["tB"], w=["tB"])
                tbuf = [(tA, "tA"), (mu, "mu")]
                for vc in range(2):
                    tn_, tnt = tbuf[vc]
                    pcv, pct = pc[vc]
                    R.op("dve", lambda e, tn_=tn_, pcv=pcv: e.tensor_tensor(out=tn_[:, 0:n], in0=pcv[:, 0:n], in1=rs[:, 0:n], op=ALU.mult),
                         r=[pct, "tB"], w=[tnt])
                    R.op("act", lambda e, vc=vc, tn_=tn_: e.activation(out=tn_[:, 0:n], in_=tn_[:, 0:n], func=AF.Identity,
                                                                       scale=self.vcol("ret_gn_g", 2 * hh + vc),
                                                                       bias=self.vcol("ret_gn_b", 2 * hh + vc)), r=[tnt], w=[tnt])
                    gsrc = gss[:, vc, :] if p < 0 else gs[p][:, vc, 0:n]
                    gtok = f"gss{vc}" if p < 0 else f"gs{p}_{vc}"
                    R.op("dve", lambda e, vc=vc, gsrc=gsrc, tn_=tn_: e.tensor_tensor(out=G2[:, g2s + vc, cols0:cols0 + n], in0=tn_[:, 0:n],
                                                                                     in1=gsrc, op=ALU.mult),
                         r=[tnt, gtok], w=[f"g2_{g2s + vc}_{i}"])

            def P1(gi):
                c0, c1 = GRP[gi]
                n = c1 - c0
                i = gi
                p = gi % 2
                samp = (gi == 4)
                j = self._rope_i % 2
                self._rope_i += 1
                R.dma("sp", ctab[j][:, 0:n], self.din["ropeC"][:, c0:c1], w=[f"ctab{j}"], dsem=f"ctab{j}")
                R.dma("sp", stab[j][:, 0:n], self.din["ropeS"][:, c0:c1], w=[f"stab{j}"], dsem=f"stab{j}")
                qk_proj(0, c0, c1, i, qs, "qs", 1.0)
                qk_proj(128, c0, c1, i, ks, "ks", DKS)
                for vc in range(2):
                    b, ps, pt = self.bank()
                    for k in range(NCH):
                        R.op("pe", lambda e, k=k, ps=ps, vc=vc: e.matmul(ps[:, 0:n], lhsT=WB[:, k, vc * 128:(vc + 1) * 128], rhs=H[:, k, c0:c1],
                                                                         start=(k == 0), stop=(k == NCH - 1)), r=[wbt, self.ht(k, i)], w=[pt])
                    if samp:
                        R.op("act", lambda e, ps=ps, vc=vc: e.activation(out=gss[:, vc, :], in_=ps[:, 0:n], func=AF.Silu), r=[pt], w=[f"gss{vc}"])
                    else:
                        R.op("act", lambda e, ps=ps, vc=vc: e.activation(out=gs[p][:, vc, 0:n], in_=ps[:, 0:n], func=AF.Silu), r=[pt], w=[f"gs{p}_{vc}"])
                if not samp:
                    for half in range(2):
                        b, ps, pt = self.bank()
                        for jq in range(2):
                            jj = 2 * half + jq
                            for k in range(NCH):
                                R.op("pe", lambda e, k=k, ps=ps, jj=jj, jq=jq: e.matmul(
                                    ps[:, jq * 256:(jq + 1) * 256], lhsT=H[:, k, c0 + jj * 128:c0 + (jj + 1) * 128], rhs=WA[:, k, 256:512],
                                    start=(k == 0), stop=(k == NCH - 1)), r=[wat, self.ht(k, i)], w=[pt])
                        R.op("act", lambda e, ps=ps, half=half: e.activation(out=vtk[p][:, 2 * half:2 * half + 2, :].rearrange("p j v -> p (j v)"),
                                                                             in_=ps[:], func=AF.Copy), r=[pt], w=[f"vtk{p}_{half}"])
                psP, ptP = perm(qs, "qs", n)
                rope(qs, "qs", psP, ptP, n, j, (qs32[:] if samp else qT[p][:, 0:n]), ("qs32" if samp else f"qT{p}"))
                psP, ptP = perm(ks, "ks", n)
                rope(ks, "ks", psP, ptP, n, j, (ks32[:] if samp else kT[p][:, 0:n]), ("ks32" if samp else f"kT{p}"))
                if not samp:
                    R.op("dve", lambda e: e.tensor_tensor(out=qd[p][:].rearrange("p (j t) -> p j t", j=4),
                                                          in0=qT[p][:].rearrange("p (j t) -> p j t", j=4),
                                                          in1=qdc[:].unsqueeze(1).to_broadcast([128, 4, 128]), op=ALU.mult),
                         r=[f"qT{p}", "qdc"], w=[f"qd{p}"])
                else:
                    R.op("dve", lambda e: e.tensor_scalar(out=qds32[:], in0=qs32[:], scalar1=gam[hh], scalar2=None, op0=ALU.mult),
                         r=["qs32"], w=["qds32"])
                    R.op("dve", lambda e: e.tensor_tensor(out=prodb[:], in0=qs32[:], in1=ks32[:], op=ALU.mult), r=["qs32", "ks32"], w=["prodb"])
                    R.op("act", lambda e: e.activation(out=ks[:, 0:NS], in_=ks32[:], func=AF.Copy), r=["ks32"], w=["ks"])
                    bd, pd, pdt = self.bank()
                    R.op("pe", lambda e: e.matmul(pd[:, 0:NS], lhsT=self.ones_bf[:], rhs=prodb[:], start=True, stop=True), r=["prodb"], w=[pdt])
                    R.op("act", lambda e: e.activation(out=dots[:], in_=pd[:, 0:NS], func=AF.Copy), r=[pdt], w=["dots"])
                    for vc in range(2):
                        b, ps, pt = self.bank()
                        for k in range(NCH):
                            R.op("pe", lambda e, k=k, ps=ps, vc=vc: e.matmul(ps[:, 0:NS], lhsT=WA[:, k, 256 + vc * 128:256 + (vc + 1) * 128],
                                                                             rhs=H[:, k, c0:c1], start=(k == 0), stop=(k == NCH - 1)),
                                 r=[wat, self.ht(k, i)], w=[pt])
                        R.op("act", lambda e, ps=ps, vc=vc: e.activation(out=vTs[:, vc, :], in_=ps[:, 0:NS], func=AF.Copy), r=[pt], w=[f"vTs{vc}"])
                    b, ps, pt = self.bank()
                    for k in range(NCH):
                        R.op("pe", lambda e, k=k, ps=ps: e.matmul(ps[0:NS, 0:256], lhsT=H[:, k, c0:c1], rhs=WA[:, k, 256:512],
                                                                  start=(k == 0), stop=(k == NCH - 1)), r=[wat, self.ht(k, i)], w=[pt])
                    R.op("act", lambda e, ps=ps: e.activation(out=vtk_s[:], in_=ps[0:NS, 0:256], func=AF.Copy), r=[pt], w=["vtk_s"])
                    b, ps, pt = self.bank()
                    R.op("pe", lambda e, ps=ps: e.matmul(ps[0:NS, 0:128], lhsT=ks[:, 0:NS], rhs=identb[:], start=True, stop=True),
                         r=["ks", "identb"], w=[pt])
                    R.op("act", lambda e, ps=ps: e.activation(out=ktk_s[:], in_=ps[0:NS, 0:128], func=AF.Copy), r=[pt], w=["ktk_s"])

            def P1c(gi):
                p = gi % 2
                b, ps, pt = self.bank()
                for jj in range(4):
                    R.op("pe", lambda e, jj=jj, ps=ps: e.matmul(ps[:, jj * 128:(jj + 1) * 128], lhsT=kT[p][:, jj * 128:(jj + 1) * 128], rhs=identb[:],
                                                                start=True, stop=True), r=[f"kT{p}", "identb"], w=[pt])
                R.op("act", lambda e, ps=ps: e.activation(out=kdt[p][:].rearrange("p j d -> p (j d)"), in_=ps[:], func=AF.Copy,
                                                          scale=kdc[:, hh:hh + 1]), r=[pt, "kdc"], w=[f"kdt{p}"])

            def P2(gi):
                c0, c1 = GRP[gi]
                n = c1 - c0
                i = gi
                p = gi % 2
                b, ps, pt = self.bank()
                for jj in range(4):
                    R.op("pe", lambda e, jj=jj, ps=ps: e.matmul(ps[:, jj * 128:(jj + 1) * 128], lhsT=kT[p][:, jj * 128:(jj + 1) * 128],
                                                                rhs=qT[p][:, jj * 128:(jj + 1) * 128], start=True, stop=True),
                         r=[f"kT{p}", f"qT{p}"], w=[pt])
                R.op("dve", lambda e, ps=ps: e.tensor_tensor(out=sc[:], in0=ps[:].rearrange("p (j t) -> p j t", j=4),
                                                             in1=dmk[:].unsqueeze(1).to_broadcast([128, 4, 128]), op=ALU.mult),
                     r=[pt, "dmk"], w=["sc"])
                sidx = [0, 1, 2, 3 + p]
                for jj in range(4):
                    b, psS, psSt = self.bank()
                    R.op("pe", lambda e, jj=jj, psS=psS: e.matmul(psS[:, 0:256], lhsT=kdt[p][:, jj, :], rhs=vtk[p][:, jj, :], start=True, stop=True),
                         r=[f"kdt{p}", f"vtk{p}_{jj // 2}"], w=[psSt])
                    R.op("dve", lambda e, jj=jj, psS=psS: e.scalar_tensor_tensor(out=Sbf[sidx[jj]][:], in0=S32[:], scalar=cdec[hh], in1=psS[:, 0:256],
                                                                                 op0=ALU.mult, op1=ALU.add), r=[psSt, "S32"], w=[f"Sbf{sidx[jj]}"])
                    R.op("dve", lambda e, psS=psS: e.scalar_tensor_tensor(out=S32[:], in0=S32[:], scalar=cdec[hh], in1=psS[:, 0:256],
                                                                          op0=ALU.mult, op1=ALU.add), r=[psSt, "S32"], w=["S32"])
                bo = [self.bank(), self.bank()]
                for jj in range(4):
                    sprev = (jj - 1) if jj > 0 else (3 + (1 - p))
                    for vc in range(2):
                        _, po, pot = bo[vc]
                        R.op("pe", lambda e, jj=jj, vc=vc, po=po: e.matmul(po[:, jj * 128:(jj + 1) * 128], lhsT=vtk[p][:, jj, vc * 128:(vc + 1) * 128],
                                                                           rhs=sc[:, jj, :], start=True, stop=False),
                             r=[f"vtk{p}_{jj // 2}", "sc"], w=[pot])
                        R.op("pe", lambda e, jj=jj, vc=vc, po=po, sprev=sprev: e.matmul(
                            po[:, jj * 128:(jj + 1) * 128], lhsT=Sbf[sprev][:, vc * 128:(vc + 1) * 128],
                            rhs=qd[p][:, jj * 128:(jj + 1) * 128], start=False, stop=True),
                            r=[f"Sbf{sprev}", f"qd{p}"], w=[pot])
                gnorm([bo[0][1][:, 0:n], bo[1][1][:, 0:n]], [bo[0][2], bo[1][2]], n, c0, i, p)

            def sbatch(sbi):
                St = Sst[sbi % 2]
                Stt = f"Sst{sbi % 2}"
                b0 = 2 * sbi
                if sbi % 4 == 0:
                    hf = sbi // 4
                    R.op("dve", lambda e, hf=hf: e.tensor_tensor(
                        out=Km[:], in0=ktk_s[:].unsqueeze(1).to_broadcast([NS, 8, 128]),
                        in1=self.ident_f[0:NS, 8 * hf:8 * hf + 8].unsqueeze(2).to_broadcast([NS, 8, 128]), op=ALU.mult),
                        r=["ktk_s", "ident"], w=["Km"])
                R.dma("sp", St[:], self.din["sret"][b0:b0 + 2, hh].rearrange("b d v -> d b v"), w=[Stt], dsem=Stt)
                b, po, pot = self.bank()
                for vc in range(2):
                    for bi in range(2):
                        bb = b0 + bi
                        R.op("pe", lambda e, bi=bi, bb=bb, vc=vc: e.matmul(
                            po[:, 2 * vc + bi:2 * vc + bi + 1], lhsT=St[:, bi, vc * 128:(vc + 1) * 128], rhs=qds32[:, bb:bb + 1],
                            start=True, stop=True), r=[Stt, "qds32"], w=[pot])
                R.op("act", lambda e: e.activation(out=crs[:, :, b0:b0 + 2], in_=po[:, 0:4].rearrange("p (v b) -> p v b", v=2), func=AF.Copy),
                     r=[pot], w=[f"crs{sbi}"])
                for bi in range(2):
                    bb = b0 + bi
                    b, psS, psSt = self.bank()
                    R.op("pe", lambda e, bb=bb, psS=psS: e.matmul(psS[:, 0:256], lhsT=Km[:, bb % 8, :], rhs=vtk_s[:], start=True, stop=True),
                         r=["Km", "vtk_s"], w=[psSt])
                    R.op("dve", lambda e, bi=bi, psS=psS: e.scalar_tensor_tensor(
                        out=St[:, bi, :], in0=St[:, bi, :], scalar=gam[hh], in1=psS[:, 0:256], op0=ALU.mult, op1=ALU.add),
                        r=[psSt, Stt], w=[Stt])
                R.dma("act", self.dout["rets"][b0:b0 + 2, hh].rearrange("b d v -> d b v"), St[:], r=[Stt], w=[f"o_rets{sbi % 2}"],
                      dsem=f"o_rets{sbi % 2}")

            def sfin():
                for vc in range(2):
                    R.op("dve", lambda e, vc=vc: e.tensor_tensor(out=os_[:, vc, :], in0=vTs[:, vc, :], in1=dots[:], op=ALU.mult),
                         r=[f"vTs{vc}", "dots"], w=[f"os{vc}"])
                    R.op("dve", lambda e, vc=vc: e.tensor_tensor(out=os_[:, vc, :], in0=os_[:, vc, :], in1=crs[:, vc, :], op=ALU.add),
                         r=[f"os{vc}"] + [f"crs{k}" for k in range(8)], w=[f"os{vc}"])
                gnorm([os_[:, 0, :], os_[:, 1, :]], ["os0", "os1"], NS, SEQ, 4, -1)

            P1(4)
            P1(0)
            P1c(0)
            for gi in range(4):
                if gi + 1 < 4:
                    P1(gi + 1)
                P2(gi)
                sbatch(2 * gi)
                sbatch(2 * gi + 1)
                if gi + 1 < 4:
                    P1c(gi + 1)
                if gi == 3:
                    R.dma("sp", self.dout["retp"][hh], S32[:], r=["S32"], w=["o_retp"], dsem="o_retp")
            sfin()
            if hh % 2 == 1:
                wo, wot = Wo
                for i in range(NTT):
                    self._ret_out(i, wo, wot, G2)

        for hh in range(8):
            head(hh, (self.wload(w_out[:, 4 * (hh // 2):4 * (hh // 2) + 4, :], "p (f n) -> p f n", f=4) if hh % 2 == 1 else None))

    def _ret_out(self, i, wo, wot, G2):
        R = self.R
        c0, c1 = TB[i], TB[i + 1]
        n = c1 - c0
        for oc in range(NCH):
            b, ps, pt = self.bank()
            for j in range(4):
                R.op("pe", lambda e, j=j, ps=ps, oc=oc: e.matmul(ps[:, 0:n], lhsT=wo[:, j, oc * 128:(oc + 1) * 128], rhs=G2[:, j, c0:c1],
                                                                 start=(j == 0), stop=(j == 3)), r=[wot, f"g2_{j}_{i}"], w=[pt])
            self.add_to_xres(oc, i, ps, pt)

    def build(self):
        nc, R = self.nc, self.R
        inp, outp = self.inp, self.outp
        mix = [m for m in self.mixers if m < self.nlayers]
        xT = inp("xT", [D, T])
        vecs_d = inp("vecs", [128, NVEC])
        ident_d = inp("ident", [128, 128])
        inp("mlp_w1", [4, D, DFF])
        inp("mlp_w2", [4, DFF, D])
        yT = outp("yT", [D, T])
        if 0 in mix:
            inp("pool_w", [1, 4, 256, 256])
            inp("spT", [D, NS, 15])
            inp("rc16", [128, 4, 16])
            outp("poolpT", [D, 15])
            outp("poolsT", [D, NS, 15])
        if 1 in mix:
            inp("gm_w_in", [1, D, 2 * D])
            inp("gm_w_out", [1, D, D])
            inp("gm_b_in", [1, 2 * D])
            inp("gm_b_s", [1, 8, 128])
            inp("gm_wsT", [128, 8, 128])
            inp("trilT", [128, 128])
            outp("gvT", [D, NS])
        if 2 in mix:
            inp("ret_w_in", [1, D, 6144])
            inp("ret_w_out", [1, 2048, D])
            inp("sret", [NS, 8, 128, 256])
            inp("permM", [128, 128])
            inp("cmat", [2, 128, 128])
            inp("kdec", [128, 8])
            inp("dmaskT", [8, 128, 128])
            inp("qdecb", [8, 128, 128])
            inp("ropeC", [128, T])
            inp("ropeS", [128, T])
            outp("retp", [8, 128, 256])
            outp("rets", [NS, 8, 128, 256])
        if 3 in mix:
            inp("lru_w_in", [1, D, 2 * D])
            inp("lru_w_out", [1, D, D])
            inp("lru_w_a", [1, 8, 128, 128])
            inp("lru_w_x", [1, 8, 128, 128])
            inp("scT", [D, 3, NS])
            inp("slT", [D, NS])
            outp("convpT", [D, 3])
            outp("convsT", [D, 3, NS])
            outp("lrupT", [128, NCH])
            outp("lrusT", [D, NS])
        self.xres = self.sb("xres", [128, NCH, T], F32)
        self.H = self.sb("H", [128, NCH, T], BF16)
        self.NW = 5
        self.ws = [self.sb(f"wslot{i}", [128, 4096], BF16) for i in range(self.NW)]
        self.w_i = 0
        self.vecs = self.sb("vecs_sb", [128, NVEC], F32)
        self.ones_bf = self.sb("ones_bf", [128, 128], BF16)
        self.ident_f = self.sb("ident_f", [128, 128], F32)
        self.eps_col = self.sb("eps_col", [128, 1], F32)
        self.one_col = self.sb("one_col", [128, 1], F32)
        self.rstd = self.sb("rstd", [128, T], F32)
        self.sq = [self.sb(f"sq{i}", [128, 512], BF16) for i in range(2)]
        self.ones_row_b = self.sb("ones_row_b", [1, 128], BF16)
        self.ones_row_f = self.sb("ones_row_f", [1, 128], F32)
        self.ps = [nc.alloc_psum_tensor(f"psb{i}", [128, 512], F32) for i in range(8)]
        self.t_base = (nc.sbuf_base + 63) // 64 * 64
        self.t_end = nc.sbuf_top
        self.t_off = self.t_base
        self.tmp_n = 0

        R.op("dve", lambda e: e.memset(self.ones_bf[:], 1.0), w=["ones"])
        R.op("dve", lambda e: e.memset(self.eps_col[:], EPS), w=["epsc"])
        R.op("dve", lambda e: e.memset(self.one_col[:], 1.0), w=["onec"])
        R.op("dve", lambda e: e.memset(self.ones_row_b[:], 1.0), w=["onesrb"])
        R.op("dve", lambda e: e.memset(self.ones_row_f[:], 1.0), w=["onesrf"])
        R.dma("sp", self.vecs[:], vecs_d, w=["vecs"], dsem="vecs")
        R.dma("sp", self.ident_f[:], ident_d, w=["ident"], dsem="vecs")
        xv = xT.rearrange("(c p) t -> p c t", p=128)
        for c in range(NCH):
            R.dma("sp", self.xres[:, c, :], xv[:, c, :], w=self.xtoks(c), dsem="xin")

        fns = {0: self.pool_mixer, 1: self.gmlp_mixer, 2: self.ret_mixer, 3: self.lru_mixer}
        for li in range(self.nlayers):
            m = li % 4
            if m in mix:
                self.tmp_reset()
                fns[m]()
            if not NOMLP:
                self.tmp_reset()
                self.mlp(li)

        self.tmp_reset()
        self.norm("final_norm", 0, out_f32_inplace=True)
        yv = yT.rearrange("(c p) t -> p c t", p=128)
        for c in range(NCH):
            R.dma("sp", yv[:, c, :], self.xres[:, c, :], r=self.xtoks(c), w=[f"yout{c}"], dsem="yout")
        R.emit()

    def host_const(self, n, inputs, core):
        f = lambda k: np.asarray(inputs[k], np.float32)
        sl = slice(core * NS, (core + 1) * NS)
        if n == "ident":
            return np.eye(128, dtype=np.float32)
        if n == "spT":
            return np.ascontiguousarray(f("state_pool")[0, sl].transpose(2, 0, 1))
        if n == "rc16":
            t = np.arange(16, dtype=np.float32)
            rc = np.stack([1.0 / np.minimum(t + 1, w) for w in (2, 4, 8, 16)], 0).astype(np.float32)
            return np.ascontiguousarray(np.broadcast_to(rc[None], (128, 4, 16)))
        if n == "sret":
            return np.ascontiguousarray(f("state_ret")[0, sl])
        if n == "cmat":
            j = np.full((128, 128), -1.0 / 256, np.float32)
            return np.stack([np.eye(128, dtype=np.float32) + j, j], 0)
        if n == "permM":
            k = np.arange(128)
            return (k[:, None] == ((k[None, :] + 64) % 128)).astype(np.float32)
        if n in ("kdec", "dmaskT", "qdecb"):
            lg = np.log1p(-np.exp2(-5.0 - np.arange(8, dtype=np.float32))).astype(np.float32)
            idx = np.arange(128, dtype=np.float32)
            if n == "kdec":
                return np.ascontiguousarray(np.exp(lg[None, :] * (127.0 - idx)[:, None]).astype(np.float32))
            if n == "qdecb":
                qd_ = np.exp(lg[:, None] * (idx + 1.0)[None, :]).astype(np.float32)
                return np.ascontiguousarray(np.broadcast_to(qd_[:, None, :], (8, 128, 128)))
            diff = idx[None, :] - idx[:, None]
            dm = np.where(diff[None] >= 0, np.exp(lg[:, None, None] * np.maximum(diff, 0.0)[None]), 0.0)
            return np.ascontiguousarray(dm.astype(np.float32))
        if n in ("ropeC", "ropeS"):
            half = 64
            freqs = np.exp(np.float32(-math.log(10000.0)) * np.arange(half, dtype=np.float32) / np.float32(half)).astype(np.float32)
            pos = np.concatenate([np.arange(SEQ), np.full(NS, PAST)]).astype(np.float32)
            ang = (pos[None, :] * freqs[:, None]).astype(np.float32)
            if n == "ropeC":
                c_ = np.cos(ang).astype(np.float32)
                return np.ascontiguousarray(np.concatenate([c_, c_], 0))
            s_ = np.sin(ang).astype(np.float32)
            return np.ascontiguousarray(np.concatenate([-s_, s_], 0))
        if n == "scT":
            return np.ascontiguousarray(f("state_conv")[0, sl].transpose(2, 1, 0))
        if n == "slT":
            return np.ascontiguousarray(f("state_lru")[0, sl].T)
        if n == "gm_wsT":
            return np.ascontiguousarray(f("gm_w_s")[0].transpose(2, 0, 1))
        if n == "trilT":
            s_ = np.arange(128)
            return (s_[None, :] >= s_[:, None]).astype(np.float32)
        raise KeyError(n)


_CACHE = {}


def get_prog(key=((0, 1, 2, 3), 4)):
    if key not in _CACHE:
        _CACHE[key] = K(*key)
    return _CACHE[key]


def pack_inputs(inputs, core, prog):
    f = lambda n: np.asarray(inputs[n], np.float32)
    m = {}
    xp = f("x_prompt")[core]
    xs = f("x_sample")[core * NS:(core + 1) * NS, 0]
    m["xT"] = np.ascontiguousarray(np.concatenate([xp, xs], axis=0).T)
    vec = np.zeros((128, NVEC), np.float32)
    for n, k in VEC_SPECS:
        if n == "gm_ws00":
            v = np.broadcast_to(f("gm_w_s")[0, :, 0, 0][None, :], (128, 8))
        elif n == "gm_bs0":
            v = np.broadcast_to(f("gm_b_s")[0, :, 0][None, :], (128, 8))
        else:
            v = cols(f(n))
        vec[:, VEC_OFF[n]:VEC_OFF[n] + k] = v
    m["vecs"] = vec
    for n in prog.din:
        if n in m:
            continue
        if n in inputs:
            m[n] = f(n)
        else:
            m[n] = prog.host_const(n, inputs, core)
    return {k: m[k] for k in prog.din}


def unpack(rs, prog):
    o = {}
    g = lambda c, n: np.asarray(rs[c][n])
    C = range(NCORES)
    o["y_prompt"] = np.ascontiguousarray(np.stack([g(c, "yT")[:, :SEQ].T for c in C], 0))
    o["y_sample"] = np.ascontiguousarray(np.concatenate([g(c, "yT")[:, SEQ:].T for c in C], 0)[:, None, :])
    d = prog.dout
    if "poolpT" in d:
        o["pool_prompt"] = np.ascontiguousarray(np.stack([g(c, "poolpT").T for c in C], 0)[None])
        o["pool_sample"] = np.ascontiguousarray(np.concatenate([g(c, "poolsT").transpose(1, 2, 0) for c in C], 0)[None])
    if "gvT" in d:
        o["gmlp_v_sample"] = np.ascontiguousarray(np.concatenate([g(c, "gvT").T for c in C], 0)[None, :, None, :])
    if "retp" in d:
        o["ret_prompt"] = np.ascontiguousarray(np.stack([g(c, "retp") for c in C], 0)[None])
        o["ret_sample"] = np.ascontiguousarray(np.concatenate([g(c, "rets") for c in C], 0)[None])
    if "convpT" in d:
        o["conv_prompt"] = np.ascontiguousarray(np.stack([g(c, "convpT").T for c in C], 0)[None])
        o["conv_sample"] = np.ascontiguousarray(np.concatenate([g(c, "convsT").transpose(2, 1, 0) for c in C], 0)[None])
        o["lru_prompt"] = np.ascontiguousarray(np.stack([g(c, "lrupT").T.reshape(-1) for c in C], 0)[None])
        o["lru_sample"] = np.ascontiguousarray(np.concatenate([g(c, "lrusT").T for c in C], 0)[None])
    return o


ORDER = ["y_prompt", "y_sample", "pool_prompt", "pool_sample", "gmlp_v_sample", "ret_prompt", "ret_sample",
         "conv_prompt", "conv_sample", "lru_prompt", "lru_sample"]


def kernel(**inputs):
    prog = get_prog()
    in_maps = [pack_inputs(inputs, c, prog) for c in range(NCORES)]
    res = run_bass_kernel_spmd(prog.nc, in_maps, core_ids=list(range(NCORES)))
    o = unpack(res.results, prog)
    return tuple(o[k] for k in ORDER)
```
